# Optimizing a Trainium2 kernel written in Bass

```python
import math
import jax
import jax.numpy as jnp
from jax import lax
import numpy as np

D_MODEL = 1024
BATCH = 2
SEQ = 8192
DEPTH = 2
DEC_BATCH = 8
DEC_SEQ = 8192
PAST_LEN = 128

HEAD_DIM = 64
N_HEADS_A = 4
DIFF_DIM = HEAD_DIM // 2
N_HEADS_B = 4
N_HEADS_C = 4
N_HEADS_D = 4
N_KV_D = 2
BRANCH_W = 4 * HEAD_DIM
N_BRANCH = 4
GRID_W = 64
NA_ROWS_MAX = 8
NA_COLS = 16
C_PATTERNS = ((128, 1), (512, 4), (2048, 16))
ROPE_THETA = 500000.0
ROPE_FRACTION = 4
AXIAL_THETA = 10000.0
Q_BLOCK = 128
D_FF = 2816
EPS = 1e-6
NEG_INF = -1e30
IN_SPLITS = (
    N_HEADS_A * HEAD_DIM, N_HEADS_A * HEAD_DIM, N_HEADS_A * HEAD_DIM,
    N_HEADS_B * HEAD_DIM, N_HEADS_B * HEAD_DIM, N_HEADS_B * HEAD_DIM,
    N_HEADS_C * HEAD_DIM, N_HEADS_C * HEAD_DIM, N_HEADS_C * HEAD_DIM,
    N_HEADS_D * HEAD_DIM, N_KV_D * HEAD_DIM, N_KV_D * HEAD_DIM,
    N_BRANCH * D_MODEL,
)
IN_COLS = 9 * BRANCH_W + N_HEADS_D * HEAD_DIM + 2 * N_KV_D * HEAD_DIM + N_BRANCH * D_MODEL

kernel_name = "hybrid_gated_encoder_4mixer"


def _rms_norm(x, g):
    xf = x.astype(jnp.float32)
    y = xf * lax.rsqrt(jnp.mean(xf * xf, axis=-1, keepdims=True) + EPS)
    return (y * g.astype(jnp.float32)).astype(x.dtype)


def _rope(x, pos, theta):
    half = x.shape[-1] // 2
    inv = jnp.exp(-math.log(theta) * jnp.arange(half, dtype=jnp.float32) / half)
    ang = pos.astype(jnp.float32)[:, None] * inv[None, :]
    ang = ang.reshape((pos.shape[0],) + (1,) * (x.ndim - 3) + (half,))
    cos, sin = jnp.cos(ang), jnp.sin(ang)
    xf = x.astype(jnp.float32)
    x1, x2 = xf[..., :half], xf[..., half:]
    return jnp.concatenate([x1 * cos - x2 * sin, x2 * cos + x1 * sin], axis=-1).astype(x.dtype)


def _partial_rope(x, pos):
    nr = x.shape[-1] // ROPE_FRACTION
    return jnp.concatenate([_rope(x[..., :nr], pos, ROPE_THETA), x[..., nr:]], axis=-1)


def _axial_rope(x, pos):
    half = x.shape[-1] // 2
    return jnp.concatenate([_rope(x[..., :half], pos // GRID_W, AXIAL_THETA),
                            _rope(x[..., half:], pos % GRID_W, AXIAL_THETA)], axis=-1)


def _sweep_query_blocks(q, fn):
    B, S = q.shape[:2]
    nb = S // Q_BLOCK
    qb = jnp.moveaxis(q.reshape((B, nb, Q_BLOCK) + q.shape[2:]), 1, 0)
    out = lax.map(fn, qb)
    return jnp.moveaxis(out, 0, 1).reshape((B, S) + out.shape[3:])


def _diff_attention(q, k, v, lam_vecs, subln_g, lam_init, pos):
    B, S, H = q.shape[:3]
    q = _partial_rope(q, pos)
    k = _partial_rope(k, pos)
    lv = lam_vecs.astype(jnp.float32)
    lam = jnp.exp(jnp.sum(lv[0] * lv[1])) - jnp.exp(jnp.sum(lv[2] * lv[3])) + lam_init
    scale = DIFF_DIM ** -0.5

    def block(qb):
        s = jnp.einsum("bqhcd,bkhcd->bhcqk", qb, k, preferred_element_type=jnp.float32) * scale
        p = jax.nn.softmax(s, axis=-1)
        a = p[:, :, 0] - lam * p[:, :, 1]
        return jnp.einsum("bhqk,bkhd->bqhd", a.astype(v.dtype), v)

    o = _sweep_query_blocks(q, block)
    o = _rms_norm(o, subln_g) * (1.0 - lam_init)
    return o.reshape(B, S, H * HEAD_DIM)


def _neighbourhood_attention(q, k, v, rpb):
    B, S, H, dh = q.shape
    rows = S // GRID_W
    kr = min(NA_ROWS_MAX, rows)
    r = jnp.arange(rows)
    row_idx = jnp.clip(r - kr // 2, 0, rows - kr)[:, None] + jnp.arange(kr)[None, :]
    c = jnp.arange(GRID_W)
    col_start = jnp.clip(c - NA_COLS // 2, 0, GRID_W - NA_COLS)
    col_in = (c[None, :] >= col_start[:, None]) & (c[None, :] < col_start[:, None] + NA_COLS)
    dr = row_idx - r[:, None] + NA_ROWS_MAX - 1
    dc = jnp.clip(c[None, :] - c[:, None] + NA_COLS - 1, 0, 2 * NA_COLS - 2)
    bias = rpb.astype(jnp.float32)[:, dr][:, :, :, dc]
    bias = jnp.where(col_in[None, None, None], bias, NEG_INF).transpose(1, 0, 3, 2, 4)
    qg = q.reshape(B, rows, GRID_W, H, dh)
    kg = jnp.take(k.reshape(B, rows, GRID_W, H, dh), row_idx, axis=1)
    vg = jnp.take(v.reshape(B, rows, GRID_W, H, dh), row_idx, axis=1)
    s = jnp.einsum("brchd,brkmhd->brhckm", qg, kg, preferred_element_type=jnp.float32) * dh ** -0.5 + bias
    p = jax.nn.softmax(s.reshape(B, rows, H, GRID_W, kr * GRID_W), axis=-1).reshape(s.shape)
    o = jnp.einsum("brhckm,brkmhd->brchd", p.astype(v.dtype), vg)
    return o.reshape(B, S, H * dh)


def _to_sub(x, d):
    B, S = x.shape[:2]
    r = x.reshape((B, S // d, d) + x.shape[2:])
    return jnp.moveaxis(r, 2, 1).reshape((B * d, S // d) + x.shape[2:])


def _from_sub(x, d, B):
    L = x.shape[1]
    r = x.reshape((B, d, L) + x.shape[2:])
    return jnp.moveaxis(r, 1, 2).reshape((B, L * d) + x.shape[2:])


def _banded_attention(q, k, v, half):
    N, L, H, dh = q.shape
    nb = -(-L // Q_BLOCK)
    lq = nb * Q_BLOCK
    kw = Q_BLOCK + 2 * half
    qb = jnp.pad(q, ((0, 0), (0, lq - L), (0, 0), (0, 0))).reshape(N, nb, Q_BLOCK, H, dh)
    pad = ((0, 0), (half, lq - L + half), (0, 0), (0, 0))
    idx = jnp.arange(nb)[:, None] * Q_BLOCK + jnp.arange(kw)[None, :]
    kb = jnp.take(jnp.pad(k, pad), idx, axis=1)
    vb = jnp.take(jnp.pad(v, pad), idx, axis=1)
    s = jnp.einsum("nbqhd,nbkhd->nbhqk", qb, kb, preferred_element_type=jnp.float32) * dh ** -0.5
    key_pos = idx - half
    q_pos = jnp.arange(nb)[:, None] * Q_BLOCK + jnp.arange(Q_BLOCK)[None, :]
    rel = key_pos[:, None, :] - q_pos[:, :, None]
    valid = (jnp.abs(rel) <= half) & (key_pos[:, None, :] >= 0) & (key_pos[:, None, :] < L)
    s = jnp.where(valid[None, :, None], s, NEG_INF)
    lse = jax.nn.logsumexp(s, axis=-1)
    p = jnp.exp(s - lse[..., None])
    o = jnp.einsum("nbhqk,nbkhd->nbqhd", p.astype(v.dtype), vb).reshape(N, lq, H, dh)[:, :L]
    lse = jnp.swapaxes(lse, 2, 3).reshape(N, lq, H)[:, :L]
    return o, lse


def _dilated_attention(q, k, v, pos):
    B, S, H, dh = q.shape
    q = _partial_rope(q, pos)
    k = _partial_rope(k, pos)
    outs, lses = [], []
    for window, dil in C_PATTERNS:
        half = window // (2 * dil)
        o, lse = _banded_attention(_to_sub(q, dil), _to_sub(k, dil), _to_sub(v, dil), half)
        outs.append(_from_sub(o, dil, B))
        lses.append(_from_sub(lse, dil, B))
    wgt = jax.nn.softmax(jnp.stack(lses, axis=0), axis=0)
    o = jnp.einsum("nbsh,nbshd->bshd", wgt, jnp.stack(outs, axis=0).astype(jnp.float32))
    return o.astype(v.dtype).reshape(B, S, H * dh)


def _axial_gqa(q, k, v, g_q, g_k, pos):
    B, S, H, dh = q.shape
    q = _axial_rope(_rms_norm(q, g_q), pos)
    k = _axial_rope(_rms_norm(k, g_k), pos)
    q = q.reshape(B, S, N_KV_D, H // N_KV_D, dh)
    scale = dh ** -0.5

    def block(qb):
        s = jnp.einsum("bqkgd,bskd->bkgqs", qb, k, preferred_element_type=jnp.float32) * scale
        p = jax.nn.softmax(s, axis=-1)
        return jnp.einsum("bkgqs,bskd->bqkgd", p.astype(v.dtype), v)

    o = _sweep_query_blocks(q, block)
    return o.reshape(B, S, H * dh)


def _layer(x, pos, lam_init, norm_attn, w_in, diff_lambda, diff_subln, na_rpb, qk_norm,
           w_branch, w_out, norm_mlp, w_up, conv_w, conv_b, w_down):
    B, S, _ = x.shape
    h = _rms_norm(x, norm_attn)
    proj = h @ w_in
    cuts = np.cumsum(IN_SPLITS)[:-1].tolist()
    aq, ak, av, bq, bk, bv, cq, ck, cv, dq, dk, dv, gate = jnp.split(proj, cuts, axis=-1)
    o_a = _diff_attention(aq.reshape(B, S, N_HEADS_A, 2, DIFF_DIM), ak.reshape(B, S, N_HEADS_A, 2, DIFF_DIM),
                          av.reshape(B, S, N_HEADS_A, HEAD_DIM), diff_lambda, diff_subln, lam_init, pos)
    o_b = _neighbourhood_attention(bq.reshape(B, S, N_HEADS_B, HEAD_DIM), bk.reshape(B, S, N_HEADS_B, HEAD_DIM),
                                   bv.reshape(B, S, N_HEADS_B, HEAD_DIM), na_rpb)
    o_c = _dilated_attention(cq.reshape(B, S, N_HEADS_C, HEAD_DIM), ck.reshape(B, S, N_HEADS_C, HEAD_DIM),
                             cv.reshape(B, S, N_HEADS_C, HEAD_DIM), pos)
    o_d = _axial_gqa(dq.reshape(B, S, N_HEADS_D, HEAD_DIM), dk.reshape(B, S, N_KV_D, HEAD_DIM),
                     dv.reshape(B, S, N_KV_D, HEAD_DIM), qk_norm[0], qk_norm[1], pos)
    gates = jax.nn.sigmoid(gate.reshape(B, S, N_BRANCH, D_MODEL))
    merged = gates[:, :, 0] * (o_a @ w_branch[0])
    merged = merged + gates[:, :, 1] * (o_b @ w_branch[1])
    merged = merged + gates[:, :, 2] * (o_c @ w_branch[2])
    merged = merged + gates[:, :, 3] * (o_d @ w_branch[3])
    x = x + merged @ w_out
    h = _rms_norm(x, norm_mlp)
    u = h @ w_up
    up = jnp.pad(u, ((0, 0), (1, 1), (0, 0)))
    u = up[:, :-2] * conv_w[0] + up[:, 1:-1] * conv_w[1] + up[:, 2:] * conv_w[2] + conv_b
    val, gt = jnp.split(u, 2, axis=-1)
    return x + (jax.nn.gelu(gt, approximate=False) * val) @ w_down


def _trunk(x, norm_attn, w_in, diff_lambda, diff_subln, na_rpb, qk_norm, w_branch, w_out,
           norm_mlp, w_up, conv_w, conv_b, w_down, norm_final):
    pos = jnp.arange(x.shape[1], dtype=jnp.int32)
    for l in range(DEPTH):
        lam_init = 0.8 - 0.6 * math.exp(-0.3 * l)
        x = _layer(x, pos, lam_init, norm_attn[l], w_in[l], diff_lambda[l], diff_subln[l], na_rpb[l],
                   qk_norm[l], w_branch[l], w_out[l], norm_mlp[l], w_up[l], conv_w[l], conv_b[l], w_down[l])
    return _rms_norm(x, norm_final)


def setup_inputs(seed: int = 0) -> dict:
    key = jax.random.key(seed)
    ks = jax.random.split(key, 16)
    f32 = jnp.float32
    nrm = lambda k, shape, s: jax.random.normal(k, shape, f32) * s
    return {
        "x_prompt": nrm(ks[0], (BATCH, SEQ, D_MODEL), 1.0),
        "x_sample": nrm(ks[1], (DEC_BATCH, DEC_SEQ, D_MODEL), 1.0),
        "norm_attn": 1.0 + nrm(ks[2], (DEPTH, D_MODEL), 0.02),
        "w_in": nrm(ks[3], (DEPTH, D_MODEL, IN_COLS), D_MODEL ** -0.5),
        "diff_lambda": nrm(ks[4], (DEPTH, 4, DIFF_DIM), 0.1),
        "diff_subln": 1.0 + nrm(ks[5], (DEPTH, HEAD_DIM), 0.02),
        "na_rpb": nrm(ks[6], (DEPTH, N_HEADS_B, 2 * NA_ROWS_MAX - 1, 2 * NA_COLS - 1), 0.1),
        "qk_norm": 1.0 + nrm(ks[7], (DEPTH, 2, HEAD_DIM), 0.02),
        "w_branch": nrm(ks[8], (DEPTH, N_BRANCH, BRANCH_W, D_MODEL), BRANCH_W ** -0.5),
        "w_out": nrm(ks[9], (DEPTH, D_MODEL, D_MODEL), D_MODEL ** -0.5),
        "norm_mlp": 1.0 + nrm(ks[10], (DEPTH, D_MODEL), 0.02),
        "w_up": nrm(ks[11], (DEPTH, D_MODEL, 2 * D_FF), D_MODEL ** -0.5),
        "conv_w": nrm(ks[12], (DEPTH, 3, 2 * D_FF), 3 ** -0.5),
        "conv_b": nrm(ks[13], (DEPTH, 2 * D_FF), 0.02),
        "w_down": nrm(ks[14], (DEPTH, D_FF, D_MODEL), D_FF ** -0.5),
        "norm_final": 1.0 + nrm(ks[15], (D_MODEL,), 0.02),
    }


def reference(x_prompt, x_sample, norm_attn, w_in, diff_lambda, diff_subln, na_rpb, qk_norm, w_branch,
              w_out, norm_mlp, w_up, conv_w, conv_b, w_down, norm_final):
    y_prompt = _trunk(x_prompt, norm_attn, w_in, diff_lambda, diff_subln, na_rpb, qk_norm, w_branch, w_out,
                      norm_mlp, w_up, conv_w, conv_b, w_down, norm_final)
    y_sample = _trunk(x_sample, norm_attn, w_in, diff_lambda, diff_subln, na_rpb, qk_norm, w_branch, w_out,
                      norm_mlp, w_up, conv_w, conv_b, w_down, norm_final)
    return (y_prompt, y_sample)
```

```python
import math
import os
import numpy as np
import concourse.bass as bass
import concourse.mybir as mybir
from concourse.bass_utils import run_bass_kernel_spmd

F32 = mybir.dt.float32
BF16 = mybir.dt.bfloat16
U8 = mybir.dt.uint8
AF = mybir.ActivationFunctionType
ALU = mybir.AluOpType
AX = mybir.AxisListType

D = 1024
KC = 8
DFF = 2816
NFC = 22
INC = 6912
EPS = 1e-6
NEG = -30000.0
N_CORES = 8
ROPE_THETA = 500000.0
AXIAL_THETA = 10000.0
PPL = 196
SELF_SYNC = os.environ.get('NOSELF') is None


class Hd:
    __slots__ = ("name", "w", "r", "dsem", "dcnt")

    def __init__(self, name):
        self.name = name
        self.w = []
        self.r = []
        self.dsem = None
        self.dcnt = 0


class Op:
    __slots__ = ("eng", "fn", "deps", "signal", "ticket", "dma", "sem", "idx")


class Prog:
    CE = ("pe", "act", "dve", "pool")

    def __init__(self, nc):
        self.nc = nc
        self.ops = {e: [] for e in ("pe", "act", "dve", "pool", "sp")}
        self.esem = {e: nc.alloc_semaphore("s_" + e) for e in self.CE}
        self.pending = {e: [] for e in self.ops}
        self.dma_sems = {}
        self.named = {}
        self.nops = 0

    @staticmethod
    def _key(o):
        return o.eng if o.dma is None else ("d", id(o.sem))

    def _push(self, lst, o):
        k = self._key(o)
        for i, p in enumerate(lst):
            if self._key(p) == k:
                lst[i] = o
                return
        lst.append(o)

    def op(self, eng, fn, reads=(), writes=(), dma=None):
        o = Op()
        o.eng = eng
        o.fn = fn
        o.signal = False
        o.ticket = None
        o.dma = dma
        o.sem = None
        o.idx = self.nops
        self.nops += 1
        deps = list(self.pending[eng])
        self.pending[eng] = []
        for h in reads:
            deps.extend(h.w)
            if h.name.startswith("bank"):
                deps.extend(r for r in h.r if r.eng != eng)
        for h in writes:
            if not (dma is not None and h.w and all(p.dma is not None for p in h.w) and not h.r):
                deps.extend(h.w)
            deps.extend(h.r)
        if dma is not None:
            nk = (dma.name, eng)
            if nk not in self.named:
                sem_ = self.nc.alloc_semaphore("d_%s_%s_%d" % (dma.name, eng, len(self.dma_sems)))
                self.named[nk] = [sem_, 0]
                self.dma_sems[id(sem_)] = [sem_, 0]
            ent = self.named[nk]
            ent[1] += 1
            o.sem = ent[0]
            o.ticket = 16 * ent[1]
            self.dma_sems[id(o.sem)][1] = o.ticket
        fd = []
        for p in deps:
            if p is o:
                continue
            if p.dma is not None:
                fd.append(p)
            elif p.eng == eng:
                if eng != "pe" and SELF_SYNC:
                    p.signal = True
                    fd.append(p)
            else:
                p.signal = True
                fd.append(p)
        o.deps = fd
        for h in reads:
            if h not in writes:
                self._push(h.r, o)
        for h in writes:
            if dma is not None and h.w and all(p.dma is not None for p in h.w) and not h.r:
                self._push(h.w, o)
            else:
                h.w = [o]
            h.r = []
        self.ops[eng].append(o)
        return o

    def barrier(self):
        lasts = []
        for e in self.CE:
            for p in reversed(self.ops[e]):
                if p.dma is None:
                    lasts.append(p)
                    break
        dm = []
        for sid, (sem, tot) in self.dma_sems.items():
            if tot > 0:
                f = Op()
                f.eng = "dma"
                f.dma = True
                f.sem = sem
                f.ticket = tot
                f.signal = False
                dm.append(f)
        for e in self.ops:
            for p in lasts:
                if p.eng != e or (e != "pe" and SELF_SYNC):
                    p.signal = True
                    self.pending[e].append(p)
            self.pending[e].extend(dm)

    def emit(self):
        nc = self.nc
        for e in self.CE:
            c = 0
            for o in self.ops[e]:
                if o.signal and o.dma is None:
                    c += 1
                    o.ticket = c
        esem = self.esem

        def mk(ename):
            def body(eng):
                waited = {}
                for o in self.ops[ename]:
                    for p in o.deps:
                        sem = p.sem if p.dma is not None else esem[p.eng]
                        v = p.ticket
                        k = id(sem)
                        if waited.get(k, 0) < v:
                            eng.wait_ge(sem, v)
                            waited[k] = v
                    if os.environ.get("DUMP"):
                        print("OP", ename, o.idx, "dma" if o.dma is not None else "", "sig=%s" % o.ticket if (o.signal or o.dma is not None) else "",
                              "waits:", [((p.sem.name if p.dma is not None else p.eng), p.ticket) for p in o.deps])
                    ins = o.fn(eng)
                    if o.dma is not None:
                        ins.then_inc(o.sem, 16)
                    elif o.signal:
                        ins.then_inc(esem[ename], 1)
            return body

        with nc.Block() as block:
            block.sync(mk("sp"))
            block.tensor(mk("pe"))
            block.scalar(mk("act"))
            block.vector(mk("dve"))
            block.gpsimd(mk("pool"))


class Arena:
    def __init__(self, nc, base, top):
        self.nc = nc
        self.base = base
        self.top = top
        self.cur = base
        self.n = 0

    def reset(self):
        self.cur = self.base

    def alloc(self, shape, dtype, name="t"):
        esz = {F32: 4, BF16: 2, U8: 1}[dtype]
        nb = esz
        for s in shape[1:]:
            nb *= s
        nb = (nb + 31) // 32 * 32
        off = self.cur
        assert off + nb <= self.top, f"SBUF arena overflow {name}: {off + nb} > {self.top}"
        self.cur += nb
        self.n += 1
        return self.nc.alloc_sbuf_tensor_at(f"{name}_{self.n}", list(shape), dtype, offset=off)


def _rope_tables(kind, S):
    pos = np.arange(S)
    cos = np.ones((128, S), np.float32)
    sin = np.zeros((128, S), np.float32)
    perm = np.zeros((128, 128), np.float32)
    for p in range(128):
        if kind == "A":
            j = p % 32
            if j >= 8:
                continue
            half, i, first, theta, pv = 4, j % 4, j < 4, ROPE_THETA, pos
        elif kind == "C":
            j = p % 64
            if j >= 16:
                continue
            half, i, first, theta, pv = 8, j % 8, j < 8, ROPE_THETA, pos
        else:
            j = p % 64
            half, theta = 16, AXIAL_THETA
            if j < 32:
                i, first, pv = j % 16, j < 16, pos // 64
            else:
                i, first, pv = (j - 32) % 16, (j - 32) < 16, pos % 64
        inv = np.exp((np.float32(-math.log(theta)) * np.arange(half, dtype=np.float32)) / np.float32(half)).astype(np.float32)
        ang = (pv.astype(np.float32) * inv[i]).astype(np.float32).astype(np.float64)
        cos[p] = np.cos(ang).astype(np.float32)
        sn = np.sin(ang).astype(np.float32)
        sin[p] = -sn if first else sn
        partner = p + half if first else p - half
        perm[partner, p] = 1.0
    return cos, sin, perm


def _cmask_tiles():
    out = np.empty((20, 128, 512), np.float32)
    kl = np.arange(128)[:, None]
    ql = np.arange(512)[None, :]
    for di in range(20):
        dlt = di - 8
        d = 128 * dlt + kl - ql
        mult = np.zeros(d.shape, np.int64)
        for dil in (1, 4, 16):
            mult += ((d % dil) == 0) & (np.abs(d) <= 64 * dil)
        with np.errstate(divide="ignore"):
            out[di] = np.where(mult > 0, np.log(np.maximum(mult, 1)).astype(np.float32), np.float32(NEG))
    return out


def _na_tiles(rpb, S):
    L = rpb.shape[0]
    rows = S // 64
    T = S // 512
    start = np.clip(np.arange(rows) - 4, 0, rows - 8)
    c = np.arange(64)
    cs = np.clip(c - 8, 0, 48)
    col_in = (c[None, :] >= cs[:, None]) & (c[None, :] < cs[:, None] + 16)
    dc = np.clip(c[None, :] - c[:, None] + 15, 0, 30)
    out = np.full((L, 4, 3, 8, 128, 512), NEG, np.float32)
    tl = [0, 1 if T > 2 else 0, T - 1]
    kk = np.arange(128)
    qq = np.arange(512)
    for vi, t in enumerate(tl):
        for j in range(8):
            kb = 4 * t - 2 + j
            if kb < 0 or kb >= S // 128:
                continue
            krow = 2 * kb + kk // 64
            kcol = kk % 64
            qrow = 8 * t + qq // 64
            qcol = qq % 64
            st = start[qrow]
            vrow = (krow[:, None] >= st[None, :]) & (krow[:, None] < st[None, :] + 8)
            dr = np.clip(krow[:, None] - qrow[None, :] + 7, 0, 14)
            valid = vrow & col_in[qcol[None, :], kcol[:, None]]
            dcc = dc[qcol[None, :], kcol[:, None]]
            vals = rpb[:, :, dr, dcc]
            out[:, :, vi, j] = np.where(valid[None, None], vals, np.float32(NEG))
    return out


def _pack_pp(norm_attn, norm_mlp, conv_w, conv_b, qk_norm, diff_subln, norm_final):
    L = norm_attn.shape[0]
    pp = np.zeros((128, L * PPL + 8), np.float32)
    p64 = np.arange(128) % 64
    for l in range(L):
        o = l * PPL
        pp[:, o:o + 8] = norm_attn[l].reshape(8, 128).T
        pp[:, o + 8:o + 16] = norm_mlp[l].reshape(8, 128).T
        for j in range(3):
            pp[:, o + 16 + j * 44:o + 16 + (j + 1) * 44] = conv_w[l, j].reshape(44, 128).T
        pp[:, o + 148:o + 192] = conv_b[l].reshape(44, 128).T
        pp[:, o + 192] = qk_norm[l, 0][p64]
        pp[:, o + 193] = qk_norm[l, 1][p64]
        pp[:, o + 194] = diff_subln[l][p64]
    pp[:, L * PPL:L * PPL + 8] = norm_final.reshape(8, 128).T
    return pp


def build_program(S, NSEQ, L=2, debug=False, stop_after=None):
    NT = S // 512
    NB = S // 128
    nc = bass.Bass("TRN2", target_bir_lowering=False)
    pg = Prog(nc)

    def din(name, shape, dt=F32):
        return nc.dram_tensor(name, list(shape), dt, kind="ExternalInput").ap()

    def dscr(name, shape, dt):
        return nc.dram_tensor(name, list(shape), dt, kind=("ExternalOutput" if debug else "Internal")).ap()

    x_in = din("x", [NSEQ, S, D])
    w_in = din("w_in", [L, D, INC])
    w_branch = din("w_branch", [L, 4, 256, D])
    w_out = din("w_out", [L, D, D])
    w_up = din("w_up", [L, D, 2 * DFF])
    w_down = din("w_down", [L, DFF, D])
    pp_in = din("pp", [128, L * PPL + 8])
    dl_in = din("dl", [L, 128])
    nab_in = din("nab", [L, 4, 3, 8, 128, 512])
    cm_in = din("cmask", [20, 128, 512])
    rope_in = {k: (din("rope%s_c" % k, [128, S]), din("rope%s_s" % k, [128, S])) for k in "ACD"}
    perm_in = din("perms", [128, 3, 128])
    ident_in = din("ident", [128, 128])
    y_out = nc.dram_tensor("y", [NSEQ, S, D], F32, kind="ExternalOutput").ap()

    xTa = dscr("xTa", [KC, 128, S], F32)
    xTb = dscr("xTb", [KC, 128, S], F32)
    hT = dscr("hT", [KC, 128, S], BF16)
    oT = dscr("oT", [KC, 128, S], BF16)
    H_x = Hd("x_in")
    H_y = Hd("y")
    H_xTa, H_xTb, H_hT, H_oT = Hd("xTa"), Hd("xTb"), Hd("hT"), Hd("oT")
    H_w = Hd("weights")
    xTa_v = xTa.rearrange("k p s -> p k s")
    xTb_v = xTb.rearrange("k p s -> p k s")
    hT_v = hT.rearrange("k p s -> p k s")
    oT_v = oT.rearrange("k p s -> p k s")

    ident = nc.alloc_sbuf_tensor("sb_ident", [128, 128], F32)
    onesf = nc.alloc_sbuf_tensor("sb_onesf", [128, 128], F32)
    blk64 = nc.alloc_sbuf_tensor("sb_blk64", [128, 128], F32)
    perms = nc.alloc_sbuf_tensor("sb_perms", [128, 3, 128], BF16)
    pp = nc.alloc_sbuf_tensor("sb_pp", [128, L * PPL + 8], F32)
    dlr = nc.alloc_sbuf_tensor("sb_dlr", [1, L * 128], F32)
    lamw = nc.alloc_sbuf_tensor("sb_lamw", [1, 80], F32)
    neglam = nc.alloc_sbuf_tensor("sb_neglam", [64, 2 * L], F32)
    gsub = nc.alloc_sbuf_tensor("sb_gsub", [64, L], F32)
    epsb = nc.alloc_sbuf_tensor("sb_epsb", [128, 1], F32)
    dummy = nc.alloc_sbuf_tensor("sb_dummy", [128, 8], F32)
    H_c = Hd("consts")
    ps = nc.alloc_psum_tensor("psum_all", [128, 8, 512], F32)
    BK = [Hd("bank%d" % i) for i in range(8)]

    base = (nc.sbuf_base + 31) // 32 * 32
    top = nc.sbuf_top
    slab = nc.alloc_sbuf_tensor("sb_slab", [128, top - base - 64], U8)
    ar = Arena(nc, base, base + top - base - 64)

    def dma(eng, out, in_, reads, writes, semh):
        return pg.op(eng, lambda e: e.dma_start(out=out, in_=in_), reads, writes, dma=semh)

    def act(out, in_, func, reads, writes, scale=None, bias=None):
        kw = {}
        if scale is not None:
            kw["scale"] = scale
        if bias is not None:
            kw["bias"] = bias
        return pg.op("act", lambda e: e.activation(out=out, in_=in_, func=func, **kw), reads, writes)

    def tt(eng, out, in0, in1, op, reads, writes):
        return pg.op(eng, lambda e: e.tensor_tensor(out=out, in0=in0, in1=in1, op=op), reads, writes)

    def stt(out, in0, scalar, in1, op0, op1, reads, writes):
        return pg.op("dve", lambda e: e.scalar_tensor_tensor(out=out, in0=in0, scalar=scalar, in1=in1, op0=op0, op1=op1),
                     reads, writes)

    def ts(eng, out, in0, s1, s2, op0, op1, reads, writes):
        if op1 is None:
            return pg.op(eng, lambda e: e.tensor_scalar(out=out, in0=in0, scalar1=s1, scalar2=None, op0=op0), reads, writes)
        return pg.op(eng, lambda e: e.tensor_scalar(out=out, in0=in0, scalar1=s1, scalar2=s2, op0=op0, op1=op1), reads, writes)

    def recip(out, in_, reads, writes):
        return pg.op("dve", lambda e: e.reciprocal(out=out, in_=in_), reads, writes)

    def mmgroup(items, reads, writes):
        def fn(e):
            ins = None
            for (o_, l_, r_, st, sp_, tp) in items:
                if tp is None:
                    ins = e.matmul(o_, lhsT=l_, rhs=r_, start=st, stop=sp_)
                else:
                    ins = e.matmul(o_, lhsT=l_, rhs=r_, start=st, stop=sp_, tile_position=tp)
            return ins
        return pg.op("pe", fn, reads, writes)

    dma("sp", ident[:], ident_in, [H_w], [H_c], H_c)
    dma("sp", pp[:], pp_in, [H_w], [H_c], H_c)
    dma("sp", dlr[:], dl_in.rearrange("(o l) n -> o (l n)", o=1), [H_w], [H_c], H_c)
    dma("pool", perms[:], perm_in, [H_w], [H_c], H_c)
    H_c2 = Hd("consts2")
    pg.op("pool", lambda e: e.memset(onesf[:], 1.0), [], [H_c2])
    pg.op("pool", lambda e: e.memset(blk64[:], 0.0), [], [H_c2])
    pg.op("pool", lambda e: e.memset(blk64[0:64, 0:64], 1.0), [], [H_c2])
    pg.op("pool", lambda e: e.memset(blk64[64:128, 64:128], 1.0), [], [H_c2])
    pg.op("pool", lambda e: e.memset(epsb[:], EPS), [], [H_c2])
    CR = [H_c, H_c2]
    H_lam = Hd("lam")
    for l in range(L):
        lam_init = 0.8 - 0.6 * math.exp(-0.3 * l)
        o = l * 128
        tt("dve", lamw[0:1, 0:32], dlr[0:1, o:o + 32], dlr[0:1, o + 32:o + 64], ALU.mult, CR, [H_lam])
        tt("dve", lamw[0:1, 32:64], dlr[0:1, o + 64:o + 96], dlr[0:1, o + 96:o + 128], ALU.mult, CR + [H_lam], [H_lam])
        pg.op("dve", lambda e: e.tensor_reduce(out=lamw[0:1, 64:66], in_=lamw[0:1, 0:64].rearrange("o (a b) -> o a b", a=2),
                                               axis=AX.X, op=ALU.add), [H_lam], [H_lam])
        act(lamw[0:1, 66:68], lamw[0:1, 64:66], AF.Exp, [H_lam], [H_lam])
        for c in range(2):
            stt(lamw[0:1, 68 + c:69 + c], lamw[0:1, 67:68], -lam_init, lamw[0:1, 66:67], ALU.add, ALU.subtract, [H_lam], [H_lam])
        mmgroup([(ps[0:64, 7, 0:2], onesf[0:1, 0:64], lamw[0:1, 68:70], True, True, None)], [H_lam, H_c2], [BK[7]])
        pg.op("dve", lambda e, l=l: e.tensor_copy(out=neglam[:, 2 * l:2 * l + 2], in_=ps[0:64, 7, 0:2]), [BK[7]], [H_lam])
        ts("dve", gsub[:, l:l + 1], pp[0:64, l * PPL + 194:l * PPL + 195], 1.0 - lam_init, None, ALU.mult, None, CR + [H_lam], [H_lam])
    CR = CR + [H_lam]

    def rms_tile(xt, H_xt, ncols, gcol, out_bf, H_out, A, tagbank):
        sq = A["sq"]
        H_sq = A["H_sq"]
        rstd = A["rstd"]
        H_rstd = A["H_rstd"]
        bank = tagbank
        items = []
        for kc in range(KC):
            s = sq[kc % 2]
            hs = H_sq[kc % 2]
            act(s[:, 0:ncols], xt[:, kc, 0:ncols], AF.Square, [H_xt], [hs])
            mmgroup([(ps[:, bank, 0:ncols], onesf[:, :], s[:, 0:ncols], kc == 0, kc == KC - 1, None)], [hs, H_c2], [BK[bank]])
        act(rstd[:, 0:ncols], ps[:, bank, 0:ncols], AF.Sqrt, [BK[bank]] + CR, [H_rstd], scale=1.0 / D, bias=epsb[:, 0:1])
        recip(rstd[:, 0:ncols], rstd[:, 0:ncols], [H_rstd], [H_rstd])
        for kc in range(KC):
            stt(out_bf[:, kc, 0:ncols], xt[:, kc, 0:ncols], pp[:, gcol + kc:gcol + kc + 1], rstd[:, 0:ncols],
                ALU.mult, ALU.mult, [H_xt, H_rstd] + CR, [H_out])

    def phase_T(seq):
        ar.reset()
        xtok = [ar.alloc([128, 4, D], F32, "xtok") for _ in range(2)]
        Hk = [Hd("xtok%d" % i) for i in range(2)]
        xt = [ar.alloc([128, KC, 512], F32, "xt") for _ in range(2)]
        Hx = [Hd("xtT%d" % i) for i in range(2)]
        for t in range(NT):
            b = t % 2
            dma("sp", xtok[b][:], x_in[seq, t * 512:(t + 1) * 512, :].rearrange("(b p) d -> p b d", p=128), [H_x], [Hk[b]], Hk[b])
            for kc in range(KC):
                bank = kc % 4
                def fn(e, kc=kc, bank=bank, b=b):
                    ins = None
                    for blk in range(4):
                        ins = e.transpose(ps[:, bank, blk * 128:(blk + 1) * 128], xtok[b][:, blk, kc * 128:(kc + 1) * 128], ident[:, :])
                    return ins
                pg.op("pe", fn, [Hk[b]] + CR, [BK[bank]])
                if kc % 2 == 0:
                    act(xt[b][:, kc, :], ps[:, bank, :], AF.Copy, [BK[bank]], [Hx[b]])
                else:
                    pg.op("dve", lambda e, kc=kc, bank=bank, b=b: e.tensor_copy(out=xt[b][:, kc, :], in_=ps[:, bank, :]), [BK[bank]], [Hx[b]])
            dma("pool", xTa_v[:, :, t * 512:(t + 1) * 512], xt[b][:], [Hx[b]], [H_xTa], Hx[b])
        pg.barrier()

    def phase_Z(seq, src_v, H_src):
        ar.reset()
        A = {"sq": [ar.alloc([128, 512], F32, "sq") for _ in range(2)], "H_sq": [Hd("sq0"), Hd("sq1")],
             "rstd": ar.alloc([128, 512], F32, "rstd"), "H_rstd": Hd("rstd")}
        xt = [ar.alloc([128, KC, 512], F32, "xt") for _ in range(2)]
        Hx = [Hd("zx%d" % i) for i in range(2)]
        yn = ar.alloc([128, KC, 512], F32, "yn")
        H_yn = Hd("yn")
        ytok = [ar.alloc([128, 4, D], F32, "ytok") for _ in range(2)]
        Hyt = [Hd("ytok%d" % i) for i in range(2)]
        gcol = L * PPL
        for t in range(NT):
            b = t % 2
            dma("sp", xt[b][:], src_v[:, :, t * 512:(t + 1) * 512], [H_src], [Hx[b]], Hx[b])
            sq, H_sq, rstd, H_rstd = A["sq"], A["H_sq"], A["rstd"], A["H_rstd"]
            for kc in range(KC):
                act(sq[kc % 2][:], xt[b][:, kc, :], AF.Square, [Hx[b]], [H_sq[kc % 2]])
                mmgroup([(ps[:, 4, :], onesf[:, :], sq[kc % 2][:], kc == 0, kc == KC - 1, None)], [H_sq[kc % 2], H_c2], [BK[4]])
            act(rstd[:], ps[:, 4, :], AF.Sqrt, [BK[4]] + CR, [H_rstd], scale=1.0 / D, bias=epsb[:, 0:1])
            recip(rstd[:], rstd[:], [H_rstd], [H_rstd])
            for kc in range(KC):
                stt(yn[:, kc, :], xt[b][:, kc, :], pp[:, gcol + kc:gcol + kc + 1], rstd[:], ALU.mult, ALU.mult,
                    [Hx[b], H_rstd] + CR, [H_yn])
            for blk in range(4):
                for half in range(2):
                    bank = (blk * 2 + half) % 4
                    def fn(e, blk=blk, half=half, bank=bank):
                        ins = None
                        for q in range(4):
                            kc = half * 4 + q
                            ins = e.transpose(ps[:, bank, q * 128:(q + 1) * 128], yn[:, kc, blk * 128:(blk + 1) * 128], ident[:, :])
                        return ins
                    pg.op("pe", fn, [H_yn] + CR, [BK[bank]])
                    if half == 0:
                        act(ytok[b][:, blk, 0:512], ps[:, bank, :], AF.Copy, [BK[bank]], [Hyt[b]])
                    else:
                        pg.op("dve", lambda e, blk=blk, bank=bank, b=b: e.tensor_copy(out=ytok[b][:, blk, 512:1024], in_=ps[:, bank, :]),
                              [BK[bank]], [Hyt[b]])
            dma("pool", y_out[seq, t * 512:(t + 1) * 512, :].rearrange("(b p) d -> p b d", p=128), ytok[b][:], [Hyt[b]], [H_y], Hyt[b])
        pg.barrier()

    def phase_0(l, src_v, H_src):
        ar.reset()
        A = {"sq": [ar.alloc([128, 512], F32, "sq") for _ in range(2)], "H_sq": [Hd("sq0"), Hd("sq1")],
             "rstd": ar.alloc([128, 512], F32, "rstd"), "H_rstd": Hd("rstd")}
        xt = [ar.alloc([128, KC, 512], F32, "xt") for _ in range(2)]
        Hx = [Hd("p0x%d" % i) for i in range(2)]
        hb = [ar.alloc([128, KC, 512], BF16, "hb") for _ in range(2)]
        Hh = [Hd("p0h%d" % i) for i in range(2)]
        for t in range(NT):
            b = t % 2
            dma("sp", xt[b][:], src_v[:, :, t * 512:(t + 1) * 512], [H_src], [Hx[b]], Hx[b])
            rms_tile(xt[b], Hx[b], 512, l * PPL + 0, hb[b], Hh[b], A, 4 + (t % 2))
            dma("pool", hT_v[:, :, t * 512:(t + 1) * 512], hb[b][:], [Hh[b]], [H_hT], Hh[b])
        pg.barrier()

    def branch_pass(l, br):
        ar.reset()
        kind = "ABCD"[br]
        HV = 2 if kind == "D" else 4
        RQ = ar.alloc([128, 2, S], BF16, "RQ")
        RK = ar.alloc([128, 2, S], BF16, "RK")
        RV = ar.alloc([128, NB, HV, 65], BF16, "RV")
        mark = ar.cur
        H_RQ = [Hd("RQ%d" % t) for t in range(NT)]
        H_RK = [Hd("RK%d" % t) for t in range(NT)]
        H_RV = [Hd("RV%d" % t) for t in range(NT)]
        H_RVo = Hd("RVones")
        if kind == "D":
            c0 = 9 * 256
            ncol = 256 + 128 + 128
        else:
            c0 = br * 768
            ncol = 768
        W = ar.alloc([128, KC, ncol], BF16, "W")
        H_W = Hd("W")
        for kc in range(KC):
            dma("pool", W[:, kc, :], w_in[l, kc * 128:(kc + 1) * 128, c0:c0 + ncol], [H_w], [H_W], H_W)
        FL = set(os.environ.get("FLAGS", "").split(","))
        if "nomem" not in FL:
            pg.op("pool", lambda e: e.memset(RV[:, :, :, 64:65], 1.0), [], [H_RVo])
        ht = [ar.alloc([128, KC, 512], BF16, "ht") for _ in range(2)]
        Hht = [Hd("ht%d" % i) for i in range(2)]
        rot = kind in "ACD"
        if rot:
            cs_t = [ar.alloc([128, 2, 512], F32, "cs") for _ in range(2)]
            Hcs = [Hd("cs%d" % i) for i in range(2)]
            a16 = [ar.alloc([128, 512], BF16, "a16") for _ in range(2)]
            Ha16 = [Hd("a16_%d" % i) for i in range(2)]
            t1 = [ar.alloc([128, 512], F32, "t1") for _ in range(2)]
            Ht1 = [Hd("t1_%d" % i) for i in range(2)]
            t2 = [ar.alloc([128, 512], F32, "t2") for _ in range(2)]
            Ht2 = [Hd("t2_%d" % i) for i in range(2)]
            pidx = "ACD".index(kind)
        if kind == "D":
            sqd = [ar.alloc([128, 512], F32, "sqd") for _ in range(2)]
            Hsqd = [Hd("sqd%d" % i) for i in range(2)]
            rsd = [ar.alloc([128, 512], F32, "rsd") for _ in range(2)]
            Hrsd = [Hd("rsd%d" % i) for i in range(2)]

        if kind == "D":
            fm = [("q", RQ, 0, [(0, 128)]), ("q", RQ, 1, [(128, 128)]),
                  ("k", RK, 0, [(256, 128)]), ("k", RK, 1, [(320, 64), (256, 64)])]
            vcol = 384
            vw = 128
        else:
            fm = [("q", RQ, 0, [(0, 128)]), ("q", RQ, 1, [(128, 128)]),
                  ("k", RK, 0, [(256, 128)]), ("k", RK, 1, [(384, 128)])]
            vcol = 512
            vw = 256
        cnt = 0
        for t in range(NT):
            b = t % 2
            tsl = slice(t * 512, (t + 1) * 512)
            dma("sp", ht[b][:], hT_v[:, :, tsl], [H_hT], [Hht[b]], Hht[b])
            if rot:
                dma("sp", cs_t[b][:, 0, :], rope_in[kind][0][:, tsl], [H_w], [Hcs[b]], Hcs[b])
                dma("sp", cs_t[b][:, 1, :], rope_in[kind][1][:, tsl], [H_w], [Hcs[b]], Hcs[b])
            for (qk, dst, dch, pieces) in fm:
                bank = cnt % 2
                pb = 2 + cnt % 2
                sb_ = 4 + cnt % 2
                u = cnt % 2
                cnt += 1
                Hdst = (H_RQ if qk == "q" else H_RK)[t]
                items = []
                for kc in range(KC):
                    mo = 0
                    for (pc0, pw) in pieces:
                        items.append((ps[mo:mo + pw, bank, :], W[:, kc, pc0:pc0 + pw], ht[b][:, kc, :], kc == 0, kc == KC - 1, None))
                        mo += pw
                if len(pieces) == 1:
                    mmgroup(items, [H_W, Hht[b]], [BK[bank]])
                else:
                    i0 = [it for i, it in enumerate(items) if i % 2 == 0]
                    i1 = [it for i, it in enumerate(items) if i % 2 == 1]
                    mmgroup(i0 + i1, [H_W, Hht[b]], [BK[bank]])
                dsl = dst[:, dch, tsl]
                if not rot or "norot" in FL:
                    if cnt % 2 == 0:
                        act(dsl, ps[:, bank, :], AF.Copy, [BK[bank]], [Hdst])
                    else:
                        pg.op("dve", lambda e, dsl=dsl, bank=bank: e.tensor_copy(out=dsl, in_=ps[:, bank, :]), [BK[bank]], [Hdst])
                    continue
                if kind == "D":
                    gcol = l * PPL + (192 if qk == "q" else 193)
                    gap = pp[:, gcol:gcol + 1]
                    act(a16[u][:], ps[:, bank, :], AF.Identity, [BK[bank]] + CR, [Ha16[u]], scale=gap)
                    act(sqd[u][:], ps[:, bank, :], AF.Square, [BK[bank]], [Hsqd[u]])
                    mmgroup([(ps[:, sb_, :], blk64[:, :], sqd[u][:], True, True, None)], [Hsqd[u], H_c2], [BK[sb_]])
                    act(rsd[u][:], ps[:, sb_, :], AF.Sqrt, [BK[sb_]] + CR, [Hrsd[u]], scale=1.0 / 64, bias=epsb[:, 0:1])
                    recip(rsd[u][:], rsd[u][:], [Hrsd[u]], [Hrsd[u]])
                    stt(t1[u][:], ps[:, bank, :], gap, cs_t[b][:, 0, :], ALU.mult, ALU.mult, [BK[bank], Hcs[b]] + CR, [Ht1[u]])
                else:
                    act(a16[u][:], ps[:, bank, :], AF.Copy, [BK[bank]], [Ha16[u]])
                    if "rotA" in FL:
                        pg.op("dve", lambda e, dsl=dsl, u=u: e.tensor_copy(out=dsl, in_=a16[u][:]), [Ha16[u]], [Hdst])
                        continue
                    if "rotA2" in FL:
                        pg.op("dve", lambda e, u=u, bank=bank: e.tensor_copy(out=t1[u][:], in_=ps[:, bank, :]), [BK[bank]] + ([Ha16[u]] if "ser" in FL else []), [Ht1[u]])
                    else:
                        tt("dve", t1[u][:], ps[:, bank, :], cs_t[b][:, 0, :], ALU.mult, [BK[bank], Hcs[b]], [Ht1[u]])
                if "rotB" in FL:
                    if "rotBact" in FL:
                        act(dsl, t1[u][:], AF.Copy, [Ht1[u]], [Hdst])
                    else:
                        pg.op("dve", lambda e, dsl=dsl, u=u: e.tensor_copy(out=dsl, in_=t1[u][:]), [Ht1[u]], [Hdst])
                    continue
                mmgroup([(ps[:, pb, :], perms[:, pidx, :], a16[u][:], True, True, None)], [Ha16[u]] + CR, [BK[pb]])
                if "rotC" in FL:
                    pg.op("dve", lambda e, dsl=dsl, pb=pb: e.tensor_copy(out=dsl, in_=ps[:, pb, :]), [BK[pb]], [Hdst])
                    continue
                tt("dve", t2[u][:], ps[:, pb, :], cs_t[b][:, 1, :], ALU.mult, [BK[pb], Hcs[b]], [Ht2[u]])
                if kind == "D":
                    tt("pool", t1[u][:], t1[u][:], t2[u][:], ALU.add, [Ht1[u], Ht2[u]], [Ht1[u]])
                    tt("dve", dsl, t1[u][:], rsd[u][:], ALU.mult, [Ht1[u], Hrsd[u]], [Hdst])
                else:
                    tt("pool" if "pooladd" in FL else "dve", dsl, t1[u][:], t2[u][:], ALU.add, [Ht1[u], Ht2[u]], [Hdst])
            for blk in range(4 if "nov" not in FL else 0):
                bank = 6 + blk % 2
                items = [(ps[:, bank, 0:vw], ht[b][:, kc, blk * 128:(blk + 1) * 128], W[:, kc, vcol:vcol + vw], kc == 0, kc == KC - 1, None)
                         for kc in range(KC)]
                mmgroup(items, [H_W, Hht[b]], [BK[bank]])
                gb = t * 4 + blk
                src = ps[:, bank, 0:vw].rearrange("p (h d) -> p h d", h=HV)
                if blk % 2 == 0:
                    act(RV[:, gb, :, 0:64], src, AF.Copy, [BK[bank]], [H_RV[t]])
                else:
                    pg.op("dve", lambda e, gb=gb, src=src: e.tensor_copy(out=RV[:, gb, :, 0:64], in_=src), [BK[bank]], [H_RV[t]])

        if os.environ.get("SUBSTOP") == "proj":
            pg.barrier()
            return
        pg.barrier()
        ar.cur = mark
        PT = [ar.alloc([128, 1024], BF16, "PT") for _ in range(3)]
        HPT = [Hd("PT%d" % i) for i in range(3)]
        fsb = [ar.alloc([65, 512], F32, "fsb") for _ in range(4)]
        Hfsb = [Hd("fsb%d" % i) for i in range(4)]
        rr = [ar.alloc([65, 512], F32, "rr") for _ in range(4)]
        Hrr = [Hd("rr%d" % i) for i in range(4)]
        ost = [ar.alloc([64, 512], BF16, "ost") for _ in range(4)]
        Host = [Hd("ost%d" % i) for i in range(4)]
        if kind == "A":
            o12 = [ar.alloc([64, 512], F32, "o12") for _ in range(4)]
            Ho12 = [Hd("o12_%d" % i) for i in range(4)]
            dd = [ar.alloc([64, 512], F32, "dd") for _ in range(2)]
            Hdd = [Hd("dd%d" % i) for i in range(2)]
            sqa = [ar.alloc([64, 512], F32, "sqa") for _ in range(2)]
            Hsqa = [Hd("sqa%d" % i) for i in range(2)]
        if kind in "BC":
            sbx = [ar.alloc([128, 1024], F32, "sbx") for _ in range(2)]
            Hsbx = [Hd("sbx%d" % i) for i in range(2)]
        if kind == "B":
            bsl = [ar.alloc([128, 2, 512], F32, "bsl") for _ in range(3)]
            Hbsl = [Hd("bsl%d" % i) for i in range(3)]
        if kind == "C":
            cm = ar.alloc([128, 20, 512], F32, "cm")
            H_cm = Hd("cm")
            for g in range(4):
                dma("sp", cm[:, g * 5:(g + 1) * 5, :], cm_in[g * 5:(g + 1) * 5].rearrange("j p n -> p j n"), [H_w], [H_cm], H_cm)
        allK = H_RK
        allV = H_RV + [H_RVo]
        scale = {"A": 32 ** -0.5, "B": 0.125, "C": 0.125, "D": 0.125}[kind]
        state = {"u": 0, "pt": 0, "sp": 0, "f": 0, "bs": 0, "bl": 0}

        def finalize(accset, unit_infos, t):
            tsl = slice(t * 512, (t + 1) * 512)
            if kind != "A":
                for i in range(2):
                    bank = accset[i]
                    f = state["f"] % 4
                    state["f"] += 1
                    pg.op("dve", lambda e, f=f, bank=bank: e.tensor_copy(out=fsb[f][0:65, :], in_=ps[0:65, bank, :]), [BK[bank]], [Hfsb[f]])
                    recip(rr[f][64:65, :], fsb[f][64:65, :], [Hfsb[f]], [Hrr[f]])
                    mmgroup([(ps[0:64, bank, :], onesf[64:65, 0:64], rr[f][64:65, :], True, True, None)], [Hrr[f], H_c2], [BK[bank]])
                    tt("dve", ost[f][:], fsb[f][0:64, :], ps[0:64, bank, :], ALU.mult, [Hfsb[f], BK[bank]], [Host[f]])
                    ch, pb_ = unit_infos[i]
                    dma("pool", oT_v[pb_:pb_ + 64, ch, tsl], ost[f][:], [Host[f]], [H_oT], Host[f])
                return
            fs = []
            for i in range(2):
                bank = accset[i]
                f = state["f"] % 4
                state["f"] += 1
                fs.append(f)
                pg.op("dve", lambda e, f=f, bank=bank: e.tensor_copy(out=fsb[f][0:65, :], in_=ps[0:65, bank, :]), [BK[bank]], [Hfsb[f]])
                recip(rr[f][64:65, :], fsb[f][64:65, :], [Hfsb[f]], [Hrr[f]])
                mmgroup([(ps[0:64, bank, :], onesf[64:65, 0:64], rr[f][64:65, :], True, True, None)], [Hrr[f], H_c2], [BK[bank]])
                tt("dve", o12[f][:], fsb[f][0:64, :], ps[0:64, bank, :], ALU.mult, [Hfsb[f], BK[bank]], [Ho12[f]])
            u = (state["f"] // 2) % 2
            stt(dd[u][:], o12[fs[1]][:], neglam[:, 2 * l:2 * l + 1], o12[fs[0]][:], ALU.mult, ALU.add,
                [Ho12[fs[0]], Ho12[fs[1]]] + CR, [Hdd[u]])
            tt("pool", sqa[u][:], dd[u][:], dd[u][:], ALU.mult, [Hdd[u]], [Hsqa[u]])
            bank = accset[0]
            mmgroup([(ps[0:64, bank, :], onesf[0:64, 0:64], sqa[u][:], True, True, None)], [Hsqa[u], H_c2], [BK[bank]])
            act(sqa[u][:], ps[0:64, bank, :], AF.Sqrt, [BK[bank]] + CR, [Hsqa[u]], scale=1.0 / 64, bias=epsb[0:64, 0:1])
            recip(sqa[u][:], sqa[u][:], [Hsqa[u]], [Hsqa[u]])
            f = fs[0]
            stt(ost[f][:], dd[u][:], gsub[:, l:l + 1], sqa[u][:], ALU.mult, ALU.mult, [Hdd[u], Hsqa[u]] + CR, [Host[f]])
            ch, pb_ = unit_infos[0]
            dma("pool", oT_v[pb_:pb_ + 64, ch, tsl], ost[f][:], [Host[f]], [H_oT], Host[f])

        def attn_unit(t, qk_items, v_aps, kbs, bias_fn, unit_infos, bias_pre=None):
            accset = (4, 5) if state["u"] % 2 == 0 else (6, 7)
            state["u"] += 1
            n = len(kbs)

            def emit_qk(j):
                kb = kbs[j]
                if bias_pre is not None:
                    bias_pre(kb)
                sp_ = state["sp"] % 2
                state["sp"] += 1
                banks = (2 * sp_, 2 * sp_ + 1)
                items = []
                for i in range(2):
                    lhsT, rhs, tp = qk_items[i](kb)
                    items.append((ps[:, banks[i], :], lhsT, rhs, True, True, tp))
                mmgroup(items, [H_RQ[t], H_RK[kb // 4]], [BK[banks[0]], BK[banks[1]]])
                return banks

            pend = emit_qk(0)
            for j in range(n):
                kb = kbs[j]
                banks = pend
                p = state["pt"] % 3
                state["pt"] += 1
                if bias_fn is None:
                    act(PT[p][:], ps[:, banks[0]:banks[0] + 2, :], AF.Exp, [BK[banks[0]], BK[banks[1]]], [HPT[p]], scale=scale)
                else:
                    sx = state["bs"] % 2
                    for i in range(2):
                        bap, bh = bias_fn(i, kb)
                        stt(sbx[sx][:, i * 512:(i + 1) * 512], ps[:, banks[i], :], scale, bap, ALU.mult, ALU.add,
                            [BK[banks[i]], bh], [Hsbx[sx]])
                    state["bs"] += 1
                    act(PT[p][:], sbx[sx][:], AF.Exp, [Hsbx[sx]], [HPT[p]])
                if j + 1 < n:
                    pend = emit_qk(j + 1)
                items = []
                for i in range(2):
                    items.append((ps[0:65, accset[i], :], v_aps[i](kb), PT[p][:, i * 512:(i + 1) * 512], j == 0, j == n - 1, None))
                mmgroup(items, [HPT[p], H_RV[kb // 4], H_RVo], [BK[accset[0]], BK[accset[1]]])
            if os.environ.get("SUBSTOP") != "nofin":
                finalize(accset, unit_infos, t)

        for t in range(NT):
            if os.environ.get("SUBSTOP") in ("unit1", "nofin") and t > 0:
                break
            tsl = slice(t * 512, (t + 1) * 512)
            if kind == "A":
                for h in range(4):
                    ch, pb_ = h // 2, (h % 2) * 64
                    def mk(i, ch=ch, pb_=pb_):
                        p0 = pb_ + 32 * i
                        tp = (96, 0) if p0 == 96 else None
                        return lambda kb: (RK[p0:p0 + 32, ch, kb * 128:(kb + 1) * 128], RQ[p0:p0 + 32, ch, tsl], tp)
                    va = lambda kb, h=h: RV[:, kb, h, 0:65]
                    attn_unit(t, [mk(0), mk(1)], [va, va], list(range(NB)), None, [(0 * 2 + ch, pb_), None])
            elif kind == "D":
                for g in range(2):
                    def mk(i, g=g):
                        pb_ = 64 * i
                        kch = (0 if g == 0 else 1) if i == 0 else (1 if g == 0 else 0)
                        return lambda kb: (RK[pb_:pb_ + 64, kch, kb * 128:(kb + 1) * 128], RQ[pb_:pb_ + 64, g, tsl], None)
                    va = lambda kb, g=g: RV[:, kb, g, 0:65]
                    attn_unit(t, [mk(0), mk(1)], [va, va], list(range(NB)), None, [(6 + g, 0), (6 + g, 64)])
            else:
                for hp in range(2):
                    def mk(i, hp=hp):
                        pb_ = 64 * i
                        return lambda kb: (RK[pb_:pb_ + 64, hp, kb * 128:(kb + 1) * 128], RQ[pb_:pb_ + 64, hp, tsl], None)
                    vas = [(lambda kb, h=2 * hp + i: RV[:, kb, h, 0:65]) for i in range(2)]
                    if kind == "B":
                        kbs = [kb for kb in range(4 * t - 2, 4 * t + 6) if 0 <= kb < NB]
                        var = 0 if t == 0 else (2 if t == NT - 1 else 1)
                        cache = {}
                        def bias_pre(kb, hp=hp, t=t, var=var, cache=cache):
                            s_ = state["bl"] % 3
                            state["bl"] += 1
                            j = kb - (4 * t - 2)
                            dma("sp", bsl[s_][:], nab_in[l, 2 * hp:2 * hp + 2, var, j].rearrange("h p n -> p h n"), [H_w], [Hbsl[s_]], Hbsl[s_])
                            cache[kb] = s_
                        def bias_fn(i, kb, cache=cache):
                            s_ = cache[kb]
                            return bsl[s_][:, i, :], Hbsl[s_]
                    else:
                        kbs = [kb for kb in range(4 * t - 8, 4 * t + 12) if 0 <= kb < NB]
                        bias_pre = None
                        def bias_fn(i, kb, t=t):
                            return cm[:, kb - 4 * t + 8, :], H_cm
                    br_ch = 2 * br + hp
                    attn_unit(t, [mk(0), mk(1)], vas, kbs, bias_fn, [(br_ch, 0), (br_ch, 64)], bias_pre)
        pg.barrier()

    def merge_pass(l, src_v, H_src, dst_v, H_dst):
        ar.reset()
        Wg = ar.alloc([128, KC, 4096], BF16, "Wg")
        Wbr = ar.alloc([128, 8, D], BF16, "Wbr")
        Wo = ar.alloc([128, KC, D], BF16, "Wo")
        H_Wm = Hd("Wm")
        for kc in range(KC):
            dma("pool", Wg[:, kc, :], w_in[l, kc * 128:(kc + 1) * 128, 2816:6912], [H_w], [H_Wm], H_Wm)
        for bq in range(4):
            dma("pool", Wbr[:, 2 * bq:2 * bq + 2, :], w_branch[l, bq].rearrange("(k p) n -> p k n", p=128), [H_w], [H_Wm], H_Wm)
        dma("pool", Wo[:], w_out[l].rearrange("(k p) n -> p k n", p=128), [H_w], [H_Wm], H_Wm)
        xt = [ar.alloc([128, KC, 512], F32, "xt") for _ in range(2)]
        Hx = [Hd("mx%d" % i) for i in range(2)]
        ht = [ar.alloc([128, KC, 512], BF16, "ht") for _ in range(2)]
        Hht = [Hd("mh%d" % i) for i in range(2)]
        ot = [ar.alloc([128, KC, 512], BF16, "ot") for _ in range(2)]
        Hot = [Hd("mo%d" % i) for i in range(2)]
        mg = ar.alloc([128, KC, 512], BF16, "mg")
        H_mg = Hd("mg")
        sg = [ar.alloc([128, 512], F32, "sg") for _ in range(2)]
        Hsg = [Hd("sg%d" % i) for i in range(2)]
        acc = [ar.alloc([128, 512], F32, "macc") for _ in range(2)]
        Hacc = [Hd("macc%d" % i) for i in range(2)]
        tmp = [ar.alloc([128, 512], F32, "mtmp") for _ in range(2)]
        Htmp = [Hd("mtmp%d" % i) for i in range(2)]
        cnt = 0
        for t in range(NT):
            b = t % 2
            tsl = slice(t * 512, (t + 1) * 512)
            dma("sp", ht[b][:], hT_v[:, :, tsl], [H_hT], [Hht[b]], Hht[b])
            dma("sp", ot[b][:], oT_v[:, :, tsl], [H_oT], [Hot[b]], Hot[b])
            dma("sp", xt[b][:], src_v[:, :, tsl], [H_src], [Hx[b]], Hx[b])
            for oc in range(KC):
                a = oc % 2
                for bq in range(4):
                    gb = cnt % 2
                    mb = 2 + cnt % 2
                    u = cnt % 2
                    cnt += 1
                    col = bq * D + oc * 128
                    mmgroup([(ps[:, gb, :], Wg[:, kc, col:col + 128], ht[b][:, kc, :], kc == 0, kc == KC - 1, None) for kc in range(KC)],
                            [H_Wm, Hht[b]], [BK[gb]])
                    mmgroup([(ps[:, mb, :], Wbr[:, 2 * bq + j, oc * 128:(oc + 1) * 128], ot[b][:, 2 * bq + j, :], j == 0, j == 1, None)
                             for j in range(2)], [H_Wm, Hot[b]], [BK[mb]])
                    act(sg[u][:], ps[:, gb, :], AF.Sigmoid, [BK[gb]], [Hsg[u]])
                    if bq == 0:
                        tt("dve", acc[a][:], sg[u][:], ps[:, mb, :], ALU.mult, [Hsg[u], BK[mb]], [Hacc[a]])
                    elif bq < 3:
                        tt("dve", tmp[u][:], sg[u][:], ps[:, mb, :], ALU.mult, [Hsg[u], BK[mb]], [Htmp[u]])
                        tt("pool", acc[a][:], acc[a][:], tmp[u][:], ALU.add, [Hacc[a], Htmp[u]], [Hacc[a]])
                    else:
                        tt("dve", tmp[u][:], sg[u][:], ps[:, mb, :], ALU.mult, [Hsg[u], BK[mb]], [Htmp[u]])
                        tt("pool", mg[:, oc, :], acc[a][:], tmp[u][:], ALU.add, [Hacc[a], Htmp[u]], [H_mg])
            for oc in range(KC):
                bank = 4 + oc % 4
                mmgroup([(ps[:, bank, :], Wo[:, kc, oc * 128:(oc + 1) * 128], mg[:, kc, :], kc == 0, kc == KC - 1, None) for kc in range(KC)],
                        [H_Wm, H_mg], [BK[bank]])
                tt("dve", xt[b][:, oc, :], xt[b][:, oc, :], ps[:, bank, :], ALU.add, [Hx[b], BK[bank]], [Hx[b]])
            dma("pool", dst_v[:, :, tsl], xt[b][:], [Hx[b]], [H_dst], Hx[b])
        pg.barrier()

    def mlp_pass(l, src_v, H_src, dst_v, H_dst):
        ar.reset()
        Wup = ar.alloc([128, KC, 2 * DFF], BF16, "Wup")
        Wdn = ar.alloc([128, NFC, D], BF16, "Wdn")
        H_Wf = Hd("Wf")
        for kc in range(KC):
            dma("pool", Wup[:, kc, :], w_up[l, kc * 128:(kc + 1) * 128, :], [H_w], [H_Wf], H_Wf)
        for j in range(0, NFC, 2):
            dma("pool", Wdn[:, j:j + 2, :], w_down[l, j * 128:(j + 2) * 128, :].rearrange("(k p) n -> p k n", p=128), [H_w], [H_Wf], H_Wf)
        A = {"sq": [ar.alloc([128, 512], F32, "sq") for _ in range(2)], "H_sq": [Hd("sq0"), Hd("sq1")],
             "rstd": ar.alloc([128, 512], F32, "rstd"), "H_rstd": Hd("rstd")}
        xt = [ar.alloc([128, KC, 512], F32, "xt") for _ in range(1)]
        Hx = [Hd("fx%d" % i) for i in range(1)]
        h2 = ar.alloc([128, KC, 512], BF16, "h2")
        H_h2 = Hd("h2")
        gT = ar.alloc([128, NFC, 512], BF16, "gT")
        H_gT = Hd("gT")
        cv = [ar.alloc([128, 512], F32, "cv") for _ in range(2)]
        Hcv = [Hd("cv%d" % i) for i in range(2)]
        cg = [ar.alloc([128, 512], F32, "cg") for _ in range(2)]
        Hcg = [Hd("cg%d" % i) for i in range(2)]
        ntile = (S + 509) // 510
        po = l * PPL
        cnt = 0
        for i in range(ntile):
            b = 0
            c0 = 510 * i
            nv = min(510, S - c0)
            lo = max(c0 - 1, 0)
            hi = min(c0 + nv + 1, S)
            off = lo - (c0 - 1)
            nl = hi - lo
            dma("sp", xt[b][:, :, off:off + nl], src_v[:, :, lo:hi], [H_src], [Hx[b]], Hx[b])
            ncols = off + nl
            if off > 0:
                pg.op("pool", lambda e: e.memset(xt[0][:, :, 0:1], 0.0), [Hx[b]], [Hx[b]])
            rms_tile(xt[b], Hx[b], ncols, po + 8, h2, H_h2, A, 6)
            if off > 0:
                pg.op("pool", lambda e: e.memset(h2[:, :, 0:1], 0.0), [H_h2], [H_h2])
            if ncols < 512:
                pg.op("pool", lambda e, ncols=ncols: e.memset(h2[:, :, ncols:512], 0.0), [H_h2], [H_h2])
            if off > 0:
                pass
            for j in range(NFC):
                u = cnt % 2
                cnt += 1
                bv = 0 + u * 2
                bg = 1 + u * 2
                mmgroup([(ps[:, bv, :], Wup[:, kc, j * 128:(j + 1) * 128], h2[:, kc, :], kc == 0, kc == KC - 1, None) for kc in range(KC)],
                        [H_Wf, H_h2], [BK[bv]])
                mmgroup([(ps[:, bg, :], Wup[:, kc, DFF + j * 128:DFF + (j + 1) * 128], h2[:, kc, :], kc == 0, kc == KC - 1, None) for kc in range(KC)],
                        [H_Wf, H_h2], [BK[bg]])
                for (bank, cbuf, Hc, chn) in ((bv, cv[u], Hcv[u], j), (bg, cg[u], Hcg[u], NFC + j)):
                    w0 = pp[:, po + 16 + 0 * 44 + chn:po + 16 + 0 * 44 + chn + 1]
                    w1 = pp[:, po + 16 + 1 * 44 + chn:po + 16 + 1 * 44 + chn + 1]
                    w2 = pp[:, po + 16 + 2 * 44 + chn:po + 16 + 2 * 44 + chn + 1]
                    bb = pp[:, po + 148 + chn:po + 148 + chn + 1]
                    act(cbuf[:, 0:510], ps[:, bank, 1:511], AF.Identity, [BK[bank]] + CR, [Hc], scale=w1, bias=bb)
                    stt(cbuf[:, 0:510], ps[:, bank, 0:510], w0, cbuf[:, 0:510], ALU.mult, ALU.add, [BK[bank], Hc] + CR, [Hc])
                    stt(cbuf[:, 0:510], ps[:, bank, 2:512], w2, cbuf[:, 0:510], ALU.mult, ALU.add, [BK[bank], Hc] + CR, [Hc])
                act(cg[u][:, 0:510], cg[u][:, 0:510], AF.Gelu, [Hcg[u]], [Hcg[u]])
                tt("pool", gT[:, j, 0:510], cg[u][:, 0:510], cv[u][:, 0:510], ALU.mult, [Hcg[u], Hcv[u]], [H_gT])
            for oc in range(KC):
                bank = 4 + oc % 2
                mmgroup([(ps[:, bank, 0:510], Wdn[:, j, oc * 128:(oc + 1) * 128], gT[:, j, 0:510], j == 0, j == NFC - 1, None) for j in range(NFC)],
                        [H_Wf, H_gT], [BK[bank]])
                tt("dve", xt[b][:, oc, 1:1 + nv], xt[b][:, oc, 1:1 + nv], ps[:, bank, 0:nv], ALU.add, [Hx[b], BK[bank]], [Hx[b]])
            dma("pool", dst_v[:, :, c0:c0 + nv], xt[b][:, :, 1:1 + nv], [Hx[b]], [H_dst], Hx[b])
        pg.barrier()

    stages = []
    for seq in range(NSEQ):
        stages.append(("T", lambda seq=seq: phase_T(seq)))
        for l in range(L):
            stages.append(("P0", lambda l=l: phase_0(l, xTa_v, H_xTa)))
            for br in range(4):
                stages.append(("BR%d" % br, lambda l=l, br=br: branch_pass(l, br)))
            stages.append(("MG", lambda l=l: merge_pass(l, xTa_v, H_xTa, xTb_v, H_xTb)))
            stages.append(("FF", lambda l=l: mlp_pass(l, xTb_v, H_xTb, xTa_v, H_xTa)))
        stages.append(("Z", lambda seq=seq: phase_Z(seq, xTa_v, H_xTa)))
    for i, (nm, fn) in enumerate(stages):
        if stop_after is not None and i >= stop_after:
            break
        fn()
    pg.barrier()
    pg.op("pool", lambda e: e.memset(dummy[:], 0.0), [], [Hd("dummy")])
    pg.emit()
    return nc


_CACHE = {}


def _host_consts(S, inputs):
    c = {}
    perms = np.zeros((128, 3, 128), np.float32)
    for i, k in enumerate("ACD"):
        cs, sn, pm = _rope_tables(k, S)
        c["rope%s_c" % k] = cs
        c["rope%s_s" % k] = sn
        perms[:, i, :] = pm
    c["perms"] = perms
    c["ident"] = np.eye(128, dtype=np.float32)
    c["cmask"] = _cmask_tiles()
    c["nab"] = _na_tiles(np.asarray(inputs["na_rpb"], np.float32), S)
    c["pp"] = _pack_pp(*[np.asarray(inputs[k], np.float32) for k in
                         ("norm_attn", "norm_mlp", "conv_w", "conv_b", "qk_norm", "diff_subln", "norm_final")])
    c["dl"] = np.ascontiguousarray(np.asarray(inputs["diff_lambda"], np.float32).reshape(-1, 128))
    for k in ("w_in", "w_branch", "w_out", "w_up", "w_down"):
        c[k] = np.ascontiguousarray(np.asarray(inputs[k], np.float32))
    return c


def run_sequences(xs, inputs, n_cores=N_CORES, nseq=None, **bk):
    n, S, _ = xs.shape
    if nseq is None:
        nseq = (n + n_cores - 1) // n_cores
    L = np.asarray(inputs["w_in"]).shape[0]
    key = (S, nseq, L, tuple(sorted(bk.items())))
    if key not in _CACHE:
        _CACHE[key] = build_program(S, nseq, L, **bk)
    nc = _CACHE[key]
    consts = _host_consts(S, inputs)
    slots = [[(c + s * n_cores) if (c + s * n_cores) < n else (c % n) for s in range(nseq)] for c in range(n_cores)]
    in_maps = []
    for c in range(n_cores):
        m = dict(consts)
        m["x"] = np.ascontiguousarray(xs[slots[c]])
        in_maps.append(m)
    res = run_bass_kernel_spmd(nc, in_maps, core_ids=list(range(n_cores)))
    out = np.empty_like(xs)
    for c in range(n_cores):
        for s in range(nseq):
            i = c + s * n_cores
            if i < n:
                out[i] = res.results[c]["y"][s]
    return out, res


def kernel(x_prompt, x_sample, norm_attn, w_in, diff_lambda, diff_subln, na_rpb, qk_norm, w_branch,
           w_out, norm_mlp, w_up, conv_w, conv_b, w_down, norm_final):
    inputs = dict(norm_attn=norm_attn, w_in=w_in, diff_lambda=diff_lambda, diff_subln=diff_subln, na_rpb=na_rpb,
                  qk_norm=qk_norm, w_branch=w_branch, w_out=w_out, norm_mlp=norm_mlp, w_up=w_up, conv_w=conv_w,
                  conv_b=conv_b, w_down=w_down, norm_final=norm_final)
    xp = np.asarray(x_prompt, np.float32)
    xs_ = np.asarray(x_sample, np.float32)
    xs = np.concatenate([xs_, xp], axis=0)
    out, _ = run_sequences(xs, inputs)
    nsamp = xs_.shape[0]
    return (np.ascontiguousarray(out[nsamp:]), np.ascontiguousarray(out[:nsamp]))
```

```python
import math
import os
import numpy as np
import concourse.bass as bass
import concourse.mybir as mybir
from concourse.bass_utils import run_bass_kernel_spmd

F32 = mybir.dt.float32
BF16 = mybir.dt.bfloat16
U8 = mybir.dt.uint8
AF = mybir.ActivationFunctionType
ALU = mybir.AluOpType
AX = mybir.AxisListType

D = 1024
KC = 8
DFF = 2816
NFC = 22
INC = 6912
EPS = 1e-6
NEG = -30000.0
N_CORES = 8
ROPE_THETA = 500000.0
AXIAL_THETA = 10000.0
PPL = 196
SELF_SYNC = os.environ.get('NOSELF') is None


class Hd:
    __slots__ = ("name", "w", "r", "dsem", "dcnt")

    def __init__(self, name):
        self.name = name
        self.w = []
        self.r = []
        self.dsem = None
        self.dcnt = 0


class Op:
    __slots__ = ("eng", "fn", "deps", "signal", "ticket", "dma", "sem", "idx")


class Prog:
    CE = ("pe", "act", "dve", "pool")

    def __init__(self, nc):
        self.nc = nc
        self.ops = {e: [] for e in ("pe", "act", "dve", "pool", "sp")}
        self.esem = {e: nc.alloc_semaphore("s_" + e) for e in self.CE}
        self.pending = {e: [] for e in self.ops}
        self.dma_sems = {}
        self.named = {}
        self.nops = 0

    @staticmethod
    def _key(o):
        return o.eng if o.dma is None else ("d", id(o.sem))

    def _push(self, lst, o):
        k = self._key(o)
        for i, p in enumerate(lst):
            if self._key(p) == k:
                lst[i] = o
                return
        lst.append(o)

    def op(self, eng, fn, reads=(), writes=(), dma=None):
        o = Op()
        o.eng = eng
        o.fn = fn
        o.signal = False
        o.ticket = None
        o.dma = dma
        o.sem = None
        o.idx = self.nops
        self.nops += 1
        deps = list(self.pending[eng])
        self.pending[eng] = []
        for h in reads:
            deps.extend(h.w)
            if h.name.startswith("bank"):
                deps.extend(r for r in h.r if r.eng != eng)
        for h in writes:
            if not (dma is not None and h.w and all(p.dma is not None for p in h.w) and not h.r):
                deps.extend(h.w)
            deps.extend(h.r)
        if dma is not None:
            nk = (dma.name, eng)
            if nk not in self.named:
                sem_ = self.nc.alloc_semaphore("d_%s_%s_%d" % (dma.name, eng, len(self.dma_sems)))
                self.named[nk] = [sem_, 0]
                self.dma_sems[id(sem_)] = [sem_, 0]
            ent = self.named[nk]
            ent[1] += 1
            o.sem = ent[0]
            o.ticket = 16 * ent[1]
            self.dma_sems[id(o.sem)][1] = o.ticket
        fd = []
        for p in deps:
            if p is o:
                continue
            if p.dma is not None:
                fd.append(p)
            elif p.eng == eng:
                if eng != "pe" and SELF_SYNC:
                    p.signal = True
                    fd.append(p)
            else:
                p.signal = True
                fd.append(p)
        o.deps = fd
        for h in reads:
            if h not in writes:
                self._push(h.r, o)
        for h in writes:
            if dma is not None and h.w and all(p.dma is not None for p in h.w) and not h.r:
                self._push(h.w, o)
            else:
                h.w = [o]
            h.r = []
        self.ops[eng].append(o)
        return o

    def barrier(self):
        lasts = []
        for e in self.CE:
            for p in reversed(self.ops[e]):
                if p.dma is None:
                    lasts.append(p)
                    break
        dm = []
        for sid, (sem, tot) in self.dma_sems.items():
            if tot > 0:
                f = Op()
                f.eng = "dma"
                f.dma = True
                f.sem = sem
                f.ticket = tot
                f.signal = False
                dm.append(f)
        for e in self.ops:
            for p in lasts:
                if p.eng != e or (e != "pe" and SELF_SYNC):
                    p.signal = True
                    self.pending[e].append(p)
            self.pending[e].extend(dm)

    def emit(self):
        nc = self.nc
        for e in self.CE:
            c = 0
            for o in self.ops[e]:
                if o.signal and o.dma is None:
                    c += 1
                    o.ticket = c
        esem = self.esem

        def mk(ename):
            def body(eng):
                waited = {}
                for o in self.ops[ename]:
                    for p in o.deps:
                        sem = p.sem if p.dma is not None else esem[p.eng]
                        v = p.ticket
                        k = id(sem)
                        if waited.get(k, 0) < v:
                            eng.wait_ge(sem, v)
                            waited[k] = v
                    if os.environ.get("DUMP"):
                        print("OP", ename, o.idx, "dma" if o.dma is not None else "", "sig=%s" % o.ticket if (o.signal or o.dma is not None) else "",
                              "waits:", [((p.sem.name if p.dma is not None else p.eng), p.ticket) for p in o.deps])
                    ins = o.fn(eng)
                    if o.dma is not None:
                        ins.then_inc(o.sem, 16)
                    elif o.signal:
                        ins.then_inc(esem[ename], 1)
            return body

        with nc.Block() as block:
            block.sync(mk("sp"))
            block.tensor(mk("pe"))
            block.scalar(mk("act"))
            block.vector(mk("dve"))
            block.gpsimd(mk("pool"))


class Arena:
    def __init__(self, nc, base, top):
        self.nc = nc
        self.base = base
        self.top = top
        self.cur = base
        self.n = 0

    def reset(self):
        self.cur = self.base

    def alloc(self, shape, dtype, name="t"):
        esz = {F32: 4, BF16: 2, U8: 1}[dtype]
        nb = esz
        for s in shape[1:]:
            nb *= s
        nb = (nb + 31) // 32 * 32
        off = self.cur
        assert off + nb <= self.top, f"SBUF arena overflow {name}: {off + nb} > {self.top}"
        self.cur += nb
        self.n += 1
        return self.nc.alloc_sbuf_tensor_at(f"{name}_{self.n}", list(shape), dtype, offset=off)


def _rope_tables(kind, S):
    pos = np.arange(S)
    cos = np.ones((128, S), np.float32)
    sin = np.zeros((128, S), np.float32)
    perm = np.zeros((128, 128), np.float32)
    for p in range(128):
        if kind == "A":
            j = p % 32
            if j >= 8:
                continue
            half, i, first, theta, pv = 4, j % 4, j < 4, ROPE_THETA, pos
        elif kind == "C":
            j = p % 64
            if j >= 16:
                continue
            half, i, first, theta, pv = 8, j % 8, j < 8, ROPE_THETA, pos
        else:
            j = p % 64
            half, theta = 16, AXIAL_THETA
            if j < 32:
                i, first, pv = j % 16, j < 16, pos // 64
            else:
                i, first, pv = (j - 32) % 16, (j - 32) < 16, pos % 64
        inv = np.exp((np.float32(-math.log(theta)) * np.arange(half, dtype=np.float32)) / np.float32(half)).astype(np.float32)
        ang = (pv.astype(np.float32) * inv[i]).astype(np.float32).astype(np.float64)
        cos[p] = np.cos(ang).astype(np.float32)
        sn = np.sin(ang).astype(np.float32)
        sin[p] = -sn if first else sn
        partner = p + half if first else p - half
        perm[partner, p] = 1.0
    return cos, sin, perm


def _cmask_tiles():
    out = np.empty((20, 128, 512), np.float32)
    kl = np.arange(128)[:, None]
    ql = np.arange(512)[None, :]
    for di in range(20):
        dlt = di - 8
        d = 128 * dlt + kl - ql
        mult = np.zeros(d.shape, np.int64)
        for dil in (1, 4, 16):
            mult += ((d % dil) == 0) & (np.abs(d) <= 64 * dil)
        with np.errstate(divide="ignore"):
            out[di] = np.where(mult > 0, np.log(np.maximum(mult, 1)).astype(np.float32), np.float32(NEG))
    return out


def _na_tiles(rpb, S):
    L = rpb.shape[0]
    rows = S // 64
    T = S // 512
    start = np.clip(np.arange(rows) - 4, 0, rows - 8)
    c = np.arange(64)
    cs = np.clip(c - 8, 0, 48)
    col_in = (c[None, :] >= cs[:, None]) & (c[None, :] < cs[:, None] + 16)
    dc = np.clip(c[None, :] - c[:, None] + 15, 0, 30)
    out = np.full((L, 4, 3, 8, 128, 512), NEG, np.float32)
    tl = [0, 1 if T > 2 else 0, T - 1]
    kk = np.arange(128)
    qq = np.arange(512)
    for vi, t in enumerate(tl):
        for j in range(8):
            kb = 4 * t - 2 + j
            if kb < 0 or kb >= S // 128:
                continue
            krow = 2 * kb + kk // 64
            kcol = kk % 64
            qrow = 8 * t + qq // 64
            qcol = qq % 64
            st = start[qrow]
            vrow = (krow[:, None] >= st[None, :]) & (krow[:, None] < st[None, :] + 8)
            dr = np.clip(krow[:, None] - qrow[None, :] + 7, 0, 14)
            valid = vrow & col_in[qcol[None, :], kcol[:, None]]
            dcc = dc[qcol[None, :], kcol[:, None]]
            vals = rpb[:, :, dr, dcc]
            out[:, :, vi, j] = np.where(valid[None, None], vals, np.float32(NEG))
    return out


def _pack_pp(norm_attn, norm_mlp, conv_w, conv_b, qk_norm, diff_subln, norm_final):
    L = norm_attn.shape[0]
    pp = np.zeros((128, L * PPL + 8), np.float32)
    p64 = np.arange(128) % 64
    for l in range(L):
        o = l * PPL
        pp[:, o:o + 8] = norm_attn[l].reshape(8, 128).T
        pp[:, o + 8:o + 16] = norm_mlp[l].reshape(8, 128).T
        for j in range(3):
            pp[:, o + 16 + j * 44:o + 16 + (j + 1) * 44] = conv_w[l, j].reshape(44, 128).T
        pp[:, o + 148:o + 192] = conv_b[l].reshape(44, 128).T
        pp[:, o + 192] = qk_norm[l, 0][p64]
        pp[:, o + 193] = qk_norm[l, 1][p64]
        pp[:, o + 194] = diff_subln[l][p64]
    pp[:, L * PPL:L * PPL + 8] = norm_final.reshape(8, 128).T
    return pp


def build_program(S, NSEQ, L=2, debug=False, stop_after=None):
    NT = S // 512
    NB = S // 128
    nc = bass.Bass("TRN2", target_bir_lowering=False)
    pg = Prog(nc)

    def din(name, shape, dt=F32):
        return nc.dram_tensor(name, list(shape), dt, kind="ExternalInput").ap()

    def dscr(name, shape, dt):
        return nc.dram_tensor(name, list(shape), dt, kind=("ExternalOutput" if debug else "Internal")).ap()

    x_in = din("x", [NSEQ, S, D])
    w_in = din("w_in", [L, D, INC])
    w_branch = din("w_branch", [L, 4, 256, D])
    w_out = din("w_out", [L, D, D])
    w_up = din("w_up", [L, D, 2 * DFF])
    w_down = din("w_down", [L, DFF, D])
    pp_in = din("pp", [128, L * PPL + 8])
    dl_in = din("dl", [L, 128])
    nab_in = din("nab", [L, 4, 3, 8, 128, 512])
    cm_in = din("cmask", [20, 128, 512])
    rope_in = {k: (din("rope%s_c" % k, [128, S]), din("rope%s_s" % k, [128, S])) for k in "ACD"}
    perm_in = din("perms", [128, 3, 128])
    ident_in = din("ident", [128, 128])
    y_out = nc.dram_tensor("y", [NSEQ, S, D], F32, kind="ExternalOutput").ap()

    xTa = dscr("xTa", [KC, 128, S], F32)
    xTb = dscr("xTb", [KC, 128, S], F32)
    hT = dscr("hT", [KC, 128, S], BF16)
    oT = dscr("oT", [KC, 128, S], BF16)
    H_x = Hd("x_in")
    H_y = Hd("y")
    H_xTa, H_xTb, H_hT, H_oT = Hd("xTa"), Hd("xTb"), Hd("hT"), Hd("oT")
    H_w = Hd("weights")
    xTa_v = xTa.rearrange("k p s -> p k s")
    xTb_v = xTb.rearrange("k p s -> p k s")
    hT_v = hT.rearrange("k p s -> p k s")
    oT_v = oT.rearrange("k p s -> p k s")

    ident = nc.alloc_sbuf_tensor("sb_ident", [128, 128], F32)
    onesf = nc.alloc_sbuf_tensor("sb_onesf", [128, 128], F32)
    blk64 = nc.alloc_sbuf_tensor("sb_blk64", [128, 128], F32)
    perms = nc.alloc_sbuf_tensor("sb_perms", [128, 3, 128], BF16)
    pp = nc.alloc_sbuf_tensor("sb_pp", [128, L * PPL + 8], F32)
    dlr = nc.alloc_sbuf_tensor("sb_dlr", [1, L * 128], F32)
    lamw = nc.alloc_sbuf_tensor("sb_lamw", [1, 80], F32)
    neglam = nc.alloc_sbuf_tensor("sb_neglam", [64, 2 * L], F32)
    gsub = nc.alloc_sbuf_tensor("sb_gsub", [64, L], F32)
    epsb = nc.alloc_sbuf_tensor("sb_epsb", [128, 1], F32)
    dummy = nc.alloc_sbuf_tensor("sb_dummy", [128, 8], F32)
    H_c = Hd("consts")
    ps = nc.alloc_psum_tensor("psum_all", [128, 8, 512], F32)
    BK = [Hd("bank%d" % i) for i in range(8)]

    base = (nc.sbuf_base + 31) // 32 * 32
    top = nc.sbuf_top
    slab = nc.alloc_sbuf_tensor("sb_slab", [128, top - base - 64], U8)
    ar = Arena(nc, base, base + top - base - 64)

    def dma(eng, out, in_, reads, writes, semh):
        return pg.op(eng, lambda e: e.dma_start(out=out, in_=in_), reads, writes, dma=semh)

    def act(out, in_, func, reads, writes, scale=None, bias=None):
        kw = {}
        if scale is not None:
            kw["scale"] = scale
        if bias is not None:
            kw["bias"] = bias
        return pg.op("act", lambda e: e.activation(out=out, in_=in_, func=func, **kw), reads, writes)

    def tt(eng, out, in0, in1, op, reads, writes):
        return pg.op(eng, lambda e: e.tensor_tensor(out=out, in0=in0, in1=in1, op=op), reads, writes)

    def stt(out, in0, scalar, in1, op0, op1, reads, writes):
        return pg.op("dve", lambda e: e.scalar_tensor_tensor(out=out, in0=in0, scalar=scalar, in1=in1, op0=op0, op1=op1),
                     reads, writes)

    def ts(eng, out, in0, s1, s2, op0, op1, reads, writes):
        if op1 is None:
            return pg.op(eng, lambda e: e.tensor_scalar(out=out, in0=in0, scalar1=s1, scalar2=None, op0=op0), reads, writes)
        return pg.op(eng, lambda e: e.tensor_scalar(out=out, in0=in0, scalar1=s1, scalar2=s2, op0=op0, op1=op1), reads, writes)

    def recip(out, in_, reads, writes):
        return pg.op("dve", lambda e: e.reciprocal(out=out, in_=in_), reads, writes)

    def mmgroup(items, reads, writes):
        def fn(e):
            ins = None
            for (o_, l_, r_, st, sp_, tp) in items:
                if tp is None:
                    ins = e.matmul(o_, lhsT=l_, rhs=r_, start=st, stop=sp_)
                else:
                    ins = e.matmul(o_, lhsT=l_, rhs=r_, start=st, stop=sp_, tile_position=tp)
            return ins
        return pg.op("pe", fn, reads, writes)

    dma("sp", ident[:], ident_in, [H_w], [H_c], H_c)
    dma("sp", pp[:], pp_in, [H_w], [H_c], H_c)
    dma("sp", dlr[:], dl_in.rearrange("(o l) n -> o (l n)", o=1), [H_w], [H_c], H_c)
    dma("pool", perms[:], perm_in, [H_w], [H_c], H_c)
    H_c2 = Hd("consts2")
    pg.op("pool", lambda e: e.memset(onesf[:], 1.0), [], [H_c2])
    pg.op("pool", lambda e: e.memset(blk64[:], 0.0), [], [H_c2])
    pg.op("pool", lambda e: e.memset(blk64[0:64, 0:64], 1.0), [], [H_c2])
    pg.op("pool", lambda e: e.memset(blk64[64:128, 64:128], 1.0), [], [H_c2])
    pg.op("pool", lambda e: e.memset(epsb[:], EPS), [], [H_c2])
    CR = [H_c, H_c2]
    H_lam = Hd("lam")
    for l in range(L):
        lam_init = 0.8 - 0.6 * math.exp(-0.3 * l)
        o = l * 128
        tt("dve", lamw[0:1, 0:32], dlr[0:1, o:o + 32], dlr[0:1, o + 32:o + 64], ALU.mult, CR, [H_lam])
        tt("dve", lamw[0:1, 32:64], dlr[0:1, o + 64:o + 96], dlr[0:1, o + 96:o + 128], ALU.mult, CR + [H_lam], [H_lam])
        pg.op("dve", lambda e: e.tensor_reduce(out=lamw[0:1, 64:66], in_=lamw[0:1, 0:64].rearrange("o (a b) -> o a b", a=2),
                                               axis=AX.X, op=ALU.add), [H_lam], [H_lam])
        act(lamw[0:1, 66:68], lamw[0:1, 64:66], AF.Exp, [H_lam], [H_lam])
        for c in range(2):
            stt(lamw[0:1, 68 + c:69 + c], lamw[0:1, 67:68], -lam_init, lamw[0:1, 66:67], ALU.add, ALU.subtract, [H_lam], [H_lam])
        mmgroup([(ps[0:64, 7, 0:2], onesf[0:1, 0:64], lamw[0:1, 68:70], True, True, None)], [H_lam, H_c2], [BK[7]])
        pg.op("dve", lambda e, l=l: e.tensor_copy(out=neglam[:, 2 * l:2 * l + 2], in_=ps[0:64, 7, 0:2]), [BK[7]], [H_lam])
        ts("dve", gsub[:, l:l + 1], pp[0:64, l * PPL + 194:l * PPL + 195], 1.0 - lam_init, None, ALU.mult, None, CR + [H_lam], [H_lam])
    CR = CR + [H_lam]

    def rms_tile(xt, H_xt, ncols, gcol, out_bf, H_out, A, tagbank):
        sq = A["sq"]
        H_sq = A["H_sq"]
        rstd = A["rstd"]
        H_rstd = A["H_rstd"]
        bank = tagbank
        items = []
        for kc in range(KC):
            s = sq[kc % 2]
            hs = H_sq[kc % 2]
            act(s[:, 0:ncols], xt[:, kc, 0:ncols], AF.Square, [H_xt], [hs])
            mmgroup([(ps[:, bank, 0:ncols], onesf[:, :], s[:, 0:ncols], kc == 0, kc == KC - 1, None)], [hs, H_c2], [BK[bank]])
        act(rstd[:, 0:ncols], ps[:, bank, 0:ncols], AF.Sqrt, [BK[bank]] + CR, [H_rstd], scale=1.0 / D, bias=epsb[:, 0:1])
        recip(rstd[:, 0:ncols], rstd[:, 0:ncols], [H_rstd], [H_rstd])
        for kc in range(KC):
            stt(out_bf[:, kc, 0:ncols], xt[:, kc, 0:ncols], pp[:, gcol + kc:gcol + kc + 1], rstd[:, 0:ncols],
                ALU.mult, ALU.mult, [H_xt, H_rstd] + CR, [H_out])

    def phase_T(seq):
        ar.reset()
        xtok = [ar.alloc([128, 4, D], F32, "xtok") for _ in range(2)]
        Hk = [Hd("xtok%d" % i) for i in range(2)]
        xt = [ar.alloc([128, KC, 512], F32, "xt") for _ in range(2)]
        Hx = [Hd("xtT%d" % i) for i in range(2)]
        for t in range(NT):
            b = t % 2
            dma("sp", xtok[b][:], x_in[seq, t * 512:(t + 1) * 512, :].rearrange("(b p) d -> p b d", p=128), [H_x], [Hk[b]], Hk[b])
            for kc in range(KC):
                bank = kc % 4
                def fn(e, kc=kc, bank=bank, b=b):
                    ins = None
                    for blk in range(4):
                        ins = e.transpose(ps[:, bank, blk * 128:(blk + 1) * 128], xtok[b][:, blk, kc * 128:(kc + 1) * 128], ident[:, :])
                    return ins
                pg.op("pe", fn, [Hk[b]] + CR, [BK[bank]])
                if kc % 2 == 0:
                    act(xt[b][:, kc, :], ps[:, bank, :], AF.Copy, [BK[bank]], [Hx[b]])
                else:
                    pg.op("dve", lambda e, kc=kc, bank=bank, b=b: e.tensor_copy(out=xt[b][:, kc, :], in_=ps[:, bank, :]), [BK[bank]], [Hx[b]])
            dma("pool", xTa_v[:, :, t * 512:(t + 1) * 512], xt[b][:], [Hx[b]], [H_xTa], Hx[b])
        pg.barrier()

    def phase_Z(seq, src_v, H_src):
        ar.reset()
        A = {"sq": [ar.alloc([128, 512], F32, "sq") for _ in range(2)], "H_sq": [Hd("sq0"), Hd("sq1")],
             "rstd": ar.alloc([128, 512], F32, "rstd"), "H_rstd": Hd("rstd")}
        xt = [ar.alloc([128, KC, 512], F32, "xt") for _ in range(2)]
        Hx = [Hd("zx%d" % i) for i in range(2)]
        yn = ar.alloc([128, KC, 512], F32, "yn")
        H_yn = Hd("yn")
        ytok = [ar.alloc([128, 4, D], F32, "ytok") for _ in range(2)]
        Hyt = [Hd("ytok%d" % i) for i in range(2)]
        gcol = L * PPL
        for t in range(NT):
            b = t % 2
            dma("sp", xt[b][:], src_v[:, :, t * 512:(t + 1) * 512], [H_src], [Hx[b]], Hx[b])
            sq, H_sq, rstd, H_rstd = A["sq"], A["H_sq"], A["rstd"], A["H_rstd"]
            for kc in range(KC):
                act(sq[kc % 2][:], xt[b][:, kc, :], AF.Square, [Hx[b]], [H_sq[kc % 2]])
                mmgroup([(ps[:, 4, :], onesf[:, :], sq[kc % 2][:], kc == 0, kc == KC - 1, None)], [H_sq[kc % 2], H_c2], [BK[4]])
            act(rstd[:], ps[:, 4, :], AF.Sqrt, [BK[4]] + CR, [H_rstd], scale=1.0 / D, bias=epsb[:, 0:1])
            recip(rstd[:], rstd[:], [H_rstd], [H_rstd])
            for kc in range(KC):
                stt(yn[:, kc, :], xt[b][:, kc, :], pp[:, gcol + kc:gcol + kc + 1], rstd[:], ALU.mult, ALU.mult,
                    [Hx[b], H_rstd] + CR, [H_yn])
            for blk in range(4):
                for half in range(2):
                    bank = (blk * 2 + half) % 4
                    def fn(e, blk=blk, half=half, bank=bank):
                        ins = None
                        for q in range(4):
                            kc = half * 4 + q
                            ins = e.transpose(ps[:, bank, q * 128:(q + 1) * 128], yn[:, kc, blk * 128:(blk + 1) * 128], ident[:, :])
                        return ins
                    pg.op("pe", fn, [H_yn] + CR, [BK[bank]])
                    if half == 0:
                        act(ytok[b][:, blk, 0:512], ps[:, bank, :], AF.Copy, [BK[bank]], [Hyt[b]])
                    else:
                        pg.op("dve", lambda e, blk=blk, bank=bank, b=b: e.tensor_copy(out=ytok[b][:, blk, 512:1024], in_=ps[:, bank, :]),
                              [BK[bank]], [Hyt[b]])
            dma("pool", y_out[seq, t * 512:(t + 1) * 512, :].rearrange("(b p) d -> p b d", p=128), ytok[b][:], [Hyt[b]], [H_y], Hyt[b])
        pg.barrier()

    def phase_0(l, src_v, H_src):
        ar.reset()
        A = {"sq": [ar.alloc([128, 512], F32, "sq") for _ in range(2)], "H_sq": [Hd("sq0"), Hd("sq1")],
             "rstd": ar.alloc([128, 512], F32, "rstd"), "H_rstd": Hd("rstd")}
        xt = [ar.alloc([128, KC, 512], F32, "xt") for _ in range(2)]
        Hx = [Hd("p0x%d" % i) for i in range(2)]
        hb = [ar.alloc([128, KC, 512], BF16, "hb") for _ in range(2)]
        Hh = [Hd("p0h%d" % i) for i in range(2)]
        for t in range(NT):
            b = t % 2
            dma("sp", xt[b][:], src_v[:, :, t * 512:(t + 1) * 512], [H_src], [Hx[b]], Hx[b])
            rms_tile(xt[b], Hx[b], 512, l * PPL + 0, hb[b], Hh[b], A, 4 + (t % 2))
            dma("pool", hT_v[:, :, t * 512:(t + 1) * 512], hb[b][:], [Hh[b]], [H_hT], Hh[b])
        pg.barrier()

    def branch_pass(l, br):
        ar.reset()
        kind = "ABCD"[br]
        HV = 2 if kind == "D" else 4
        RQ = ar.alloc([128, 2, S], BF16, "RQ")
        RK = ar.alloc([128, 2, S], BF16, "RK")
        RV = ar.alloc([128, NB, HV, 128], BF16, "RV")
        mark = ar.cur
        H_RQ = [Hd("RQ%d" % t) for t in range(NT)]
        H_RK = [Hd("RK%d" % t) for t in range(NT)]
        H_RV = [Hd("RV%d" % t) for t in range(NT)]
        H_RVo = Hd("RVones")
        if kind == "D":
            c0 = 9 * 256
            ncol = 256 + 128 + 128
        else:
            c0 = br * 768
            ncol = 768
        W = ar.alloc([128, KC, ncol], BF16, "W")
        H_W = Hd("W")
        for kc in range(KC):
            dma("pool", W[:, kc, :], w_in[l, kc * 128:(kc + 1) * 128, c0:c0 + ncol], [H_w], [H_W], H_W)
        FL = set(os.environ.get("FLAGS", "").split(","))
        if "nomem" not in FL:
            pg.op("pool", lambda e: e.memset(RV[:, :, :, 64:128], 1.0), [], [H_RVo])
        ht = [ar.alloc([128, KC, 512], BF16, "ht") for _ in range(2)]
        Hht = [Hd("ht%d" % i) for i in range(2)]
        rot = kind in "ACD"
        if rot:
            cs_t = [ar.alloc([128, 2, 512], F32, "cs") for _ in range(2)]
            Hcs = [Hd("cs%d" % i) for i in range(2)]
            a16 = [ar.alloc([128, 512], BF16, "a16") for _ in range(2)]
            Ha16 = [Hd("a16_%d" % i) for i in range(2)]
            t1 = [ar.alloc([128, 512], F32, "t1") for _ in range(2)]
            Ht1 = [Hd("t1_%d" % i) for i in range(2)]
            t2 = [ar.alloc([128, 512], F32, "t2") for _ in range(2)]
            Ht2 = [Hd("t2_%d" % i) for i in range(2)]
            pidx = "ACD".index(kind)
        if kind == "D":
            sqd = [ar.alloc([128, 512], F32, "sqd") for _ in range(2)]
            Hsqd = [Hd("sqd%d" % i) for i in range(2)]
            rsd = [ar.alloc([128, 512], F32, "rsd") for _ in range(2)]
            Hrsd = [Hd("rsd%d" % i) for i in range(2)]

        if kind == "D":
            fm = [("q", RQ, 0, [(0, 128)]), ("q", RQ, 1, [(128, 128)]),
                  ("k", RK, 0, [(256, 128)]), ("k", RK, 1, [(320, 64), (256, 64)])]
            vcol = 384
            vw = 128
        else:
            fm = [("q", RQ, 0, [(0, 128)]), ("q", RQ, 1, [(128, 128)]),
                  ("k", RK, 0, [(256, 128)]), ("k", RK, 1, [(384, 128)])]
            vcol = 512
            vw = 256
        cnt = 0
        for t in range(NT):
            b = t % 2
            tsl = slice(t * 512, (t + 1) * 512)
            dma("sp", ht[b][:], hT_v[:, :, tsl], [H_hT], [Hht[b]], Hht[b])
            if rot:
                dma("sp", cs_t[b][:, 0, :], rope_in[kind][0][:, tsl], [H_w], [Hcs[b]], Hcs[b])
                dma("sp", cs_t[b][:, 1, :], rope_in[kind][1][:, tsl], [H_w], [Hcs[b]], Hcs[b])
            for (qk, dst, dch, pieces) in fm:
                bank = cnt % 2
                pb = 2 + cnt % 2
                sb_ = 4 + cnt % 2
                u = cnt % 2
                cnt += 1
                Hdst = (H_RQ if qk == "q" else H_RK)[t]
                items = []
                for kc in range(KC):
                    mo = 0
                    for (pc0, pw) in pieces:
                        items.append((ps[mo:mo + pw, bank, :], W[:, kc, pc0:pc0 + pw], ht[b][:, kc, :], kc == 0, kc == KC - 1, None))
                        mo += pw
                if len(pieces) == 1:
                    mmgroup(items, [H_W, Hht[b]], [BK[bank]])
                else:
                    i0 = [it for i, it in enumerate(items) if i % 2 == 0]
                    i1 = [it for i, it in enumerate(items) if i % 2 == 1]
                    mmgroup(i0 + i1, [H_W, Hht[b]], [BK[bank]])
                dsl = dst[:, dch, tsl]
                if not rot or "norot" in FL:
                    if cnt % 2 == 0:
                        act(dsl, ps[:, bank, :], AF.Copy, [BK[bank]], [Hdst])
                    else:
                        pg.op("dve", lambda e, dsl=dsl, bank=bank: e.tensor_copy(out=dsl, in_=ps[:, bank, :]), [BK[bank]], [Hdst])
                    continue
                if kind == "D":
                    gcol = l * PPL + (192 if qk == "q" else 193)
                    gap = pp[:, gcol:gcol + 1]
                    act(a16[u][:], ps[:, bank, :], AF.Identity, [BK[bank]] + CR, [Ha16[u]], scale=gap)
                    act(sqd[u][:], ps[:, bank, :], AF.Square, [BK[bank]], [Hsqd[u]])
                    mmgroup([(ps[:, sb_, :], blk64[:, :], sqd[u][:], True, True, None)], [Hsqd[u], H_c2], [BK[sb_]])
                    act(rsd[u][:], ps[:, sb_, :], AF.Sqrt, [BK[sb_]] + CR, [Hrsd[u]], scale=1.0 / 64, bias=epsb[:, 0:1])
                    recip(rsd[u][:], rsd[u][:], [Hrsd[u]], [Hrsd[u]])
                    stt(t1[u][:], ps[:, bank, :], gap, cs_t[b][:, 0, :], ALU.mult, ALU.mult, [BK[bank], Hcs[b]] + CR, [Ht1[u]])
                else:
                    act(a16[u][:], ps[:, bank, :], AF.Copy, [BK[bank]], [Ha16[u]])
                    if "rotA" in FL:
                        pg.op("dve", lambda e, dsl=dsl, u=u: e.tensor_copy(out=dsl, in_=a16[u][:]), [Ha16[u]], [Hdst])
                        continue
                    if "rotA2" in FL:
                        pg.op("dve", lambda e, u=u, bank=bank: e.tensor_copy(out=t1[u][:], in_=ps[:, bank, :]), [BK[bank]] + ([Ha16[u]] if "ser" in FL else []), [Ht1[u]])
                    else:
                        tt("dve", t1[u][:], ps[:, bank, :], cs_t[b][:, 0, :], ALU.mult, [BK[bank], Hcs[b]], [Ht1[u]])
                if "rotB" in FL:
                    if "rotBact" in FL:
                        act(dsl, t1[u][:], AF.Copy, [Ht1[u]], [Hdst])
                    else:
                        pg.op("dve", lambda e, dsl=dsl, u=u: e.tensor_copy(out=dsl, in_=t1[u][:]), [Ht1[u]], [Hdst])
                    continue
                mmgroup([(ps[:, pb, :], perms[:, pidx, :], a16[u][:], True, True, None)], [Ha16[u]] + CR, [BK[pb]])
                if "rotC" in FL:
                    pg.op("dve", lambda e, dsl=dsl, pb=pb: e.tensor_copy(out=dsl, in_=ps[:, pb, :]), [BK[pb]], [Hdst])
                    continue
                tt("dve", t2[u][:], ps[:, pb, :], cs_t[b][:, 1, :], ALU.mult, [BK[pb], Hcs[b]], [Ht2[u]])
                if kind == "D":
                    tt("pool", t1[u][:], t1[u][:], t2[u][:], ALU.add, [Ht1[u], Ht2[u]], [Ht1[u]])
                    tt("dve", dsl, t1[u][:], rsd[u][:], ALU.mult, [Ht1[u], Hrsd[u]], [Hdst])
                else:
                    tt("pool" if "pooladd" in FL else "dve", dsl, t1[u][:], t2[u][:], ALU.add, [Ht1[u], Ht2[u]], [Hdst])
            for blk in range(4 if "nov" not in FL else 0):
                bank = 6 + blk % 2
                items = [(ps[:, bank, 0:vw], ht[b][:, kc, blk * 128:(blk + 1) * 128], W[:, kc, vcol:vcol + vw], kc == 0, kc == KC - 1, None)
                         for kc in range(KC)]
                mmgroup(items, [H_W, Hht[b]], [BK[bank]])
                gb = t * 4 + blk
                src = ps[:, bank, 0:vw].rearrange("p (h d) -> p h d", h=HV)
                if blk % 2 == 0:
                    act(RV[:, gb, :, 0:64], src, AF.Copy, [BK[bank]], [H_RV[t]])
                else:
                    pg.op("dve", lambda e, gb=gb, src=src: e.tensor_copy(out=RV[:, gb, :, 0:64], in_=src), [BK[bank]], [H_RV[t]])

        if os.environ.get("SUBSTOP") == "proj":
            pg.barrier()
            return
        pg.barrier()
        ar.cur = mark
        PT = [ar.alloc([128, 1024], BF16, "PT") for _ in range(3)]
        HPT = [Hd("PT%d" % i) for i in range(3)]
        fsb = [ar.alloc([65, 512], F32, "fsb") for _ in range(4)]
        Hfsb = [Hd("fsb%d" % i) for i in range(4)]
        rr = fsb
        Hrr = Hfsb
        ost = [ar.alloc([64, 512], BF16, "ost") for _ in range(4)]
        Host = [Hd("ost%d" % i) for i in range(4)]
        if kind == "A":
            o12 = [ar.alloc([64, 512], F32, "o12") for _ in range(4)]
            Ho12 = [Hd("o12_%d" % i) for i in range(4)]
            dd = [ar.alloc([64, 512], F32, "dd") for _ in range(2)]
            Hdd = [Hd("dd%d" % i) for i in range(2)]
            sqa = [ar.alloc([64, 512], F32, "sqa") for _ in range(2)]
            Hsqa = [Hd("sqa%d" % i) for i in range(2)]
        if kind in "BC":
            sbx = [ar.alloc([128, 1024], F32, "sbx") for _ in range(2)]
            Hsbx = [Hd("sbx%d" % i) for i in range(2)]
        if kind == "B":
            bsl = [ar.alloc([128, 2, 512], F32, "bsl") for _ in range(3)]
            Hbsl = [Hd("bsl%d" % i) for i in range(3)]
            bres = ar.alloc([128, 8, 2, 512], F32, "bres")
            H_bres = Hd("bres")
        if kind == "C":
            cm = ar.alloc([128, 20, 512], F32, "cm")
            H_cm = Hd("cm")
            for g in range(4):
                dma("sp", cm[:, g * 5:(g + 1) * 5, :], cm_in[g * 5:(g + 1) * 5].rearrange("j p n -> p j n"), [H_w], [H_cm], H_cm)
        allK = H_RK
        allV = H_RV + [H_RVo]
        scale = {"A": 32 ** -0.5, "B": 0.125, "C": 0.125, "D": 0.125}[kind]
        state = {"u": 0, "pt": 0, "sp": 0, "f": 0, "bs": 0, "bl": 0}

        def finalize(accset, unit_infos, t):
            tsl = slice(t * 512, (t + 1) * 512)
            if kind != "A":
                for i in range(2):
                    bank = accset[i]
                    f = state["f"] % 4
                    state["f"] += 1
                    pg.op("dve", lambda e, f=f, bank=bank: e.tensor_copy(out=fsb[f][0:65, :], in_=ps[0:65, bank, :]), [BK[bank]], [Hfsb[f]])
                    recip(rr[f][64:65, :], fsb[f][64:65, :], [Hfsb[f]], [Hrr[f]])
                    mmgroup([(ps[0:64, bank, :], onesf[64:65, 0:64], rr[f][64:65, :], True, True, None)], [Hrr[f], H_c2], [BK[bank]])
                    tt("dve", ost[f][:], fsb[f][0:64, :], ps[0:64, bank, :], ALU.mult, [Hfsb[f], BK[bank]], [Host[f]])
                    ch, pb_ = unit_infos[i]
                    dma("pool", oT_v[pb_:pb_ + 64, ch, tsl], ost[f][:], [Host[f]], [H_oT], Host[f])
                return
            fs = []
            for i in range(2):
                bank = accset[i]
                f = state["f"] % 4
                state["f"] += 1
                fs.append(f)
                pg.op("dve", lambda e, f=f, bank=bank: e.tensor_copy(out=fsb[f][0:65, :], in_=ps[0:65, bank, :]), [BK[bank]], [Hfsb[f]])
                recip(rr[f][64:65, :], fsb[f][64:65, :], [Hfsb[f]], [Hrr[f]])
                mmgroup([(ps[0:64, bank, :], onesf[64:65, 0:64], rr[f][64:65, :], True, True, None)], [Hrr[f], H_c2], [BK[bank]])
                tt("dve", o12[f][:], fsb[f][0:64, :], ps[0:64, bank, :], ALU.mult, [Hfsb[f], BK[bank]], [Ho12[f]])
            u = (state["f"] // 2) % 2
            stt(dd[u][:], o12[fs[1]][:], neglam[:, 2 * l:2 * l + 1], o12[fs[0]][:], ALU.mult, ALU.add,
                [Ho12[fs[0]], Ho12[fs[1]]] + CR, [Hdd[u]])
            tt("pool", sqa[u][:], dd[u][:], dd[u][:], ALU.mult, [Hdd[u]], [Hsqa[u]])
            bank = accset[0]
            mmgroup([(ps[0:64, bank, :], onesf[0:64, 0:64], sqa[u][:], True, True, None)], [Hsqa[u], H_c2], [BK[bank]])
            act(sqa[u][:], ps[0:64, bank, :], AF.Sqrt, [BK[bank]] + CR, [Hsqa[u]], scale=1.0 / 64, bias=epsb[0:64, 0:1])
            recip(sqa[u][:], sqa[u][:], [Hsqa[u]], [Hsqa[u]])
            f = fs[0]
            stt(ost[f][:], dd[u][:], gsub[:, l:l + 1], sqa[u][:], ALU.mult, ALU.mult, [Hdd[u], Hsqa[u]] + CR, [Host[f]])
            ch, pb_ = unit_infos[0]
            dma("pool", oT_v[pb_:pb_ + 64, ch, tsl], ost[f][:], [Host[f]], [H_oT], Host[f])

        def attn_unit(t, qk_items, v_aps, kbs, bias_fn, unit_infos, bias_pre=None):
            accset = (4, 5) if state["u"] % 2 == 0 else (6, 7)
            state["u"] += 1
            n = len(kbs)

            def emit_qk(j):
                kb = kbs[j]
                if bias_pre is not None:
                    bias_pre(kb)
                sp_ = state["sp"] % 2
                state["sp"] += 1
                banks = (2 * sp_, 2 * sp_ + 1)
                items = []
                for i in range(2):
                    lhsT, rhs, tp = qk_items[i](kb)
                    items.append((ps[:, banks[i], :], lhsT, rhs, True, True, tp))
                mmgroup(items, [H_RQ[t], H_RK[kb // 4]], [BK[banks[0]], BK[banks[1]]])
                return banks

            pend = emit_qk(0)
            for j in range(n):
                kb = kbs[j]
                banks = pend
                p = state["pt"] % 3
                state["pt"] += 1
                if bias_fn is None:
                    act(PT[p][:], ps[:, banks[0]:banks[0] + 2, :], AF.Exp, [BK[banks[0]], BK[banks[1]]], [HPT[p]], scale=scale)
                else:
                    sx = state["bs"] % 2
                    for i in range(2):
                        bap, bh = bias_fn(i, kb)
                        stt(sbx[sx][:, i * 512:(i + 1) * 512], ps[:, banks[i], :], scale, bap, ALU.mult, ALU.add,
                            [BK[banks[i]], bh], [Hsbx[sx]])
                    state["bs"] += 1
                    act(PT[p][:], sbx[sx][:], AF.Exp, [Hsbx[sx]], [HPT[p]])
                if j + 1 < n:
                    pend = emit_qk(j + 1)
                items = []
                for i in range(2):
                    items.append((ps[:, accset[i], :], v_aps[i](kb), PT[p][:, i * 512:(i + 1) * 512], j == 0, j == n - 1, None))
                mmgroup(items, [HPT[p], H_RV[kb // 4], H_RVo], [BK[accset[0]], BK[accset[1]]])
            if os.environ.get("SUBSTOP") != "nofin":
                finalize(accset, unit_infos, t)

        if kind == "B":
            order = [(t, hp) for hp in range(2) for t in range(NT)]
        else:
            order = [(t, None) for t in range(NT)]
        for (t, hp_sel) in order:
            if os.environ.get("SUBSTOP") in ("unit1", "nofin") and t > 0:
                break
            tsl = slice(t * 512, (t + 1) * 512)
            if kind == "B" and t == 0 and NT > 2:
                for j in range(8):
                    dma("sp", bres[:, j, :, :], nab_in[l, 2 * hp_sel:2 * hp_sel + 2, 1, j].rearrange("h p n -> p h n"), [H_w], [H_bres], H_bres)
            if kind == "A":
                for h in range(4):
                    ch, pb_ = h // 2, (h % 2) * 64
                    def mk(i, ch=ch, pb_=pb_):
                        p0 = pb_ + 32 * i
                        tp = (96, 0) if p0 == 96 else None
                        return lambda kb: (RK[p0:p0 + 32, ch, kb * 128:(kb + 1) * 128], RQ[p0:p0 + 32, ch, tsl], tp)
                    va = lambda kb, h=h: RV[:, kb, h, :]
                    attn_unit(t, [mk(0), mk(1)], [va, va], list(range(NB)), None, [(0 * 2 + ch, pb_), None])
            elif kind == "D":
                for g in range(2):
                    def mk(i, g=g):
                        pb_ = 64 * i
                        kch = (0 if g == 0 else 1) if i == 0 else (1 if g == 0 else 0)
                        return lambda kb: (RK[pb_:pb_ + 64, kch, kb * 128:(kb + 1) * 128], RQ[pb_:pb_ + 64, g, tsl], None)
                    va = lambda kb, g=g: RV[:, kb, g, :]
                    attn_unit(t, [mk(0), mk(1)], [va, va], list(range(NB)), None, [(6 + g, 0), (6 + g, 64)])
            else:
                for hp in ([hp_sel] if hp_sel is not None else range(2)):
                    def mk(i, hp=hp):
                        pb_ = 64 * i
                        return lambda kb: (RK[pb_:pb_ + 64, hp, kb * 128:(kb + 1) * 128], RQ[pb_:pb_ + 64, hp, tsl], None)
                    vas = [(lambda kb, h=2 * hp + i: RV[:, kb, h, :]) for i in range(2)]
                    if kind == "B":
                        kbs = [kb for kb in range(4 * t - 2, 4 * t + 6) if 0 <= kb < NB]
                        var = 0 if t == 0 else (2 if t == NT - 1 else 1)
                        cache = {}
                        def bias_pre(kb, hp=hp, t=t, var=var, cache=cache):
                            s_ = state["bl"] % 3
                            state["bl"] += 1
                            j = kb - (4 * t - 2)
                            dma("sp", bsl[s_][:], nab_in[l, 2 * hp:2 * hp + 2, var, j].rearrange("h p n -> p h n"), [H_w], [Hbsl[s_]], Hbsl[s_])
                            cache[kb] = s_
                        def bias_fn(i, kb, cache=cache):
                            s_ = cache[kb]
                            return bsl[s_][:, i, :], Hbsl[s_]
                        if var == 1:
                            bias_pre = None
                            def bias_fn(i, kb, t=t):
                                return bres[:, kb - (4 * t - 2), i, :], H_bres
                    else:
                        kbs = [kb for kb in range(4 * t - 8, 4 * t + 12) if 0 <= kb < NB]
                        bias_pre = None
                        def bias_fn(i, kb, t=t):
                            return cm[:, kb - 4 * t + 8, :], H_cm
                    br_ch = 2 * br + hp
                    attn_unit(t, [mk(0), mk(1)], vas, kbs, bias_fn, [(br_ch, 0), (br_ch, 64)], bias_pre)
        pg.barrier()

    def merge_pass(l, src_v, H_src, dst_v, H_dst):
        ar.reset()
        Wg = ar.alloc([128, KC, 4096], BF16, "Wg")
        Wbr = ar.alloc([128, 8, D], BF16, "Wbr")
        Wo = ar.alloc([128, KC, D], BF16, "Wo")
        H_Wm = Hd("Wm")
        for kc in range(KC):
            dma("pool", Wg[:, kc, :], w_in[l, kc * 128:(kc + 1) * 128, 2816:6912], [H_w], [H_Wm], H_Wm)
        for bq in range(4):
            dma("pool", Wbr[:, 2 * bq:2 * bq + 2, :], w_branch[l, bq].rearrange("(k p) n -> p k n", p=128), [H_w], [H_Wm], H_Wm)
        dma("pool", Wo[:], w_out[l].rearrange("(k p) n -> p k n", p=128), [H_w], [H_Wm], H_Wm)
        xt = [ar.alloc([128, KC, 512], F32, "xt") for _ in range(2)]
        Hx = [Hd("mx%d" % i) for i in range(2)]
        ht = [ar.alloc([128, KC, 512], BF16, "ht") for _ in range(2)]
        Hht = [Hd("mh%d" % i) for i in range(2)]
        ot = [ar.alloc([128, KC, 512], BF16, "ot") for _ in range(2)]
        Hot = [Hd("mo%d" % i) for i in range(2)]
        mg = ar.alloc([128, KC, 512], BF16, "mg")
        H_mg = Hd("mg")
        sg = [ar.alloc([128, 512], F32, "sg") for _ in range(2)]
        Hsg = [Hd("sg%d" % i) for i in range(2)]
        acc = [ar.alloc([128, 512], F32, "macc") for _ in range(2)]
        Hacc = [Hd("macc%d" % i) for i in range(2)]
        tmp = [ar.alloc([128, 512], F32, "mtmp") for _ in range(2)]
        Htmp = [Hd("mtmp%d" % i) for i in range(2)]
        cnt = 0
        for t in range(NT):
            b = t % 2
            tsl = slice(t * 512, (t + 1) * 512)
            dma("sp", ht[b][:], hT_v[:, :, tsl], [H_hT], [Hht[b]], Hht[b])
            dma("sp", ot[b][:], oT_v[:, :, tsl], [H_oT], [Hot[b]], Hot[b])
            dma("sp", xt[b][:], src_v[:, :, tsl], [H_src], [Hx[b]], Hx[b])
            for oc in range(KC):
                a = oc % 2
                for bq in range(4):
                    gb = cnt % 2
                    mb = 2 + cnt % 2
                    u = cnt % 2
                    cnt += 1
                    col = bq * D + oc * 128
                    mmgroup([(ps[:, gb, :], Wg[:, kc, col:col + 128], ht[b][:, kc, :], kc == 0, kc == KC - 1, None) for kc in range(KC)],
                            [H_Wm, Hht[b]], [BK[gb]])
                    mmgroup([(ps[:, mb, :], Wbr[:, 2 * bq + j, oc * 128:(oc + 1) * 128], ot[b][:, 2 * bq + j, :], j == 0, j == 1, None)
                             for j in range(2)], [H_Wm, Hot[b]], [BK[mb]])
                    act(sg[u][:], ps[:, gb, :], AF.Sigmoid, [BK[gb]], [Hsg[u]])
                    if bq == 0:
                        tt("dve", acc[a][:], sg[u][:], ps[:, mb, :], ALU.mult, [Hsg[u], BK[mb]], [Hacc[a]])
                    elif bq < 3:
                        tt("dve", tmp[u][:], sg[u][:], ps[:, mb, :], ALU.mult, [Hsg[u], BK[mb]], [Htmp[u]])
                        tt("pool", acc[a][:], acc[a][:], tmp[u][:], ALU.add, [Hacc[a], Htmp[u]], [Hacc[a]])
                    else:
                        tt("dve", tmp[u][:], sg[u][:], ps[:, mb, :], ALU.mult, [Hsg[u], BK[mb]], [Htmp[u]])
                        tt("pool", mg[:, oc, :], acc[a][:], tmp[u][:], ALU.add, [Hacc[a], Htmp[u]], [H_mg])
            for oc in range(KC):
                bank = 4 + oc % 4
                mmgroup([(ps[:, bank, :], Wo[:, kc, oc * 128:(oc + 1) * 128], mg[:, kc, :], kc == 0, kc == KC - 1, None) for kc in range(KC)],
                        [H_Wm, H_mg], [BK[bank]])
                tt("dve", xt[b][:, oc, :], xt[b][:, oc, :], ps[:, bank, :], ALU.add, [Hx[b], BK[bank]], [Hx[b]])
            dma("pool", dst_v[:, :, tsl], xt[b][:], [Hx[b]], [H_dst], Hx[b])
        pg.barrier()

    def mlp_pass(l, src_v, H_src, dst_v, H_dst):
        ar.reset()
        Wup = ar.alloc([128, KC, 2 * DFF], BF16, "Wup")
        Wdn = ar.alloc([128, NFC, D], BF16, "Wdn")
        H_Wf = Hd("Wf")
        for kc in range(KC):
            dma("pool", Wup[:, kc, :], w_up[l, kc * 128:(kc + 1) * 128, :], [H_w], [H_Wf], H_Wf)
        for j in range(0, NFC, 2):
            dma("pool", Wdn[:, j:j + 2, :], w_down[l, j * 128:(j + 2) * 128, :].rearrange("(k p) n -> p k n", p=128), [H_w], [H_Wf], H_Wf)
        A = {"sq": [ar.alloc([128, 512], F32, "sq") for _ in range(2)], "H_sq": [Hd("sq0"), Hd("sq1")],
             "rstd": ar.alloc([128, 512], F32, "rstd"), "H_rstd": Hd("rstd")}
        xt = [ar.alloc([128, KC, 512], F32, "xt") for _ in range(1)]
        Hx = [Hd("fx%d" % i) for i in range(1)]
        h2 = ar.alloc([128, KC, 512], BF16, "h2")
        H_h2 = Hd("h2")
        gT = ar.alloc([128, NFC, 512], BF16, "gT")
        H_gT = Hd("gT")
        cv = [ar.alloc([128, 512], F32, "cv") for _ in range(2)]
        Hcv = [Hd("cv%d" % i) for i in range(2)]
        cg = [ar.alloc([128, 512], F32, "cg") for _ in range(2)]
        Hcg = [Hd("cg%d" % i) for i in range(2)]
        ntile = (S + 509) // 510
        po = l * PPL
        cnt = 0
        for i in range(ntile):
            b = 0
            c0 = 510 * i
            nv = min(510, S - c0)
            lo = max(c0 - 1, 0)
            hi = min(c0 + nv + 1, S)
            off = lo - (c0 - 1)
            nl = hi - lo
            dma("sp", xt[b][:, :, off:off + nl], src_v[:, :, lo:hi], [H_src], [Hx[b]], Hx[b])
            ncols = off + nl
            if off > 0:
                pg.op("pool", lambda e: e.memset(xt[0][:, :, 0:1], 0.0), [Hx[b]], [Hx[b]])
            rms_tile(xt[b], Hx[b], ncols, po + 8, h2, H_h2, A, 6)
            if off > 0:
                pg.op("pool", lambda e: e.memset(h2[:, :, 0:1], 0.0), [H_h2], [H_h2])
            if ncols < 512:
                pg.op("pool", lambda e, ncols=ncols: e.memset(h2[:, :, ncols:512], 0.0), [H_h2], [H_h2])
            if off > 0:
                pass
            for j in range(NFC):
                u = cnt % 2
                cnt += 1
                bv = 0 + u * 2
                bg = 1 + u * 2
                mmgroup([(ps[:, bv, :], Wup[:, kc, j * 128:(j + 1) * 128], h2[:, kc, :], kc == 0, kc == KC - 1, None) for kc in range(KC)],
                        [H_Wf, H_h2], [BK[bv]])
                mmgroup([(ps[:, bg, :], Wup[:, kc, DFF + j * 128:DFF + (j + 1) * 128], h2[:, kc, :], kc == 0, kc == KC - 1, None) for kc in range(KC)],
                        [H_Wf, H_h2], [BK[bg]])
                for (bank, cbuf, Hc, chn) in ((bv, cv[u], Hcv[u], j), (bg, cg[u], Hcg[u], NFC + j)):
                    w0 = pp[:, po + 16 + 0 * 44 + chn:po + 16 + 0 * 44 + chn + 1]
                    w1 = pp[:, po + 16 + 1 * 44 + chn:po + 16 + 1 * 44 + chn + 1]
                    w2 = pp[:, po + 16 + 2 * 44 + chn:po + 16 + 2 * 44 + chn + 1]
                    bb = pp[:, po + 148 + chn:po + 148 + chn + 1]
                    act(cbuf[:, 0:510], ps[:, bank, 1:511], AF.Identity, [BK[bank]] + CR, [Hc], scale=w1, bias=bb)
                    stt(cbuf[:, 0:510], ps[:, bank, 0:510], w0, cbuf[:, 0:510], ALU.mult, ALU.add, [BK[bank], Hc] + CR, [Hc])
                    stt(cbuf[:, 0:510], ps[:, bank, 2:512], w2, cbuf[:, 0:510], ALU.mult, ALU.add, [BK[bank], Hc] + CR, [Hc])
                act(cg[u][:, 0:510], cg[u][:, 0:510], AF.Gelu, [Hcg[u]], [Hcg[u]])
                tt("pool", gT[:, j, 0:510], cg[u][:, 0:510], cv[u][:, 0:510], ALU.mult, [Hcg[u], Hcv[u]], [H_gT])
            for oc in range(KC):
                bank = 4 + oc % 2
                mmgroup([(ps[:, bank, 0:510], Wdn[:, j, oc * 128:(oc + 1) * 128], gT[:, j, 0:510], j == 0, j == NFC - 1, None) for j in range(NFC)],
                        [H_Wf, H_gT], [BK[bank]])
                tt("dve", xt[b][:, oc, 1:1 + nv], xt[b][:, oc, 1:1 + nv], ps[:, bank, 0:nv], ALU.add, [Hx[b], BK[bank]], [Hx[b]])
            dma("pool", dst_v[:, :, c0:c0 + nv], xt[b][:, :, 1:1 + nv], [Hx[b]], [H_dst], Hx[b])
        pg.barrier()

    stages = []
    for seq in range(NSEQ):
        stages.append(("T", lambda seq=seq: phase_T(seq)))
        for l in range(L):
            stages.append(("P0", lambda l=l: phase_0(l, xTa_v, H_xTa)))
            for br in range(4):
                stages.append(("BR%d" % br, lambda l=l, br=br: branch_pass(l, br)))
            stages.append(("MG", lambda l=l: merge_pass(l, xTa_v, H_xTa, xTb_v, H_xTb)))
            stages.append(("FF", lambda l=l: mlp_pass(l, xTb_v, H_xTb, xTa_v, H_xTa)))
        stages.append(("Z", lambda seq=seq: phase_Z(seq, xTa_v, H_xTa)))
    for i, (nm, fn) in enumerate(stages):
        if stop_after is not None and i >= stop_after:
            break
        fn()
    pg.barrier()
    pg.op("pool", lambda e: e.memset(dummy[:], 0.0), [], [Hd("dummy")])
    pg.emit()
    return nc


_CACHE = {}


def _host_consts(S, inputs):
    c = {}
    perms = np.zeros((128, 3, 128), np.float32)
    for i, k in enumerate("ACD"):
        cs, sn, pm = _rope_tables(k, S)
        c["rope%s_c" % k] = cs
        c["rope%s_s" % k] = sn
        perms[:, i, :] = pm
    c["perms"] = perms
    c["ident"] = np.eye(128, dtype=np.float32)
    c["cmask"] = _cmask_tiles()
    c["nab"] = _na_tiles(np.asarray(inputs["na_rpb"], np.float32), S)
    c["pp"] = _pack_pp(*[np.asarray(inputs[k], np.float32) for k in
                         ("norm_attn", "norm_mlp", "conv_w", "conv_b", "qk_norm", "diff_subln", "norm_final")])
    c["dl"] = np.ascontiguousarray(np.asarray(inputs["diff_lambda"], np.float32).reshape(-1, 128))
    for k in ("w_in", "w_branch", "w_out", "w_up", "w_down"):
        c[k] = np.ascontiguousarray(np.asarray(inputs[k], np.float32))
    return c


def run_sequences(xs, inputs, n_cores=N_CORES, nseq=None, **bk):
    n, S, _ = xs.shape
    if nseq is None:
        nseq = (n + n_cores - 1) // n_cores
    L = np.asarray(inputs["w_in"]).shape[0]
    key = (S, nseq, L, tuple(sorted(bk.items())))
    if key not in _CACHE:
        _CACHE[key] = build_program(S, nseq, L, **bk)
    nc = _CACHE[key]
    consts = _host_consts(S, inputs)
    slots = [[(c + s * n_cores) if (c + s * n_cores) < n else (c % n) for s in range(nseq)] for c in range(n_cores)]
    in_maps = []
    for c in range(n_cores):
        m = dict(consts)
        m["x"] = np.ascontiguousarray(xs[slots[c]])
        in_maps.append(m)
    res = run_bass_kernel_spmd(nc, in_maps, core_ids=list(range(n_cores)))
    out = np.empty_like(xs)
    for c in range(n_cores):
        for s in range(nseq):
            i = c + s * n_cores
            if i < n:
                out[i] = res.results[c]["y"][s]
    return out, res


def kernel(x_prompt, x_sample, norm_attn, w_in, diff_lambda, diff_subln, na_rpb, qk_norm, w_branch,
           w_out, norm_mlp, w_up, conv_w, conv_b, w_down, norm_final):
    inputs = dict(norm_attn=norm_attn, w_in=w_in, diff_lambda=diff_lambda, diff_subln=diff_subln, na_rpb=na_rpb,
                  qk_norm=qk_norm, w_branch=w_branch, w_out=w_out, norm_mlp=norm_mlp, w_up=w_up, conv_w=conv_w,
                  conv_b=conv_b, w_down=w_down, norm_final=norm_final)
    xp = np.asarray(x_prompt, np.float32)
    xs_ = np.asarray(x_sample, np.float32)
    xs = np.concatenate([xs_, xp], axis=0)
    out, _ = run_sequences(xs, inputs)
    nsamp = xs_.shape[0]
    return (np.ascontiguousarray(out[nsamp:]), np.ascontiguousarray(out[:nsamp]))
```

```python
import math
import os
import numpy as np
import concourse.bass as bass
import concourse.mybir as mybir
from concourse.bass_utils import run_bass_kernel_spmd

F32 = mybir.dt.float32
BF16 = mybir.dt.bfloat16
U8 = mybir.dt.uint8
AF = mybir.ActivationFunctionType
ALU = mybir.AluOpType
AX = mybir.AxisListType

D = 1024
KC = 8
DFF = 2816
NFC = 22
INC = 6912
EPS = 1e-6
NEG = -30000.0
N_CORES = 8
ROPE_THETA = 500000.0
AXIAL_THETA = 10000.0
PPL = 196
SELF_SYNC = os.environ.get('NOSELF') is None


class Hd:
    __slots__ = ("name", "w", "r", "dsem", "dcnt")

    def __init__(self, name):
        self.name = name
        self.w = []
        self.r = []
        self.dsem = None
        self.dcnt = 0


class Op:
    __slots__ = ("eng", "fn", "deps", "signal", "ticket", "dma", "sem", "idx")


class Prog:
    CE = ("pe", "act", "dve", "pool")

    def __init__(self, nc):
        self.nc = nc
        self.ops = {e: [] for e in ("pe", "act", "dve", "pool", "sp")}
        self.esem = {e: nc.alloc_semaphore("s_" + e) for e in self.CE}
        self.pending = {e: [] for e in self.ops}
        self.dma_sems = {}
        self.named = {}
        self.nops = 0

    @staticmethod
    def _key(o):
        return o.eng if o.dma is None else ("d", id(o.sem))

    def _push(self, lst, o):
        k = self._key(o)
        for i, p in enumerate(lst):
            if self._key(p) == k:
                lst[i] = o
                return
        lst.append(o)

    def op(self, eng, fn, reads=(), writes=(), dma=None):
        o = Op()
        o.eng = eng
        o.fn = fn
        o.signal = False
        o.ticket = None
        o.dma = dma
        o.sem = None
        o.idx = self.nops
        self.nops += 1
        deps = list(self.pending[eng])
        self.pending[eng] = []
        for h in reads:
            deps.extend(h.w)
            if h.name.startswith("bank"):
                deps.extend(r for r in h.r if r.eng != eng)
        for h in writes:
            if not (dma is not None and h.w and all(p.dma is not None for p in h.w) and not h.r):
                deps.extend(h.w)
            deps.extend(h.r)
        if dma is not None:
            nk = (dma.name, eng)
            if nk not in self.named:
                sem_ = self.nc.alloc_semaphore("d_%s_%s_%d" % (dma.name, eng, len(self.dma_sems)))
                self.named[nk] = [sem_, 0]
                self.dma_sems[id(sem_)] = [sem_, 0]
            ent = self.named[nk]
            ent[1] += 1
            o.sem = ent[0]
            o.ticket = 16 * ent[1]
            self.dma_sems[id(o.sem)][1] = o.ticket
        fd = []
        for p in deps:
            if p is o:
                continue
            if p.dma is not None:
                fd.append(p)
            elif p.eng == eng:
                if eng != "pe" and SELF_SYNC:
                    p.signal = True
                    fd.append(p)
            else:
                p.signal = True
                fd.append(p)
        o.deps = fd
        for h in reads:
            if h not in writes:
                self._push(h.r, o)
        for h in writes:
            if dma is not None and h.w and all(p.dma is not None for p in h.w) and not h.r:
                self._push(h.w, o)
            else:
                h.w = [o]
            h.r = []
        self.ops[eng].append(o)
        return o

    def barrier(self):
        lasts = []
        for e in self.CE:
            for p in reversed(self.ops[e]):
                if p.dma is None:
                    lasts.append(p)
                    break
        dm = []
        for sid, (sem, tot) in self.dma_sems.items():
            if tot > 0:
                f = Op()
                f.eng = "dma"
                f.dma = True
                f.sem = sem
                f.ticket = tot
                f.signal = False
                dm.append(f)
        for e in self.ops:
            for p in lasts:
                if p.eng != e or (e != "pe" and SELF_SYNC):
                    p.signal = True
                    self.pending[e].append(p)
            self.pending[e].extend(dm)

    def emit(self):
        nc = self.nc
        for e in self.CE:
            c = 0
            for o in self.ops[e]:
                if o.signal and o.dma is None:
                    c += 1
                    o.ticket = c
        esem = self.esem

        def mk(ename):
            def body(eng):
                waited = {}
                for o in self.ops[ename]:
                    for p in o.deps:
                        sem = p.sem if p.dma is not None else esem[p.eng]
                        v = p.ticket
                        k = id(sem)
                        if waited.get(k, 0) < v:
                            eng.wait_ge(sem, v)
                            waited[k] = v
                    if os.environ.get("DUMP"):
                        print("OP", ename, o.idx, "dma" if o.dma is not None else "", "sig=%s" % o.ticket if (o.signal or o.dma is not None) else "",
                              "waits:", [((p.sem.name if p.dma is not None else p.eng), p.ticket) for p in o.deps])
                    ins = o.fn(eng)
                    if o.dma is not None:
                        ins.then_inc(o.sem, 16)
                    elif o.signal:
                        ins.then_inc(esem[ename], 1)
            return body

        with nc.Block() as block:
            block.sync(mk("sp"))
            block.tensor(mk("pe"))
            block.scalar(mk("act"))
            block.vector(mk("dve"))
            block.gpsimd(mk("pool"))


class Arena:
    def __init__(self, nc, base, top):
        self.nc = nc
        self.base = base
        self.top = top
        self.cur = base
        self.n = 0

    def reset(self):
        self.cur = self.base

    def alloc(self, shape, dtype, name="t"):
        esz = {F32: 4, BF16: 2, U8: 1}[dtype]
        nb = esz
        for s in shape[1:]:
            nb *= s
        nb = (nb + 31) // 32 * 32
        off = self.cur
        assert off + nb <= self.top, f"SBUF arena overflow {name}: {off + nb} > {self.top}"
        self.cur += nb
        self.n += 1
        return self.nc.alloc_sbuf_tensor_at(f"{name}_{self.n}", list(shape), dtype, offset=off)


def _rope_tables(kind, S):
    pos = np.arange(S)
    cos = np.ones((128, S), np.float32)
    sin = np.zeros((128, S), np.float32)
    perm = np.zeros((128, 128), np.float32)
    for p in range(128):
        if kind == "A":
            j = p % 32
            if j >= 8:
                continue
            half, i, first, theta, pv = 4, j % 4, j < 4, ROPE_THETA, pos
        elif kind == "C":
            j = p % 64
            if j >= 16:
                continue
            half, i, first, theta, pv = 8, j % 8, j < 8, ROPE_THETA, pos
        else:
            j = p % 64
            half, theta = 16, AXIAL_THETA
            if j < 32:
                i, first, pv = j % 16, j < 16, pos // 64
            else:
                i, first, pv = (j - 32) % 16, (j - 32) < 16, pos % 64
        inv = np.exp((np.float32(-math.log(theta)) * np.arange(half, dtype=np.float32)) / np.float32(half)).astype(np.float32)
        ang = (pv.astype(np.float32) * inv[i]).astype(np.float32).astype(np.float64)
        cos[p] = np.cos(ang).astype(np.float32)
        sn = np.sin(ang).astype(np.float32)
        sin[p] = -sn if first else sn
        partner = p + half if first else p - half
        perm[partner, p] = 1.0
    return cos, sin, perm


def _cmask_tiles():
    out = np.empty((20, 128, 512), np.float32)
    kl = np.arange(128)[:, None]
    ql = np.arange(512)[None, :]
    for di in range(20):
        dlt = di - 8
        d = 128 * dlt + kl - ql
        mult = np.zeros(d.shape, np.int64)
        for dil in (1, 4, 16):
            mult += ((d % dil) == 0) & (np.abs(d) <= 64 * dil)
        with np.errstate(divide="ignore"):
            out[di] = np.where(mult > 0, np.log(np.maximum(mult, 1)).astype(np.float32), np.float32(NEG))
    return out


def _na_tiles(rpb, S):
    L = rpb.shape[0]
    rows = S // 64
    T = S // 512
    start = np.clip(np.arange(rows) - 4, 0, rows - 8)
    c = np.arange(64)
    cs = np.clip(c - 8, 0, 48)
    col_in = (c[None, :] >= cs[:, None]) & (c[None, :] < cs[:, None] + 16)
    dc = np.clip(c[None, :] - c[:, None] + 15, 0, 30)
    out = np.full((L, 4, 3, 8, 128, 512), NEG, np.float32)
    tl = [0, 1 if T > 2 else 0, T - 1]
    kk = np.arange(128)
    qq = np.arange(512)
    for vi, t in enumerate(tl):
        for j in range(8):
            kb = 4 * t - 2 + j
            if kb < 0 or kb >= S // 128:
                continue
            krow = 2 * kb + kk // 64
            kcol = kk % 64
            qrow = 8 * t + qq // 64
            qcol = qq % 64
            st = start[qrow]
            vrow = (krow[:, None] >= st[None, :]) & (krow[:, None] < st[None, :] + 8)
            dr = np.clip(krow[:, None] - qrow[None, :] + 7, 0, 14)
            valid = vrow & col_in[qcol[None, :], kcol[:, None]]
            dcc = dc[qcol[None, :], kcol[:, None]]
            vals = rpb[:, :, dr, dcc]
            out[:, :, vi, j] = np.where(valid[None, None], vals, np.float32(NEG))
    return out


def _pack_pp(norm_attn, norm_mlp, conv_w, conv_b, qk_norm, diff_subln, norm_final):
    L = norm_attn.shape[0]
    pp = np.zeros((128, L * PPL + 8), np.float32)
    p64 = np.arange(128) % 64
    for l in range(L):
        o = l * PPL
        pp[:, o:o + 8] = norm_attn[l].reshape(8, 128).T
        pp[:, o + 8:o + 16] = norm_mlp[l].reshape(8, 128).T
        for j in range(3):
            pp[:, o + 16 + j * 44:o + 16 + (j + 1) * 44] = conv_w[l, j].reshape(44, 128).T
        pp[:, o + 148:o + 192] = conv_b[l].reshape(44, 128).T
        pp[:, o + 192] = qk_norm[l, 0][p64]
        pp[:, o + 193] = qk_norm[l, 1][p64]
        pp[:, o + 194] = diff_subln[l][p64]
    pp[:, L * PPL:L * PPL + 8] = norm_final.reshape(8, 128).T
    return pp


def build_program(S, NSEQ, L=2, debug=False, stop_after=None):
    NT = S // 512
    NB = S // 128
    nc = bass.Bass("TRN2", target_bir_lowering=False)
    pg = Prog(nc)

    def din(name, shape, dt=F32):
        return nc.dram_tensor(name, list(shape), dt, kind="ExternalInput").ap()

    def dscr(name, shape, dt):
        return nc.dram_tensor(name, list(shape), dt, kind=("ExternalOutput" if debug else "Internal")).ap()

    x_in = din("x", [NSEQ, S, D])
    w_in = din("w_in", [L, D, INC])
    w_branch = din("w_branch", [L, 4, 256, D])
    w_out = din("w_out", [L, D, D])
    w_up = din("w_up", [L, D, 2 * DFF])
    w_down = din("w_down", [L, DFF, D])
    pp_in = din("pp", [128, L * PPL + 8])
    dl_in = din("dl", [L, 128])
    nab_in = din("nab", [L, 4, 3, 8, 128, 512])
    cm_in = din("cmask", [20, 128, 512])
    rope_in = {k: (din("rope%s_c" % k, [128, S]), din("rope%s_s" % k, [128, S])) for k in "ACD"}
    perm_in = din("perms", [128, 3, 128])
    ident_in = din("ident", [128, 128])
    y_out = nc.dram_tensor("y", [NSEQ, S, D], F32, kind="ExternalOutput").ap()

    xTa = dscr("xTa", [KC, 128, S], F32)
    xTb = dscr("xTb", [KC, 128, S], F32)
    hT = dscr("hT", [KC, 128, S], BF16)
    oT = dscr("oT", [KC, 128, S], BF16)
    H_x = Hd("x_in")
    H_y = Hd("y")
    H_xTa, H_xTb, H_hT, H_oT = Hd("xTa"), Hd("xTb"), Hd("hT"), Hd("oT")
    H_w = Hd("weights")
    xTa_v = xTa.rearrange("k p s -> p k s")
    xTb_v = xTb.rearrange("k p s -> p k s")
    hT_v = hT.rearrange("k p s -> p k s")
    oT_v = oT.rearrange("k p s -> p k s")

    ident = nc.alloc_sbuf_tensor("sb_ident", [128, 128], F32)
    onesf = nc.alloc_sbuf_tensor("sb_onesf", [128, 128], F32)
    blk64 = nc.alloc_sbuf_tensor("sb_blk64", [128, 128], F32)
    perms = nc.alloc_sbuf_tensor("sb_perms", [128, 3, 128], BF16)
    pp = nc.alloc_sbuf_tensor("sb_pp", [128, L * PPL + 8], F32)
    dlr = nc.alloc_sbuf_tensor("sb_dlr", [1, L * 128], F32)
    lamw = nc.alloc_sbuf_tensor("sb_lamw", [1, 80], F32)
    neglam = nc.alloc_sbuf_tensor("sb_neglam", [64, 2 * L], F32)
    gsub = nc.alloc_sbuf_tensor("sb_gsub", [64, L], F32)
    epsb = nc.alloc_sbuf_tensor("sb_epsb", [128, 1], F32)
    dummy = nc.alloc_sbuf_tensor("sb_dummy", [128, 8], F32)
    H_c = Hd("consts")
    ps = nc.alloc_psum_tensor("psum_all", [128, 8, 512], F32)
    BK = [Hd("bank%d" % i) for i in range(8)]

    base = (nc.sbuf_base + 31) // 32 * 32
    top = nc.sbuf_top
    slab = nc.alloc_sbuf_tensor("sb_slab", [128, top - base - 64], U8)
    ar = Arena(nc, base, base + top - base - 64)

    def dma(eng, out, in_, reads, writes, semh):
        return pg.op(eng, lambda e: e.dma_start(out=out, in_=in_), reads, writes, dma=semh)

    def act(out, in_, func, reads, writes, scale=None, bias=None):
        kw = {}
        if scale is not None:
            kw["scale"] = scale
        if bias is not None:
            kw["bias"] = bias
        return pg.op("act", lambda e: e.activation(out=out, in_=in_, func=func, **kw), reads, writes)

    def tt(eng, out, in0, in1, op, reads, writes):
        return pg.op(eng, lambda e: e.tensor_tensor(out=out, in0=in0, in1=in1, op=op), reads, writes)

    def stt(out, in0, scalar, in1, op0, op1, reads, writes):
        return pg.op("dve", lambda e: e.scalar_tensor_tensor(out=out, in0=in0, scalar=scalar, in1=in1, op0=op0, op1=op1),
                     reads, writes)

    def ts(eng, out, in0, s1, s2, op0, op1, reads, writes):
        if op1 is None:
            return pg.op(eng, lambda e: e.tensor_scalar(out=out, in0=in0, scalar1=s1, scalar2=None, op0=op0), reads, writes)
        return pg.op(eng, lambda e: e.tensor_scalar(out=out, in0=in0, scalar1=s1, scalar2=s2, op0=op0, op1=op1), reads, writes)

    def recip(out, in_, reads, writes):
        return pg.op("dve", lambda e: e.reciprocal(out=out, in_=in_), reads, writes)

    def mmgroup(items, reads, writes):
        def fn(e):
            ins = None
            for (o_, l_, r_, st, sp_, tp) in items:
                if tp is None:
                    ins = e.matmul(o_, lhsT=l_, rhs=r_, start=st, stop=sp_)
                else:
                    ins = e.matmul(o_, lhsT=l_, rhs=r_, start=st, stop=sp_, tile_position=tp)
            return ins
        return pg.op("pe", fn, reads, writes)

    dma("sp", ident[:], ident_in, [H_w], [H_c], H_c)
    dma("sp", pp[:], pp_in, [H_w], [H_c], H_c)
    dma("sp", dlr[:], dl_in.rearrange("(o l) n -> o (l n)", o=1), [H_w], [H_c], H_c)
    dma("pool", perms[:], perm_in, [H_w], [H_c], H_c)
    H_c2 = Hd("consts2")
    pg.op("pool", lambda e: e.memset(onesf[:], 1.0), [], [H_c2])
    pg.op("pool", lambda e: e.memset(blk64[:], 0.0), [], [H_c2])
    pg.op("pool", lambda e: e.memset(blk64[0:64, 0:64], 1.0), [], [H_c2])
    pg.op("pool", lambda e: e.memset(blk64[64:128, 64:128], 1.0), [], [H_c2])
    pg.op("pool", lambda e: e.memset(epsb[:], EPS), [], [H_c2])
    CR = [H_c, H_c2]
    H_lam = Hd("lam")
    for l in range(L):
        lam_init = 0.8 - 0.6 * math.exp(-0.3 * l)
        o = l * 128
        tt("dve", lamw[0:1, 0:32], dlr[0:1, o:o + 32], dlr[0:1, o + 32:o + 64], ALU.mult, CR, [H_lam])
        tt("dve", lamw[0:1, 32:64], dlr[0:1, o + 64:o + 96], dlr[0:1, o + 96:o + 128], ALU.mult, CR + [H_lam], [H_lam])
        pg.op("dve", lambda e: e.tensor_reduce(out=lamw[0:1, 64:66], in_=lamw[0:1, 0:64].rearrange("o (a b) -> o a b", a=2),
                                               axis=AX.X, op=ALU.add), [H_lam], [H_lam])
        act(lamw[0:1, 66:68], lamw[0:1, 64:66], AF.Exp, [H_lam], [H_lam])
        for c in range(2):
            stt(lamw[0:1, 68 + c:69 + c], lamw[0:1, 67:68], -lam_init, lamw[0:1, 66:67], ALU.add, ALU.subtract, [H_lam], [H_lam])
        mmgroup([(ps[0:64, 7, 0:2], onesf[0:1, 0:64], lamw[0:1, 68:70], True, True, None)], [H_lam, H_c2], [BK[7]])
        pg.op("dve", lambda e, l=l: e.tensor_copy(out=neglam[:, 2 * l:2 * l + 2], in_=ps[0:64, 7, 0:2]), [BK[7]], [H_lam])
        ts("dve", gsub[:, l:l + 1], pp[0:64, l * PPL + 194:l * PPL + 195], 1.0 - lam_init, None, ALU.mult, None, CR + [H_lam], [H_lam])
    CR = CR + [H_lam]

    def rms_tile(xt, H_xt, ncols, gcol, out_bf, H_out, A, tagbank):
        sq = A["sq"]
        H_sq = A["H_sq"]
        rstd = A["rstd"]
        H_rstd = A["H_rstd"]
        bank = tagbank
        items = []
        for kc in range(KC):
            s = sq[kc % 2]
            hs = H_sq[kc % 2]
            act(s[:, 0:ncols], xt[:, kc, 0:ncols], AF.Square, [H_xt], [hs])
            mmgroup([(ps[:, bank, 0:ncols], onesf[:, :], s[:, 0:ncols], kc == 0, kc == KC - 1, None)], [hs, H_c2], [BK[bank]])
        act(rstd[:, 0:ncols], ps[:, bank, 0:ncols], AF.Sqrt, [BK[bank]] + CR, [H_rstd], scale=1.0 / D, bias=epsb[:, 0:1])
        recip(rstd[:, 0:ncols], rstd[:, 0:ncols], [H_rstd], [H_rstd])
        for kc in range(KC):
            stt(out_bf[:, kc, 0:ncols], xt[:, kc, 0:ncols], pp[:, gcol + kc:gcol + kc + 1], rstd[:, 0:ncols],
                ALU.mult, ALU.mult, [H_xt, H_rstd] + CR, [H_out])

    def phase_T(seq):
        ar.reset()
        xtok = [ar.alloc([128, 4, D], F32, "xtok") for _ in range(2)]
        Hk = [Hd("xtok%d" % i) for i in range(2)]
        xt = [ar.alloc([128, KC, 512], F32, "xt") for _ in range(2)]
        Hx = [Hd("xtT%d" % i) for i in range(2)]
        for t in range(NT):
            b = t % 2
            dma("sp", xtok[b][:], x_in[seq, t * 512:(t + 1) * 512, :].rearrange("(b p) d -> p b d", p=128), [H_x], [Hk[b]], Hk[b])
            for kc in range(KC):
                bank = kc % 4
                def fn(e, kc=kc, bank=bank, b=b):
                    ins = None
                    for blk in range(4):
                        ins = e.transpose(ps[:, bank, blk * 128:(blk + 1) * 128], xtok[b][:, blk, kc * 128:(kc + 1) * 128], ident[:, :])
                    return ins
                pg.op("pe", fn, [Hk[b]] + CR, [BK[bank]])
                if kc % 2 == 0:
                    act(xt[b][:, kc, :], ps[:, bank, :], AF.Copy, [BK[bank]], [Hx[b]])
                else:
                    pg.op("dve", lambda e, kc=kc, bank=bank, b=b: e.tensor_copy(out=xt[b][:, kc, :], in_=ps[:, bank, :]), [BK[bank]], [Hx[b]])
            dma("pool", xTa_v[:, :, t * 512:(t + 1) * 512], xt[b][:], [Hx[b]], [H_xTa], Hx[b])
        pg.barrier()

    def phase_Z(seq, src_v, H_src):
        ar.reset()
        A = {"sq": [ar.alloc([128, 512], F32, "sq") for _ in range(2)], "H_sq": [Hd("sq0"), Hd("sq1")],
             "rstd": ar.alloc([128, 512], F32, "rstd"), "H_rstd": Hd("rstd")}
        xt = [ar.alloc([128, KC, 512], F32, "xt") for _ in range(2)]
        Hx = [Hd("zx%d" % i) for i in range(2)]
        yn = ar.alloc([128, KC, 512], F32, "yn")
        H_yn = Hd("yn")
        ytok = [ar.alloc([128, 4, D], F32, "ytok") for _ in range(2)]
        Hyt = [Hd("ytok%d" % i) for i in range(2)]
        gcol = L * PPL
        for t in range(NT):
            b = t % 2
            dma("sp", xt[b][:], src_v[:, :, t * 512:(t + 1) * 512], [H_src], [Hx[b]], Hx[b])
            sq, H_sq, rstd, H_rstd = A["sq"], A["H_sq"], A["rstd"], A["H_rstd"]
            for kc in range(KC):
                act(sq[kc % 2][:], xt[b][:, kc, :], AF.Square, [Hx[b]], [H_sq[kc % 2]])
                mmgroup([(ps[:, 4, :], onesf[:, :], sq[kc % 2][:], kc == 0, kc == KC - 1, None)], [H_sq[kc % 2], H_c2], [BK[4]])
            act(rstd[:], ps[:, 4, :], AF.Sqrt, [BK[4]] + CR, [H_rstd], scale=1.0 / D, bias=epsb[:, 0:1])
            recip(rstd[:], rstd[:], [H_rstd], [H_rstd])
            for kc in range(KC):
                stt(yn[:, kc, :], xt[b][:, kc, :], pp[:, gcol + kc:gcol + kc + 1], rstd[:], ALU.mult, ALU.mult,
                    [Hx[b], H_rstd] + CR, [H_yn])
            for blk in range(4):
                for half in range(2):
                    bank = (blk * 2 + half) % 4
                    def fn(e, blk=blk, half=half, bank=bank):
                        ins = None
                        for q in range(4):
                            kc = half * 4 + q
                            ins = e.transpose(ps[:, bank, q * 128:(q + 1) * 128], yn[:, kc, blk * 128:(blk + 1) * 128], ident[:, :])
                        return ins
                    pg.op("pe", fn, [H_yn] + CR, [BK[bank]])
                    if half == 0:
                        act(ytok[b][:, blk, 0:512], ps[:, bank, :], AF.Copy, [BK[bank]], [Hyt[b]])
                    else:
                        pg.op("dve", lambda e, blk=blk, bank=bank, b=b: e.tensor_copy(out=ytok[b][:, blk, 512:1024], in_=ps[:, bank, :]),
                              [BK[bank]], [Hyt[b]])
            dma("pool", y_out[seq, t * 512:(t + 1) * 512, :].rearrange("(b p) d -> p b d", p=128), ytok[b][:], [Hyt[b]], [H_y], Hyt[b])
        pg.barrier()

    def phase_0(l, src_v, H_src):
        ar.reset()
        A = {"sq": [ar.alloc([128, 512], F32, "sq") for _ in range(2)], "H_sq": [Hd("sq0"), Hd("sq1")],
             "rstd": ar.alloc([128, 512], F32, "rstd"), "H_rstd": Hd("rstd")}
        xt = [ar.alloc([128, KC, 512], F32, "xt") for _ in range(2)]
        Hx = [Hd("p0x%d" % i) for i in range(2)]
        hb = [ar.alloc([128, KC, 512], BF16, "hb") for _ in range(2)]
        Hh = [Hd("p0h%d" % i) for i in range(2)]
        for t in range(NT):
            b = t % 2
            dma("sp", xt[b][:], src_v[:, :, t * 512:(t + 1) * 512], [H_src], [Hx[b]], Hx[b])
            rms_tile(xt[b], Hx[b], 512, l * PPL + 0, hb[b], Hh[b], A, 4 + (t % 2))
            dma("pool", hT_v[:, :, t * 512:(t + 1) * 512], hb[b][:], [Hh[b]], [H_hT], Hh[b])
        pg.barrier()

    def branch_pass(l, br):
        ar.reset()
        kind = "ABCD"[br]
        HV = 2 if kind == "D" else 4
        RQ = ar.alloc([128, 2, S], BF16, "RQ")
        RK = ar.alloc([128, 2, S], BF16, "RK")
        RV = ar.alloc([128, NB, HV, 128], BF16, "RV")
        mark = ar.cur
        H_RQ = [Hd("RQ%d" % t) for t in range(NT)]
        H_RK = [Hd("RK%d" % t) for t in range(NT)]
        H_RV = [Hd("RV%d" % t) for t in range(NT)]
        H_RVo = Hd("RVones")
        if kind == "D":
            c0 = 9 * 256
            ncol = 256 + 128 + 128
        else:
            c0 = br * 768
            ncol = 768
        W = ar.alloc([128, KC, ncol], BF16, "W")
        H_W = Hd("W")
        for kc in range(KC):
            dma("pool", W[:, kc, :], w_in[l, kc * 128:(kc + 1) * 128, c0:c0 + ncol], [H_w], [H_W], H_W)
        FL = set(os.environ.get("FLAGS", "").split(","))
        if "nomem" not in FL:
            pg.op("pool", lambda e: e.memset(RV[:, :, :, 64:128], 1.0), [], [H_RVo])
        ht = [ar.alloc([128, KC, 512], BF16, "ht") for _ in range(2)]
        Hht = [Hd("ht%d" % i) for i in range(2)]
        rot = kind in "ACD"
        if rot:
            cs_t = [ar.alloc([128, 2, 512], F32, "cs") for _ in range(2)]
            Hcs = [Hd("cs%d" % i) for i in range(2)]
            a16 = [ar.alloc([128, 512], BF16, "a16") for _ in range(2)]
            Ha16 = [Hd("a16_%d" % i) for i in range(2)]
            t1 = [ar.alloc([128, 512], F32, "t1") for _ in range(2)]
            Ht1 = [Hd("t1_%d" % i) for i in range(2)]
            t2 = [ar.alloc([128, 512], F32, "t2") for _ in range(2)]
            Ht2 = [Hd("t2_%d" % i) for i in range(2)]
            pidx = "ACD".index(kind)
        if kind == "D":
            sqd = [ar.alloc([128, 512], F32, "sqd") for _ in range(2)]
            Hsqd = [Hd("sqd%d" % i) for i in range(2)]
            rsd = [ar.alloc([128, 512], F32, "rsd") for _ in range(2)]
            Hrsd = [Hd("rsd%d" % i) for i in range(2)]

        if kind == "D":
            fm = [("q", RQ, 0, [(0, 128)]), ("q", RQ, 1, [(128, 128)]),
                  ("k", RK, 0, [(256, 128)]), ("k", RK, 1, [(320, 64), (256, 64)])]
            vcol = 384
            vw = 128
        else:
            fm = [("q", RQ, 0, [(0, 128)]), ("q", RQ, 1, [(128, 128)]),
                  ("k", RK, 0, [(256, 128)]), ("k", RK, 1, [(384, 128)])]
            vcol = 512
            vw = 256
        cnt = 0
        for t in range(NT):
            b = t % 2
            tsl = slice(t * 512, (t + 1) * 512)
            dma("sp", ht[b][:], hT_v[:, :, tsl], [H_hT], [Hht[b]], Hht[b])
            if rot:
                dma("sp", cs_t[b][:, 0, :], rope_in[kind][0][:, tsl], [H_w], [Hcs[b]], Hcs[b])
                dma("sp", cs_t[b][:, 1, :], rope_in[kind][1][:, tsl], [H_w], [Hcs[b]], Hcs[b])
            for (qk, dst, dch, pieces) in fm:
                bank = cnt % 2
                pb = 2 + cnt % 2
                sb_ = 4 + cnt % 2
                u = cnt % 2
                cnt += 1
                Hdst = (H_RQ if qk == "q" else H_RK)[t]
                items = []
                for kc in range(KC):
                    mo = 0
                    for (pc0, pw) in pieces:
                        items.append((ps[mo:mo + pw, bank, :], W[:, kc, pc0:pc0 + pw], ht[b][:, kc, :], kc == 0, kc == KC - 1, None))
                        mo += pw
                if len(pieces) == 1:
                    mmgroup(items, [H_W, Hht[b]], [BK[bank]])
                else:
                    i0 = [it for i, it in enumerate(items) if i % 2 == 0]
                    i1 = [it for i, it in enumerate(items) if i % 2 == 1]
                    mmgroup(i0 + i1, [H_W, Hht[b]], [BK[bank]])
                dsl = dst[:, dch, tsl]
                if not rot or "norot" in FL:
                    if cnt % 2 == 0:
                        act(dsl, ps[:, bank, :], AF.Copy, [BK[bank]], [Hdst])
                    else:
                        pg.op("dve", lambda e, dsl=dsl, bank=bank: e.tensor_copy(out=dsl, in_=ps[:, bank, :]), [BK[bank]], [Hdst])
                    continue
                if kind == "D":
                    gcol = l * PPL + (192 if qk == "q" else 193)
                    gap = pp[:, gcol:gcol + 1]
                    act(a16[u][:], ps[:, bank, :], AF.Identity, [BK[bank]] + CR, [Ha16[u]], scale=gap)
                    act(sqd[u][:], ps[:, bank, :], AF.Square, [BK[bank]], [Hsqd[u]])
                    mmgroup([(ps[:, sb_, :], blk64[:, :], sqd[u][:], True, True, None)], [Hsqd[u], H_c2], [BK[sb_]])
                    act(rsd[u][:], ps[:, sb_, :], AF.Sqrt, [BK[sb_]] + CR, [Hrsd[u]], scale=1.0 / 64, bias=epsb[:, 0:1])
                    recip(rsd[u][:], rsd[u][:], [Hrsd[u]], [Hrsd[u]])
                    stt(t1[u][:], ps[:, bank, :], gap, cs_t[b][:, 0, :], ALU.mult, ALU.mult, [BK[bank], Hcs[b]] + CR, [Ht1[u]])
                else:
                    act(a16[u][:], ps[:, bank, :], AF.Copy, [BK[bank]], [Ha16[u]])
                    if "rotA" in FL:
                        pg.op("dve", lambda e, dsl=dsl, u=u: e.tensor_copy(out=dsl, in_=a16[u][:]), [Ha16[u]], [Hdst])
                        continue
                    if "rotA2" in FL:
                        pg.op("dve", lambda e, u=u, bank=bank: e.tensor_copy(out=t1[u][:], in_=ps[:, bank, :]), [BK[bank]] + ([Ha16[u]] if "ser" in FL else []), [Ht1[u]])
                    else:
                        tt("dve", t1[u][:], ps[:, bank, :], cs_t[b][:, 0, :], ALU.mult, [BK[bank], Hcs[b]], [Ht1[u]])
                if "rotB" in FL:
                    if "rotBact" in FL:
                        act(dsl, t1[u][:], AF.Copy, [Ht1[u]], [Hdst])
                    else:
                        pg.op("dve", lambda e, dsl=dsl, u=u: e.tensor_copy(out=dsl, in_=t1[u][:]), [Ht1[u]], [Hdst])
                    continue
                mmgroup([(ps[:, pb, :], perms[:, pidx, :], a16[u][:], True, True, None)], [Ha16[u]] + CR, [BK[pb]])
                if "rotC" in FL:
                    pg.op("dve", lambda e, dsl=dsl, pb=pb: e.tensor_copy(out=dsl, in_=ps[:, pb, :]), [BK[pb]], [Hdst])
                    continue
                tt("dve", t2[u][:], ps[:, pb, :], cs_t[b][:, 1, :], ALU.mult, [BK[pb], Hcs[b]], [Ht2[u]])
                if kind == "D":
                    tt("pool", t1[u][:], t1[u][:], t2[u][:], ALU.add, [Ht1[u], Ht2[u]], [Ht1[u]])
                    tt("dve", dsl, t1[u][:], rsd[u][:], ALU.mult, [Ht1[u], Hrsd[u]], [Hdst])
                else:
                    tt("pool" if "pooladd" in FL else "dve", dsl, t1[u][:], t2[u][:], ALU.add, [Ht1[u], Ht2[u]], [Hdst])
            for blk in range(4 if "nov" not in FL else 0):
                bank = 6 + blk % 2
                items = [(ps[:, bank, 0:vw], ht[b][:, kc, blk * 128:(blk + 1) * 128], W[:, kc, vcol:vcol + vw], kc == 0, kc == KC - 1, None)
                         for kc in range(KC)]
                mmgroup(items, [H_W, Hht[b]], [BK[bank]])
                gb = t * 4 + blk
                src = ps[:, bank, 0:vw].rearrange("p (h d) -> p h d", h=HV)
                if blk % 2 == 0:
                    act(RV[:, gb, :, 0:64], src, AF.Copy, [BK[bank]], [H_RV[t]])
                else:
                    pg.op("dve", lambda e, gb=gb, src=src: e.tensor_copy(out=RV[:, gb, :, 0:64], in_=src), [BK[bank]], [H_RV[t]])

        if os.environ.get("SUBSTOP") == "proj":
            pg.barrier()
            return
        pg.barrier()
        ar.cur = mark
        PT = [ar.alloc([128, 1024], BF16, "PT") for _ in range(3)]
        HPT = [Hd("PT%d" % i) for i in range(3)]
        fsb = [ar.alloc([65, 512], F32, "fsb") for _ in range(4)]
        Hfsb = [Hd("fsb%d" % i) for i in range(4)]
        rr = fsb
        Hrr = Hfsb
        ost = [ar.alloc([64, 512], BF16, "ost") for _ in range(4)]
        Host = [Hd("ost%d" % i) for i in range(4)]
        if kind in "AD":
            KP = ar.alloc([128, 2, S], BF16, "KP")
            H_KP = Hd("KP")
        if kind == "A":
            o12 = [ar.alloc([64, 512], F32, "o12") for _ in range(4)]
            Ho12 = [Hd("o12_%d" % i) for i in range(4)]
            dd = [ar.alloc([64, 512], F32, "dd") for _ in range(2)]
            Hdd = [Hd("dd%d" % i) for i in range(2)]
            sqa = [ar.alloc([64, 512], F32, "sqa") for _ in range(2)]
            Hsqa = [Hd("sqa%d" % i) for i in range(2)]
        if kind in "BC":
            sbx = [ar.alloc([128, 1024], F32, "sbx") for _ in range(2)]
            Hsbx = [Hd("sbx%d" % i) for i in range(2)]
        if kind == "B":
            bsl = [ar.alloc([128, 2, 512], F32, "bsl") for _ in range(3)]
            Hbsl = [Hd("bsl%d" % i) for i in range(3)]
            bres = ar.alloc([128, 8, 2, 512], F32, "bres")
            H_bres = Hd("bres")
        if kind == "C":
            cm = ar.alloc([128, 20, 512], F32, "cm")
            H_cm = Hd("cm")
            for g in range(4):
                dma("sp", cm[:, g * 5:(g + 1) * 5, :], cm_in[g * 5:(g + 1) * 5].rearrange("j p n -> p j n"), [H_w], [H_cm], H_cm)
        allK = H_RK
        allV = H_RV + [H_RVo]
        scale = {"A": 32 ** -0.5, "B": 0.125, "C": 0.125, "D": 0.125}[kind]
        state = {"u": 0, "pt": 0, "sp": 0, "f": 0, "bs": 0, "bl": 0}

        def finalize(accset, unit_infos, t):
            tsl = slice(t * 512, (t + 1) * 512)
            if kind != "A":
                for i in range(2):
                    bank = accset[i]
                    f = state["f"] % 4
                    state["f"] += 1
                    pg.op("dve", lambda e, f=f, bank=bank: e.tensor_copy(out=fsb[f][0:65, :], in_=ps[0:65, bank, :]), [BK[bank]], [Hfsb[f]])
                    recip(rr[f][64:65, :], fsb[f][64:65, :], [Hfsb[f]], [Hrr[f]])
                    mmgroup([(ps[0:64, bank, :], onesf[64:65, 0:64], rr[f][64:65, :], True, True, None)], [Hrr[f], H_c2], [BK[bank]])
                    tt("dve", ost[f][:], fsb[f][0:64, :], ps[0:64, bank, :], ALU.mult, [Hfsb[f], BK[bank]], [Host[f]])
                    ch, pb_ = unit_infos[i]
                    dma("pool", oT_v[pb_:pb_ + 64, ch, tsl], ost[f][:], [Host[f]], [H_oT], Host[f])
                return
            fs = []
            for i in range(2):
                bank = accset[i]
                f = state["f"] % 4
                state["f"] += 1
                fs.append(f)
                pg.op("dve", lambda e, f=f, bank=bank: e.tensor_copy(out=fsb[f][0:65, :], in_=ps[0:65, bank, :]), [BK[bank]], [Hfsb[f]])
                recip(rr[f][64:65, :], fsb[f][64:65, :], [Hfsb[f]], [Hrr[f]])
                mmgroup([(ps[0:64, bank, :], onesf[64:65, 0:64], rr[f][64:65, :], True, True, None)], [Hrr[f], H_c2], [BK[bank]])
                tt("dve", o12[f][:], fsb[f][0:64, :], ps[0:64, bank, :], ALU.mult, [Hfsb[f], BK[bank]], [Ho12[f]])
            u = (state["f"] // 2) % 2
            stt(dd[u][:], o12[fs[1]][:], neglam[:, 2 * l:2 * l + 1], o12[fs[0]][:], ALU.mult, ALU.add,
                [Ho12[fs[0]], Ho12[fs[1]]] + CR, [Hdd[u]])
            tt("pool", sqa[u][:], dd[u][:], dd[u][:], ALU.mult, [Hdd[u]], [Hsqa[u]])
            bank = accset[0]
            mmgroup([(ps[0:64, bank, :], onesf[0:64, 0:64], sqa[u][:], True, True, None)], [Hsqa[u], H_c2], [BK[bank]])
            act(sqa[u][:], ps[0:64, bank, :], AF.Sqrt, [BK[bank]] + CR, [Hsqa[u]], scale=1.0 / 64, bias=epsb[0:64, 0:1])
            recip(sqa[u][:], sqa[u][:], [Hsqa[u]], [Hsqa[u]])
            f = fs[0]
            stt(ost[f][:], dd[u][:], gsub[:, l:l + 1], sqa[u][:], ALU.mult, ALU.mult, [Hdd[u], Hsqa[u]] + CR, [Host[f]])
            ch, pb_ = unit_infos[0]
            dma("pool", oT_v[pb_:pb_ + 64, ch, tsl], ost[f][:], [Host[f]], [H_oT], Host[f])

        def attn_unit(t, qk_items, v_aps, kbs, bias_fn, unit_infos, bias_pre=None, kh=None):
            accset = (4, 5) if state["u"] % 2 == 0 else (6, 7)
            state["u"] += 1
            n = len(kbs)

            def emit_qk(j):
                kb = kbs[j]
                if bias_pre is not None:
                    bias_pre(kb)
                sp_ = state["sp"] % 2
                state["sp"] += 1
                banks = (2 * sp_, 2 * sp_ + 1)
                items = []
                for i in range(2):
                    lhsT, rhs, tp = qk_items[i](kb)
                    items.append((ps[:, banks[i], :], lhsT, rhs, True, True, tp))
                mmgroup(items, [H_RQ[t]] + ([H_RK[kb // 4]] if kh is None else kh), [BK[banks[0]], BK[banks[1]]])
                return banks

            pend = emit_qk(0)
            for j in range(n):
                kb = kbs[j]
                banks = pend
                p = state["pt"] % 3
                state["pt"] += 1
                if bias_fn is None:
                    act(PT[p][:], ps[:, banks[0]:banks[0] + 2, :], AF.Exp, [BK[banks[0]], BK[banks[1]]], [HPT[p]], scale=scale)
                else:
                    sx = state["bs"] % 2
                    for i in range(2):
                        bap, bh = bias_fn(i, kb)
                        stt(sbx[sx][:, i * 512:(i + 1) * 512], ps[:, banks[i], :], scale, bap, ALU.mult, ALU.add,
                            [BK[banks[i]], bh], [Hsbx[sx]])
                    state["bs"] += 1
                    act(PT[p][:], sbx[sx][:], AF.Exp, [Hsbx[sx]], [HPT[p]])
                if j + 1 < n:
                    pend = emit_qk(j + 1)
                items = []
                for i in range(2):
                    items.append((ps[:, accset[i], :], v_aps[i](kb), PT[p][:, i * 512:(i + 1) * 512], j == 0, j == n - 1, None))
                mmgroup(items, [HPT[p], H_RV[kb // 4], H_RVo], [BK[accset[0]], BK[accset[1]]])
            if os.environ.get("SUBSTOP") != "nofin":
                finalize(accset, unit_infos, t)

        if kind == "A":
            order = [(t, h) for h in range(4) for t in range(NT)]
        elif kind == "D":
            order = [(t, g) for g in range(2) for t in range(NT)]
        elif kind == "B":
            order = [(t, hp) for hp in range(2) for t in range(NT)]
        else:
            order = [(t, None) for t in range(NT)]
        for (t, hp_sel) in order:
            if os.environ.get("SUBSTOP") in ("unit1", "nofin") and t > 0:
                break
            tsl = slice(t * 512, (t + 1) * 512)
            if kind == "B" and t == 0 and NT > 2:
                for j in range(8):
                    dma("sp", bres[:, j, :, :], nab_in[l, 2 * hp_sel:2 * hp_sel + 2, 1, j].rearrange("h p n -> p h n"), [H_w], [H_bres], H_bres)
            if kind == "A":
                for h in [hp_sel]:
                    ch, pb_ = h // 2, (h % 2) * 64
                    if t == 0:
                        pg.op("pool", lambda e: e.memset(KP[:], 0.0), [], [H_KP])
                        for c in range(2):
                            p0 = pb_ + 32 * c
                            pg.op("pool", lambda e, p0=p0, c=c, ch=ch: e.tensor_copy(out=KP[p0:p0 + 32, c, :], in_=RK[p0:p0 + 32, ch, :]),
                                  list(H_RK), [H_KP])
                    def mk(i, ch=ch):
                        return lambda kb: (KP[:, i, kb * 128:(kb + 1) * 128], RQ[:, ch, tsl], None)
                    va = lambda kb, h=h: RV[:, kb, h, :]
                    attn_unit(t, [mk(0), mk(1)], [va, va], list(range(NB)), None, [(0 * 2 + ch, pb_), None], kh=[H_KP])
            elif kind == "D":
                for g in [hp_sel]:
                    if t == 0:
                        pg.op("pool", lambda e: e.memset(KP[:], 0.0), [], [H_KP])
                        for i in range(2):
                            pb_ = 64 * i
                            kch = (0 if g == 0 else 1) if i == 0 else (1 if g == 0 else 0)
                            pg.op("pool", lambda e, pb_=pb_, i=i, kch=kch: e.tensor_copy(out=KP[pb_:pb_ + 64, i, :], in_=RK[pb_:pb_ + 64, kch, :]),
                                  list(H_RK), [H_KP])
                    def mk(i, g=g):
                        return lambda kb: (KP[:, i, kb * 128:(kb + 1) * 128], RQ[:, g, tsl], None)
                    va = lambda kb, g=g: RV[:, kb, g, :]
                    attn_unit(t, [mk(0), mk(1)], [va, va], list(range(NB)), None, [(6 + g, 0), (6 + g, 64)], kh=[H_KP])
            else:
                for hp in ([hp_sel] if hp_sel is not None else range(2)):
                    def mk(i, hp=hp):
                        pb_ = 64 * i
                        return lambda kb: (RK[pb_:pb_ + 64, hp, kb * 128:(kb + 1) * 128], RQ[pb_:pb_ + 64, hp, tsl], None)
                    vas = [(lambda kb, h=2 * hp + i: RV[:, kb, h, :]) for i in range(2)]
                    if kind == "B":
                        kbs = [kb for kb in range(4 * t - 2, 4 * t + 6) if 0 <= kb < NB]
                        var = 0 if t == 0 else (2 if t == NT - 1 else 1)
                        cache = {}
                        def bias_pre(kb, hp=hp, t=t, var=var, cache=cache):
                            s_ = state["bl"] % 3
                            state["bl"] += 1
                            j = kb - (4 * t - 2)
                            dma("sp", bsl[s_][:], nab_in[l, 2 * hp:2 * hp + 2, var, j].rearrange("h p n -> p h n"), [H_w], [Hbsl[s_]], Hbsl[s_])
                            cache[kb] = s_
                        def bias_fn(i, kb, cache=cache):
                            s_ = cache[kb]
                            return bsl[s_][:, i, :], Hbsl[s_]
                        if var == 1:
                            bias_pre = None
                            def bias_fn(i, kb, t=t):
                                return bres[:, kb - (4 * t - 2), i, :], H_bres
                    else:
                        kbs = [kb for kb in range(4 * t - 8, 4 * t + 12) if 0 <= kb < NB]
                        bias_pre = None
                        def bias_fn(i, kb, t=t):
                            return cm[:, kb - 4 * t + 8, :], H_cm
                    br_ch = 2 * br + hp
                    attn_unit(t, [mk(0), mk(1)], vas, kbs, bias_fn, [(br_ch, 0), (br_ch, 64)], bias_pre)
        pg.barrier()

    def merge_pass(l, src_v, H_src, dst_v, H_dst):
        ar.reset()
        Wg = ar.alloc([128, KC, 4096], BF16, "Wg")
        Wbr = ar.alloc([128, 8, D], BF16, "Wbr")
        Wo = ar.alloc([128, KC, D], BF16, "Wo")
        H_Wm = Hd("Wm")
        for kc in range(KC):
            dma("pool", Wg[:, kc, :], w_in[l, kc * 128:(kc + 1) * 128, 2816:6912], [H_w], [H_Wm], H_Wm)
        for bq in range(4):
            dma("pool", Wbr[:, 2 * bq:2 * bq + 2, :], w_branch[l, bq].rearrange("(k p) n -> p k n", p=128), [H_w], [H_Wm], H_Wm)
        dma("pool", Wo[:], w_out[l].rearrange("(k p) n -> p k n", p=128), [H_w], [H_Wm], H_Wm)
        xt = [ar.alloc([128, KC, 512], F32, "xt") for _ in range(2)]
        Hx = [Hd("mx%d" % i) for i in range(2)]
        ht = [ar.alloc([128, KC, 512], BF16, "ht") for _ in range(2)]
        Hht = [Hd("mh%d" % i) for i in range(2)]
        ot = [ar.alloc([128, KC, 512], BF16, "ot") for _ in range(2)]
        Hot = [Hd("mo%d" % i) for i in range(2)]
        mg = ar.alloc([128, KC, 512], BF16, "mg")
        H_mg = Hd("mg")
        sg = [ar.alloc([128, 512], F32, "sg") for _ in range(2)]
        Hsg = [Hd("sg%d" % i) for i in range(2)]
        acc = [ar.alloc([128, 512], F32, "macc") for _ in range(2)]
        Hacc = [Hd("macc%d" % i) for i in range(2)]
        tmp = [ar.alloc([128, 512], F32, "mtmp") for _ in range(2)]
        Htmp = [Hd("mtmp%d" % i) for i in range(2)]
        cnt = 0
        for t in range(NT):
            b = t % 2
            tsl = slice(t * 512, (t + 1) * 512)
            dma("sp", ht[b][:], hT_v[:, :, tsl], [H_hT], [Hht[b]], Hht[b])
            dma("sp", ot[b][:], oT_v[:, :, tsl], [H_oT], [Hot[b]], Hot[b])
            dma("sp", xt[b][:], src_v[:, :, tsl], [H_src], [Hx[b]], Hx[b])
            for oc in range(KC):
                a = oc % 2
                for bq in range(4):
                    gb = cnt % 2
                    mb = 2 + cnt % 2
                    u = cnt % 2
                    cnt += 1
                    col = bq * D + oc * 128
                    mmgroup([(ps[:, gb, :], Wg[:, kc, col:col + 128], ht[b][:, kc, :], kc == 0, kc == KC - 1, None) for kc in range(KC)],
                            [H_Wm, Hht[b]], [BK[gb]])
                    mmgroup([(ps[:, mb, :], Wbr[:, 2 * bq + j, oc * 128:(oc + 1) * 128], ot[b][:, 2 * bq + j, :], j == 0, j == 1, None)
                             for j in range(2)], [H_Wm, Hot[b]], [BK[mb]])
                    act(sg[u][:], ps[:, gb, :], AF.Sigmoid, [BK[gb]], [Hsg[u]])
                    if bq == 0:
                        tt("dve", acc[a][:], sg[u][:], ps[:, mb, :], ALU.mult, [Hsg[u], BK[mb]], [Hacc[a]])
                    elif bq < 3:
                        tt("dve", tmp[u][:], sg[u][:], ps[:, mb, :], ALU.mult, [Hsg[u], BK[mb]], [Htmp[u]])
                        tt("pool", acc[a][:], acc[a][:], tmp[u][:], ALU.add, [Hacc[a], Htmp[u]], [Hacc[a]])
                    else:
                        tt("dve", tmp[u][:], sg[u][:], ps[:, mb, :], ALU.mult, [Hsg[u], BK[mb]], [Htmp[u]])
                        tt("pool", mg[:, oc, :], acc[a][:], tmp[u][:], ALU.add, [Hacc[a], Htmp[u]], [H_mg])
            for oc in range(KC):
                bank = 4 + oc % 4
                mmgroup([(ps[:, bank, :], Wo[:, kc, oc * 128:(oc + 1) * 128], mg[:, kc, :], kc == 0, kc == KC - 1, None) for kc in range(KC)],
                        [H_Wm, H_mg], [BK[bank]])
                tt("dve", xt[b][:, oc, :], xt[b][:, oc, :], ps[:, bank, :], ALU.add, [Hx[b], BK[bank]], [Hx[b]])
            dma("pool", dst_v[:, :, tsl], xt[b][:], [Hx[b]], [H_dst], Hx[b])
        pg.barrier()

    def mlp_pass(l, src_v, H_src, dst_v, H_dst):
        ar.reset()
        Wup = ar.alloc([128, KC, 2 * DFF], BF16, "Wup")
        Wdn = ar.alloc([128, NFC, D], BF16, "Wdn")
        H_Wf = Hd("Wf")
        for kc in range(KC):
            dma("pool", Wup[:, kc, :], w_up[l, kc * 128:(kc + 1) * 128, :], [H_w], [H_Wf], H_Wf)
        for j in range(0, NFC, 2):
            dma("pool", Wdn[:, j:j + 2, :], w_down[l, j * 128:(j + 2) * 128, :].rearrange("(k p) n -> p k n", p=128), [H_w], [H_Wf], H_Wf)
        A = {"sq": [ar.alloc([128, 512], F32, "sq") for _ in range(2)], "H_sq": [Hd("sq0"), Hd("sq1")],
             "rstd": ar.alloc([128, 512], F32, "rstd"), "H_rstd": Hd("rstd")}
        xt = [ar.alloc([128, KC, 512], F32, "xt") for _ in range(1)]
        Hx = [Hd("fx%d" % i) for i in range(1)]
        h2 = ar.alloc([128, KC, 512], BF16, "h2")
        H_h2 = Hd("h2")
        gT = ar.alloc([128, NFC, 512], BF16, "gT")
        H_gT = Hd("gT")
        cv = [ar.alloc([128, 512], F32, "cv") for _ in range(2)]
        Hcv = [Hd("cv%d" % i) for i in range(2)]
        cg = [ar.alloc([128, 512], F32, "cg") for _ in range(2)]
        Hcg = [Hd("cg%d" % i) for i in range(2)]
        ntile = (S + 509) // 510
        po = l * PPL
        cnt = 0
        for i in range(ntile):
            b = 0
            c0 = 510 * i
            nv = min(510, S - c0)
            lo = max(c0 - 1, 0)
            hi = min(c0 + nv + 1, S)
            off = lo - (c0 - 1)
            nl = hi - lo
            dma("sp", xt[b][:, :, off:off + nl], src_v[:, :, lo:hi], [H_src], [Hx[b]], Hx[b])
            ncols = off + nl
            if off > 0:
                pg.op("pool", lambda e: e.memset(xt[0][:, :, 0:1], 0.0), [Hx[b]], [Hx[b]])
            rms_tile(xt[b], Hx[b], ncols, po + 8, h2, H_h2, A, 6)
            if off > 0:
                pg.op("pool", lambda e: e.memset(h2[:, :, 0:1], 0.0), [H_h2], [H_h2])
            if ncols < 512:
                pg.op("pool", lambda e, ncols=ncols: e.memset(h2[:, :, ncols:512], 0.0), [H_h2], [H_h2])
            if off > 0:
                pass
            for j in range(NFC):
                u = cnt % 2
                cnt += 1
                bv = 0 + u * 2
                bg = 1 + u * 2
                mmgroup([(ps[:, bv, :], Wup[:, kc, j * 128:(j + 1) * 128], h2[:, kc, :], kc == 0, kc == KC - 1, None) for kc in range(KC)],
                        [H_Wf, H_h2], [BK[bv]])
                mmgroup([(ps[:, bg, :], Wup[:, kc, DFF + j * 128:DFF + (j + 1) * 128], h2[:, kc, :], kc == 0, kc == KC - 1, None) for kc in range(KC)],
                        [H_Wf, H_h2], [BK[bg]])
                for (bank, cbuf, Hc, chn) in ((bv, cv[u], Hcv[u], j), (bg, cg[u], Hcg[u], NFC + j)):
                    w0 = pp[:, po + 16 + 0 * 44 + chn:po + 16 + 0 * 44 + chn + 1]
                    w1 = pp[:, po + 16 + 1 * 44 + chn:po + 16 + 1 * 44 + chn + 1]
                    w2 = pp[:, po + 16 + 2 * 44 + chn:po + 16 + 2 * 44 + chn + 1]
                    bb = pp[:, po + 148 + chn:po + 148 + chn + 1]
                    act(cbuf[:, 0:510], ps[:, bank, 1:511], AF.Identity, [BK[bank]] + CR, [Hc], scale=w1, bias=bb)
                    stt(cbuf[:, 0:510], ps[:, bank, 0:510], w0, cbuf[:, 0:510], ALU.mult, ALU.add, [BK[bank], Hc] + CR, [Hc])
                    stt(cbuf[:, 0:510], ps[:, bank, 2:512], w2, cbuf[:, 0:510], ALU.mult, ALU.add, [BK[bank], Hc] + CR, [Hc])
                act(cg[u][:, 0:510], cg[u][:, 0:510], AF.Gelu, [Hcg[u]], [Hcg[u]])
                tt("pool", gT[:, j, 0:510], cg[u][:, 0:510], cv[u][:, 0:510], ALU.mult, [Hcg[u], Hcv[u]], [H_gT])
            for oc in range(KC):
                bank = 4 + oc % 2
                mmgroup([(ps[:, bank, 0:510], Wdn[:, j, oc * 128:(oc + 1) * 128], gT[:, j, 0:510], j == 0, j == NFC - 1, None) for j in range(NFC)],
                        [H_Wf, H_gT], [BK[bank]])
                tt("dve", xt[b][:, oc, 1:1 + nv], xt[b][:, oc, 1:1 + nv], ps[:, bank, 0:nv], ALU.add, [Hx[b], BK[bank]], [Hx[b]])
            dma("pool", dst_v[:, :, c0:c0 + nv], xt[b][:, :, 1:1 + nv], [Hx[b]], [H_dst], Hx[b])
        pg.barrier()

    stages = []
    for seq in range(NSEQ):
        stages.append(("T", lambda seq=seq: phase_T(seq)))
        for l in range(L):
            stages.append(("P0", lambda l=l: phase_0(l, xTa_v, H_xTa)))
            for br in range(4):
                stages.append(("BR%d" % br, lambda l=l, br=br: branch_pass(l, br)))
            stages.append(("MG", lambda l=l: merge_pass(l, xTa_v, H_xTa, xTb_v, H_xTb)))
            stages.append(("FF", lambda l=l: mlp_pass(l, xTb_v, H_xTb, xTa_v, H_xTa)))
        stages.append(("Z", lambda seq=seq: phase_Z(seq, xTa_v, H_xTa)))
    for i, (nm, fn) in enumerate(stages):
        if stop_after is not None and i >= stop_after:
            break
        fn()
    pg.barrier()
    pg.op("pool", lambda e: e.memset(dummy[:], 0.0), [], [Hd("dummy")])
    pg.emit()
    return nc


_CACHE = {}


def _host_consts(S, inputs):
    c = {}
    perms = np.zeros((128, 3, 128), np.float32)
    for i, k in enumerate("ACD"):
        cs, sn, pm = _rope_tables(k, S)
        c["rope%s_c" % k] = cs
        c["rope%s_s" % k] = sn
        perms[:, i, :] = pm
    c["perms"] = perms
    c["ident"] = np.eye(128, dtype=np.float32)
    c["cmask"] = _cmask_tiles()
    c["nab"] = _na_tiles(np.asarray(inputs["na_rpb"], np.float32), S)
    c["pp"] = _pack_pp(*[np.asarray(inputs[k], np.float32) for k in
                         ("norm_attn", "norm_mlp", "conv_w", "conv_b", "qk_norm", "diff_subln", "norm_final")])
    c["dl"] = np.ascontiguousarray(np.asarray(inputs["diff_lambda"], np.float32).reshape(-1, 128))
    for k in ("w_in", "w_branch", "w_out", "w_up", "w_down"):
        c[k] = np.ascontiguousarray(np.asarray(inputs[k], np.float32))
    return c


def run_sequences(xs, inputs, n_cores=N_CORES, nseq=None, **bk):
    n, S, _ = xs.shape
    if nseq is None:
        nseq = (n + n_cores - 1) // n_cores
    L = np.asarray(inputs["w_in"]).shape[0]
    key = (S, nseq, L, tuple(sorted(bk.items())))
    if key not in _CACHE:
        _CACHE[key] = build_program(S, nseq, L, **bk)
    nc = _CACHE[key]
    consts = _host_consts(S, inputs)
    slots = [[(c + s * n_cores) if (c + s * n_cores) < n else (c % n) for s in range(nseq)] for c in range(n_cores)]
    in_maps = []
    for c in range(n_cores):
        m = dict(consts)
        m["x"] = np.ascontiguousarray(xs[slots[c]])
        in_maps.append(m)
    res = run_bass_kernel_spmd(nc, in_maps, core_ids=list(range(n_cores)))
    out = np.empty_like(xs)
    for c in range(n_cores):
        for s in range(nseq):
            i = c + s * n_cores
            if i < n:
                out[i] = res.results[c]["y"][s]
    return out, res


def kernel(x_prompt, x_sample, norm_attn, w_in, diff_lambda, diff_subln, na_rpb, qk_norm, w_branch,
           w_out, norm_mlp, w_up, conv_w, conv_b, w_down, norm_final):
    inputs = dict(norm_attn=norm_attn, w_in=w_in, diff_lambda=diff_lambda, diff_subln=diff_subln, na_rpb=na_rpb,
                  qk_norm=qk_norm, w_branch=w_branch, w_out=w_out, norm_mlp=norm_mlp, w_up=w_up, conv_w=conv_w,
                  conv_b=conv_b, w_down=w_down, norm_final=norm_final)
    xp = np.asarray(x_prompt, np.float32)
    xs_ = np.asarray(x_sample, np.float32)
    xs = np.concatenate([xs_, xp], axis=0)
    out, _ = run_sequences(xs, inputs)
    nsamp = xs_.shape[0]
    return (np.ascontiguousarray(out[nsamp:]), np.ascontiguousarray(out[:nsamp]))
```

```python
import math
import os
import numpy as np
import concourse.bass as bass
import concourse.mybir as mybir
from concourse.bass_utils import run_bass_kernel_spmd

F32 = mybir.dt.float32
BF16 = mybir.dt.bfloat16
U8 = mybir.dt.uint8
AF = mybir.ActivationFunctionType
ALU = mybir.AluOpType
AX = mybir.AxisListType

D = 1024
KC = 8
DFF = 2816
NFC = 22
INC = 6912
EPS = 1e-6
NEG = -30000.0
N_CORES = 8
ROPE_THETA = 500000.0
AXIAL_THETA = 10000.0
PPL = 196
SELF_SYNC = os.environ.get('NOSELF') is None


class Hd:
    __slots__ = ("name", "w", "r", "dsem", "dcnt")

    def __init__(self, name):
        self.name = name
        self.w = []
        self.r = []
        self.dsem = None
        self.dcnt = 0


class Op:
    __slots__ = ("eng", "fn", "deps", "signal", "ticket", "dma", "sem", "idx")


class Prog:
    CE = ("pe", "act", "dve", "pool")

    def __init__(self, nc):
        self.nc = nc
        self.ops = {e: [] for e in ("pe", "act", "dve", "pool", "sp")}
        self.esem = {e: nc.alloc_semaphore("s_" + e) for e in self.CE}
        self.pending = {e: [] for e in self.ops}
        self.dma_sems = {}
        self.named = {}
        self.nops = 0

    @staticmethod
    def _key(o):
        return o.eng if o.dma is None else ("d", id(o.sem))

    def _push(self, lst, o):
        k = self._key(o)
        for i, p in enumerate(lst):
            if self._key(p) == k:
                lst[i] = o
                return
        lst.append(o)

    def op(self, eng, fn, reads=(), writes=(), dma=None):
        o = Op()
        o.eng = eng
        o.fn = fn
        o.signal = False
        o.ticket = None
        o.dma = dma
        o.sem = None
        o.idx = self.nops
        self.nops += 1
        deps = list(self.pending[eng])
        self.pending[eng] = []
        for h in reads:
            deps.extend(h.w)
            if h.name.startswith("bank"):
                deps.extend(r for r in h.r if r.eng != eng)
        for h in writes:
            if not (dma is not None and h.w and all(p.dma is not None for p in h.w) and not h.r):
                deps.extend(h.w)
            deps.extend(h.r)
        if dma is not None:
            nk = (dma.name, eng)
            if nk not in self.named:
                sem_ = self.nc.alloc_semaphore("d_%s_%s_%d" % (dma.name, eng, len(self.dma_sems)))
                self.named[nk] = [sem_, 0]
                self.dma_sems[id(sem_)] = [sem_, 0]
            ent = self.named[nk]
            ent[1] += 1
            o.sem = ent[0]
            o.ticket = 16 * ent[1]
            self.dma_sems[id(o.sem)][1] = o.ticket
        fd = []
        for p in deps:
            if p is o:
                continue
            if p.dma is not None:
                fd.append(p)
            elif p.eng == eng:
                if eng != "pe" and SELF_SYNC:
                    p.signal = True
                    fd.append(p)
            else:
                p.signal = True
                fd.append(p)
        o.deps = fd
        for h in reads:
            if h not in writes:
                self._push(h.r, o)
        for h in writes:
            if dma is not None and h.w and all(p.dma is not None for p in h.w) and not h.r:
                self._push(h.w, o)
            else:
                h.w = [o]
            h.r = []
        self.ops[eng].append(o)
        return o

    def barrier(self):
        lasts = []
        for e in self.CE:
            for p in reversed(self.ops[e]):
                if p.dma is None:
                    lasts.append(p)
                    break
        dm = []
        for sid, (sem, tot) in self.dma_sems.items():
            if tot > 0:
                f = Op()
                f.eng = "dma"
                f.dma = True
                f.sem = sem
                f.ticket = tot
                f.signal = False
                dm.append(f)
        for e in self.ops:
            for p in lasts:
                if p.eng != e or (e != "pe" and SELF_SYNC):
                    p.signal = True
                    self.pending[e].append(p)
            self.pending[e].extend(dm)

    def emit(self):
        nc = self.nc
        for e in self.CE:
            c = 0
            for o in self.ops[e]:
                if o.signal and o.dma is None:
                    c += 1
                    o.ticket = c
        esem = self.esem

        def mk(ename):
            def body(eng):
                waited = {}
                for o in self.ops[ename]:
                    for p in o.deps:
                        sem = p.sem if p.dma is not None else esem[p.eng]
                        v = p.ticket
                        k = id(sem)
                        if waited.get(k, 0) < v:
                            eng.wait_ge(sem, v)
                            waited[k] = v
                    if os.environ.get("DUMP"):
                        print("OP", ename, o.idx, "dma" if o.dma is not None else "", "sig=%s" % o.ticket if (o.signal or o.dma is not None) else "",
                              "waits:", [((p.sem.name if p.dma is not None else p.eng), p.ticket) for p in o.deps])
                    ins = o.fn(eng)
                    if o.dma is not None:
                        ins.then_inc(o.sem, 16)
                    elif o.signal:
                        ins.then_inc(esem[ename], 1)
            return body

        with nc.Block() as block:
            block.sync(mk("sp"))
            block.tensor(mk("pe"))
            block.scalar(mk("act"))
            block.vector(mk("dve"))
            block.gpsimd(mk("pool"))


class Arena:
    def __init__(self, nc, base, top):
        self.nc = nc
        self.base = base
        self.top = top
        self.cur = base
        self.n = 0

    def reset(self):
        self.cur = self.base

    def alloc(self, shape, dtype, name="t"):
        esz = {F32: 4, BF16: 2, U8: 1}[dtype]
        nb = esz
        for s in shape[1:]:
            nb *= s
        nb = (nb + 31) // 32 * 32
        off = self.cur
        assert off + nb <= self.top, f"SBUF arena overflow {name}: {off + nb} > {self.top}"
        self.cur += nb
        self.n += 1
        return self.nc.alloc_sbuf_tensor_at(f"{name}_{self.n}", list(shape), dtype, offset=off)


def _rope_tables(kind, S):
    pos = np.arange(S)
    cos = np.ones((128, S), np.float32)
    sin = np.zeros((128, S), np.float32)
    perm = np.zeros((128, 128), np.float32)
    for p in range(128):
        if kind == "A":
            j = p % 32
            if j >= 8:
                continue
            half, i, first, theta, pv = 4, j % 4, j < 4, ROPE_THETA, pos
        elif kind == "C":
            j = p % 64
            if j >= 16:
                continue
            half, i, first, theta, pv = 8, j % 8, j < 8, ROPE_THETA, pos
        else:
            j = p % 64
            half, theta = 16, AXIAL_THETA
            if j < 32:
                i, first, pv = j % 16, j < 16, pos // 64
            else:
                i, first, pv = (j - 32) % 16, (j - 32) < 16, pos % 64
        inv = np.exp((np.float32(-math.log(theta)) * np.arange(half, dtype=np.float32)) / np.float32(half)).astype(np.float32)
        ang = (pv.astype(np.float32) * inv[i]).astype(np.float32).astype(np.float64)
        cos[p] = np.cos(ang).astype(np.float32)
        sn = np.sin(ang).astype(np.float32)
        sin[p] = -sn if first else sn
        partner = p + half if first else p - half
        perm[partner, p] = 1.0
    return cos, sin, perm


def _cmask_tiles():
    out = np.empty((20, 128, 512), np.float32)
    kl = np.arange(128)[:, None]
    ql = np.arange(512)[None, :]
    for di in range(20):
        dlt = di - 8
        d = 128 * dlt + kl - ql
        mult = np.zeros(d.shape, np.int64)
        for dil in (1, 4, 16):
            mult += ((d % dil) == 0) & (np.abs(d) <= 64 * dil)
        with np.errstate(divide="ignore"):
            out[di] = np.where(mult > 0, np.log(np.maximum(mult, 1)).astype(np.float32), np.float32(NEG))
    return out


def _na_tiles(rpb, S):
    L = rpb.shape[0]
    rows = S // 64
    T = S // 512
    start = np.clip(np.arange(rows) - 4, 0, rows - 8)
    c = np.arange(64)
    cs = np.clip(c - 8, 0, 48)
    col_in = (c[None, :] >= cs[:, None]) & (c[None, :] < cs[:, None] + 16)
    dc = np.clip(c[None, :] - c[:, None] + 15, 0, 30)
    out = np.full((L, 4, 3, 8, 128, 512), NEG, np.float32)
    tl = [0, 1 if T > 2 else 0, T - 1]
    kk = np.arange(128)
    qq = np.arange(512)
    for vi, t in enumerate(tl):
        for j in range(8):
            kb = 4 * t - 2 + j
            if kb < 0 or kb >= S // 128:
                continue
            krow = 2 * kb + kk // 64
            kcol = kk % 64
            qrow = 8 * t + qq // 64
            qcol = qq % 64
            st = start[qrow]
            vrow = (krow[:, None] >= st[None, :]) & (krow[:, None] < st[None, :] + 8)
            dr = np.clip(krow[:, None] - qrow[None, :] + 7, 0, 14)
            valid = vrow & col_in[qcol[None, :], kcol[:, None]]
            dcc = dc[qcol[None, :], kcol[:, None]]
            vals = rpb[:, :, dr, dcc]
            out[:, :, vi, j] = np.where(valid[None, None], vals, np.float32(NEG))
    return out


def _pack_pp(norm_attn, norm_mlp, conv_w, conv_b, qk_norm, diff_subln, norm_final):
    L = norm_attn.shape[0]
    pp = np.zeros((128, L * PPL + 8), np.float32)
    p64 = np.arange(128) % 64
    for l in range(L):
        o = l * PPL
        pp[:, o:o + 8] = norm_attn[l].reshape(8, 128).T
        pp[:, o + 8:o + 16] = norm_mlp[l].reshape(8, 128).T
        for j in range(3):
            pp[:, o + 16 + j * 44:o + 16 + (j + 1) * 44] = conv_w[l, j].reshape(44, 128).T
        pp[:, o + 148:o + 192] = conv_b[l].reshape(44, 128).T
        pp[:, o + 192] = qk_norm[l, 0][p64]
        pp[:, o + 193] = qk_norm[l, 1][p64]
        pp[:, o + 194] = diff_subln[l][p64]
    pp[:, L * PPL:L * PPL + 8] = norm_final.reshape(8, 128).T
    return pp


def build_program(S, NSEQ, L=2, debug=False, stop_after=None):
    NT = S // 512
    NB = S // 128
    nc = bass.Bass("TRN2", target_bir_lowering=False)
    pg = Prog(nc)

    def din(name, shape, dt=F32):
        return nc.dram_tensor(name, list(shape), dt, kind="ExternalInput").ap()

    def dscr(name, shape, dt):
        return nc.dram_tensor(name, list(shape), dt, kind=("ExternalOutput" if debug else "Internal")).ap()

    x_in = din("x", [NSEQ, S, D])
    w_in = din("w_in", [L, D, INC])
    w_branch = din("w_branch", [L, 4, 256, D])
    w_out = din("w_out", [L, D, D])
    w_up = din("w_up", [L, D, 2 * DFF])
    w_down = din("w_down", [L, DFF, D])
    pp_in = din("pp", [128, L * PPL + 8])
    dl_in = din("dl", [L, 128])
    nab_in = din("nab", [L, 4, 3, 8, 128, 512])
    cm_in = din("cmask", [20, 128, 512])
    rope_in = {k: (din("rope%s_c" % k, [128, S]), din("rope%s_s" % k, [128, S])) for k in "ACD"}
    perm_in = din("perms", [128, 3, 128])
    ident_in = din("ident", [128, 128])
    y_out = nc.dram_tensor("y", [NSEQ, S, D], F32, kind="ExternalOutput").ap()

    xTa = dscr("xTa", [KC, 128, S], F32)
    xTb = dscr("xTb", [KC, 128, S], F32)
    hT = dscr("hT", [KC, 128, S], BF16)
    oT = dscr("oT", [KC, 128, S], BF16)
    H_x = Hd("x_in")
    H_y = Hd("y")
    H_xTa, H_xTb, H_hT, H_oT = Hd("xTa"), Hd("xTb"), Hd("hT"), Hd("oT")
    H_w = Hd("weights")
    xTa_v = xTa.rearrange("k p s -> p k s")
    xTb_v = xTb.rearrange("k p s -> p k s")
    hT_v = hT.rearrange("k p s -> p k s")
    oT_v = oT.rearrange("k p s -> p k s")

    ident = nc.alloc_sbuf_tensor("sb_ident", [128, 128], F32)
    onesf = nc.alloc_sbuf_tensor("sb_onesf", [128, 128], F32)
    blk64 = nc.alloc_sbuf_tensor("sb_blk64", [128, 128], F32)
    perms = nc.alloc_sbuf_tensor("sb_perms", [128, 3, 128], BF16)
    pp = nc.alloc_sbuf_tensor("sb_pp", [128, L * PPL + 8], F32)
    dlr = nc.alloc_sbuf_tensor("sb_dlr", [1, L * 128], F32)
    lamw = nc.alloc_sbuf_tensor("sb_lamw", [1, 80], F32)
    neglam = nc.alloc_sbuf_tensor("sb_neglam", [64, 2 * L], F32)
    gsub = nc.alloc_sbuf_tensor("sb_gsub", [64, L], F32)
    epsb = nc.alloc_sbuf_tensor("sb_epsb", [128, 1], F32)
    dummy = nc.alloc_sbuf_tensor("sb_dummy", [128, 8], F32)
    H_c = Hd("consts")
    ps = nc.alloc_psum_tensor("psum_all", [128, 8, 512], F32)
    BK = [Hd("bank%d" % i) for i in range(8)]

    base = (nc.sbuf_base + 31) // 32 * 32
    top = nc.sbuf_top
    slab = nc.alloc_sbuf_tensor("sb_slab", [128, top - base - 64], U8)
    ar = Arena(nc, base, base + top - base - 64)

    def dma(eng, out, in_, reads, writes, semh):
        return pg.op(eng, lambda e: e.dma_start(out=out, in_=in_), reads, writes, dma=semh)

    def act(out, in_, func, reads, writes, scale=None, bias=None):
        kw = {}
        if scale is not None:
            kw["scale"] = scale
        if bias is not None:
            kw["bias"] = bias
        return pg.op("act", lambda e: e.activation(out=out, in_=in_, func=func, **kw), reads, writes)

    def tt(eng, out, in0, in1, op, reads, writes):
        return pg.op(eng, lambda e: e.tensor_tensor(out=out, in0=in0, in1=in1, op=op), reads, writes)

    def stt(out, in0, scalar, in1, op0, op1, reads, writes):
        return pg.op("dve", lambda e: e.scalar_tensor_tensor(out=out, in0=in0, scalar=scalar, in1=in1, op0=op0, op1=op1),
                     reads, writes)

    def ts(eng, out, in0, s1, s2, op0, op1, reads, writes):
        if op1 is None:
            return pg.op(eng, lambda e: e.tensor_scalar(out=out, in0=in0, scalar1=s1, scalar2=None, op0=op0), reads, writes)
        return pg.op(eng, lambda e: e.tensor_scalar(out=out, in0=in0, scalar1=s1, scalar2=s2, op0=op0, op1=op1), reads, writes)

    def recip(out, in_, reads, writes):
        return pg.op("dve", lambda e: e.reciprocal(out=out, in_=in_), reads, writes)

    def mmgroup(items, reads, writes):
        def fn(e):
            ins = None
            for (o_, l_, r_, st, sp_, tp) in items:
                if tp is None:
                    ins = e.matmul(o_, lhsT=l_, rhs=r_, start=st, stop=sp_)
                else:
                    ins = e.matmul(o_, lhsT=l_, rhs=r_, start=st, stop=sp_, tile_position=tp)
            return ins
        return pg.op("pe", fn, reads, writes)

    dma("sp", ident[:], ident_in, [H_w], [H_c], H_c)
    dma("sp", pp[:], pp_in, [H_w], [H_c], H_c)
    dma("sp", dlr[:], dl_in.rearrange("(o l) n -> o (l n)", o=1), [H_w], [H_c], H_c)
    dma("pool", perms[:], perm_in, [H_w], [H_c], H_c)
    H_c2 = Hd("consts2")
    pg.op("pool", lambda e: e.memset(onesf[:], 1.0), [], [H_c2])
    pg.op("pool", lambda e: e.memset(blk64[:], 0.0), [], [H_c2])
    pg.op("pool", lambda e: e.memset(blk64[0:64, 0:64], 1.0), [], [H_c2])
    pg.op("pool", lambda e: e.memset(blk64[64:128, 64:128], 1.0), [], [H_c2])
    pg.op("pool", lambda e: e.memset(epsb[:], EPS), [], [H_c2])
    CR = [H_c, H_c2]
    H_lam = Hd("lam")
    for l in range(L):
        lam_init = 0.8 - 0.6 * math.exp(-0.3 * l)
        o = l * 128
        tt("dve", lamw[0:1, 0:32], dlr[0:1, o:o + 32], dlr[0:1, o + 32:o + 64], ALU.mult, CR, [H_lam])
        tt("dve", lamw[0:1, 32:64], dlr[0:1, o + 64:o + 96], dlr[0:1, o + 96:o + 128], ALU.mult, CR + [H_lam], [H_lam])
        pg.op("dve", lambda e: e.tensor_reduce(out=lamw[0:1, 64:66], in_=lamw[0:1, 0:64].rearrange("o (a b) -> o a b", a=2),
                                               axis=AX.X, op=ALU.add), [H_lam], [H_lam])
        act(lamw[0:1, 66:68], lamw[0:1, 64:66], AF.Exp, [H_lam], [H_lam])
        for c in range(2):
            stt(lamw[0:1, 68 + c:69 + c], lamw[0:1, 67:68], -lam_init, lamw[0:1, 66:67], ALU.add, ALU.subtract, [H_lam], [H_lam])
        mmgroup([(ps[0:64, 7, 0:2], onesf[0:1, 0:64], lamw[0:1, 68:70], True, True, None)], [H_lam, H_c2], [BK[7]])
        pg.op("dve", lambda e, l=l: e.tensor_copy(out=neglam[:, 2 * l:2 * l + 2], in_=ps[0:64, 7, 0:2]), [BK[7]], [H_lam])
        ts("dve", gsub[:, l:l + 1], pp[0:64, l * PPL + 194:l * PPL + 195], 1.0 - lam_init, None, ALU.mult, None, CR + [H_lam], [H_lam])
    CR = CR + [H_lam]

    def rms_tile(xt, H_xt, ncols, gcol, out_bf, H_out, A, tagbank):
        sq = A["sq"]
        H_sq = A["H_sq"]
        rstd = A["rstd"]
        H_rstd = A["H_rstd"]
        bank = tagbank
        items = []
        for kc in range(KC):
            s = sq[kc % 2]
            hs = H_sq[kc % 2]
            act(s[:, 0:ncols], xt[:, kc, 0:ncols], AF.Square, [H_xt], [hs])
            mmgroup([(ps[:, bank, 0:ncols], onesf[:, :], s[:, 0:ncols], kc == 0, kc == KC - 1, None)], [hs, H_c2], [BK[bank]])
        act(rstd[:, 0:ncols], ps[:, bank, 0:ncols], AF.Sqrt, [BK[bank]] + CR, [H_rstd], scale=1.0 / D, bias=epsb[:, 0:1])
        recip(rstd[:, 0:ncols], rstd[:, 0:ncols], [H_rstd], [H_rstd])
        for kc in range(KC):
            stt(out_bf[:, kc, 0:ncols], xt[:, kc, 0:ncols], pp[:, gcol + kc:gcol + kc + 1], rstd[:, 0:ncols],
                ALU.mult, ALU.mult, [H_xt, H_rstd] + CR, [H_out])

    def phase_T(seq):
        ar.reset()
        xtok = [ar.alloc([128, 4, D], F32, "xtok") for _ in range(2)]
        Hk = [Hd("xtok%d" % i) for i in range(2)]
        xt = [ar.alloc([128, KC, 512], F32, "xt") for _ in range(2)]
        Hx = [Hd("xtT%d" % i) for i in range(2)]
        for t in range(NT):
            b = t % 2
            dma("sp", xtok[b][:], x_in[seq, t * 512:(t + 1) * 512, :].rearrange("(b p) d -> p b d", p=128), [H_x], [Hk[b]], Hk[b])
            for kc in range(KC):
                bank = kc % 4
                def fn(e, kc=kc, bank=bank, b=b):
                    ins = None
                    for blk in range(4):
                        ins = e.transpose(ps[:, bank, blk * 128:(blk + 1) * 128], xtok[b][:, blk, kc * 128:(kc + 1) * 128], ident[:, :])
                    return ins
                pg.op("pe", fn, [Hk[b]] + CR, [BK[bank]])
                if kc % 2 == 0:
                    act(xt[b][:, kc, :], ps[:, bank, :], AF.Copy, [BK[bank]], [Hx[b]])
                else:
                    pg.op("dve", lambda e, kc=kc, bank=bank, b=b: e.tensor_copy(out=xt[b][:, kc, :], in_=ps[:, bank, :]), [BK[bank]], [Hx[b]])
            dma("pool", xTa_v[:, :, t * 512:(t + 1) * 512], xt[b][:], [Hx[b]], [H_xTa], Hx[b])
        pg.barrier()

    def phase_Z(seq, src_v, H_src):
        ar.reset()
        A = {"sq": [ar.alloc([128, 512], F32, "sq") for _ in range(2)], "H_sq": [Hd("sq0"), Hd("sq1")],
             "rstd": ar.alloc([128, 512], F32, "rstd"), "H_rstd": Hd("rstd")}
        xt = [ar.alloc([128, KC, 512], F32, "xt") for _ in range(2)]
        Hx = [Hd("zx%d" % i) for i in range(2)]
        yn = ar.alloc([128, KC, 512], F32, "yn")
        H_yn = Hd("yn")
        ytok = [ar.alloc([128, 4, D], F32, "ytok") for _ in range(2)]
        Hyt = [Hd("ytok%d" % i) for i in range(2)]
        gcol = L * PPL
        for t in range(NT):
            b = t % 2
            dma("sp", xt[b][:], src_v[:, :, t * 512:(t + 1) * 512], [H_src], [Hx[b]], Hx[b])
            sq, H_sq, rstd, H_rstd = A["sq"], A["H_sq"], A["rstd"], A["H_rstd"]
            for kc in range(KC):
                act(sq[kc % 2][:], xt[b][:, kc, :], AF.Square, [Hx[b]], [H_sq[kc % 2]])
                mmgroup([(ps[:, 4, :], onesf[:, :], sq[kc % 2][:], kc == 0, kc == KC - 1, None)], [H_sq[kc % 2], H_c2], [BK[4]])
            act(rstd[:], ps[:, 4, :], AF.Sqrt, [BK[4]] + CR, [H_rstd], scale=1.0 / D, bias=epsb[:, 0:1])
            recip(rstd[:], rstd[:], [H_rstd], [H_rstd])
            for kc in range(KC):
                stt(yn[:, kc, :], xt[b][:, kc, :], pp[:, gcol + kc:gcol + kc + 1], rstd[:], ALU.mult, ALU.mult,
                    [Hx[b], H_rstd] + CR, [H_yn])
            for blk in range(4):
                for half in range(2):
                    bank = (blk * 2 + half) % 4
                    def fn(e, blk=blk, half=half, bank=bank):
                        ins = None
                        for q in range(4):
                            kc = half * 4 + q
                            ins = e.transpose(ps[:, bank, q * 128:(q + 1) * 128], yn[:, kc, blk * 128:(blk + 1) * 128], ident[:, :])
                        return ins
                    pg.op("pe", fn, [H_yn] + CR, [BK[bank]])
                    if half == 0:
                        act(ytok[b][:, blk, 0:512], ps[:, bank, :], AF.Copy, [BK[bank]], [Hyt[b]])
                    else:
                        pg.op("dve", lambda e, blk=blk, bank=bank, b=b: e.tensor_copy(out=ytok[b][:, blk, 512:1024], in_=ps[:, bank, :]),
                              [BK[bank]], [Hyt[b]])
            dma("pool", y_out[seq, t * 512:(t + 1) * 512, :].rearrange("(b p) d -> p b d", p=128), ytok[b][:], [Hyt[b]], [H_y], Hyt[b])
        pg.barrier()

    def phase_0(l, src_v, H_src):
        ar.reset()
        A = {"sq": [ar.alloc([128, 512], F32, "sq") for _ in range(2)], "H_sq": [Hd("sq0"), Hd("sq1")],
             "rstd": ar.alloc([128, 512], F32, "rstd"), "H_rstd": Hd("rstd")}
        xt = [ar.alloc([128, KC, 512], F32, "xt") for _ in range(2)]
        Hx = [Hd("p0x%d" % i) for i in range(2)]
        hb = [ar.alloc([128, KC, 512], BF16, "hb") for _ in range(2)]
        Hh = [Hd("p0h%d" % i) for i in range(2)]
        for t in range(NT):
            b = t % 2
            dma("sp", xt[b][:], src_v[:, :, t * 512:(t + 1) * 512], [H_src], [Hx[b]], Hx[b])
            rms_tile(xt[b], Hx[b], 512, l * PPL + 0, hb[b], Hh[b], A, 4 + (t % 2))
            dma("pool", hT_v[:, :, t * 512:(t + 1) * 512], hb[b][:], [Hh[b]], [H_hT], Hh[b])
        pg.barrier()

    def branch_pass(l, br):
        ar.reset()
        kind = "ABCD"[br]
        HV = 2 if kind == "D" else 4
        RQ = ar.alloc([128, 2, S], BF16, "RQ")
        RK = ar.alloc([128, 2, S], BF16, "RK")
        RV = ar.alloc([128, NB, HV, 128], BF16, "RV")
        mark = ar.cur
        H_RQ = [Hd("RQ%d" % t) for t in range(NT)]
        H_RK = [Hd("RK%d" % t) for t in range(NT)]
        H_RV = [Hd("RV%d" % t) for t in range(NT)]
        H_RVo = Hd("RVones")
        if kind == "D":
            c0 = 9 * 256
            ncol = 256 + 128 + 128
        else:
            c0 = br * 768
            ncol = 768
        W = ar.alloc([128, KC, ncol], BF16, "W")
        H_W = Hd("W")
        for kc in range(KC):
            dma("pool", W[:, kc, :], w_in[l, kc * 128:(kc + 1) * 128, c0:c0 + ncol], [H_w], [H_W], H_W)
        FL = set(os.environ.get("FLAGS", "").split(","))
        if "nomem" not in FL:
            pg.op("pool", lambda e: e.memset(RV[:, :, :, 64:128], 1.0), [], [H_RVo])
        ht = [ar.alloc([128, KC, 512], BF16, "ht") for _ in range(2)]
        Hht = [Hd("ht%d" % i) for i in range(2)]
        rot = kind in "ACD"
        if rot:
            cs_t = [ar.alloc([128, 2, 512], F32, "cs") for _ in range(2)]
            Hcs = [Hd("cs%d" % i) for i in range(2)]
            a16 = [ar.alloc([128, 512], BF16, "a16") for _ in range(2)]
            Ha16 = [Hd("a16_%d" % i) for i in range(2)]
            t1 = [ar.alloc([128, 512], F32, "t1") for _ in range(2)]
            Ht1 = [Hd("t1_%d" % i) for i in range(2)]
            t2 = [ar.alloc([128, 512], F32, "t2") for _ in range(2)]
            Ht2 = [Hd("t2_%d" % i) for i in range(2)]
            pidx = "ACD".index(kind)
        if kind == "D":
            sqd = [ar.alloc([128, 512], F32, "sqd") for _ in range(2)]
            Hsqd = [Hd("sqd%d" % i) for i in range(2)]
            rsd = [ar.alloc([128, 512], F32, "rsd") for _ in range(2)]
            Hrsd = [Hd("rsd%d" % i) for i in range(2)]

        if kind == "D":
            fm = [("q", RQ, 0, [(0, 128)]), ("q", RQ, 1, [(128, 128)]),
                  ("k", RK, 0, [(256, 128)]), ("k", RK, 1, [(320, 64), (256, 64)])]
            vcol = 384
            vw = 128
        else:
            fm = [("q", RQ, 0, [(0, 128)]), ("q", RQ, 1, [(128, 128)]),
                  ("k", RK, 0, [(256, 128)]), ("k", RK, 1, [(384, 128)])]
            vcol = 512
            vw = 256
        cnt = 0
        for t in range(NT):
            b = t % 2
            tsl = slice(t * 512, (t + 1) * 512)
            dma("sp", ht[b][:], hT_v[:, :, tsl], [H_hT], [Hht[b]], Hht[b])
            if rot:
                dma("sp", cs_t[b][:, 0, :], rope_in[kind][0][:, tsl], [H_w], [Hcs[b]], Hcs[b])
                dma("sp", cs_t[b][:, 1, :], rope_in[kind][1][:, tsl], [H_w], [Hcs[b]], Hcs[b])
            for (qk, dst, dch, pieces) in fm:
                bank = cnt % 2
                pb = 2 + cnt % 2
                sb_ = 4 + cnt % 2
                u = cnt % 2
                cnt += 1
                Hdst = (H_RQ if qk == "q" else H_RK)[t]
                items = []
                for kc in range(KC):
                    mo = 0
                    for (pc0, pw) in pieces:
                        items.append((ps[mo:mo + pw, bank, :], W[:, kc, pc0:pc0 + pw], ht[b][:, kc, :], kc == 0, kc == KC - 1, None))
                        mo += pw
                if len(pieces) == 1:
                    mmgroup(items, [H_W, Hht[b]], [BK[bank]])
                else:
                    i0 = [it for i, it in enumerate(items) if i % 2 == 0]
                    i1 = [it for i, it in enumerate(items) if i % 2 == 1]
                    mmgroup(i0 + i1, [H_W, Hht[b]], [BK[bank]])
                dsl = dst[:, dch, tsl]
                if not rot or "norot" in FL:
                    if cnt % 2 == 0:
                        act(dsl, ps[:, bank, :], AF.Copy, [BK[bank]], [Hdst])
                    else:
                        pg.op("dve", lambda e, dsl=dsl, bank=bank: e.tensor_copy(out=dsl, in_=ps[:, bank, :]), [BK[bank]], [Hdst])
                    continue
                if kind == "D":
                    gcol = l * PPL + (192 if qk == "q" else 193)
                    gap = pp[:, gcol:gcol + 1]
                    act(a16[u][:], ps[:, bank, :], AF.Identity, [BK[bank]] + CR, [Ha16[u]], scale=gap)
                    act(sqd[u][:], ps[:, bank, :], AF.Square, [BK[bank]], [Hsqd[u]])
                    mmgroup([(ps[:, sb_, :], blk64[:, :], sqd[u][:], True, True, None)], [Hsqd[u], H_c2], [BK[sb_]])
                    act(rsd[u][:], ps[:, sb_, :], AF.Sqrt, [BK[sb_]] + CR, [Hrsd[u]], scale=1.0 / 64, bias=epsb[:, 0:1])
                    recip(rsd[u][:], rsd[u][:], [Hrsd[u]], [Hrsd[u]])
                    stt(t1[u][:], ps[:, bank, :], gap, cs_t[b][:, 0, :], ALU.mult, ALU.mult, [BK[bank], Hcs[b]] + CR, [Ht1[u]])
                else:
                    act(a16[u][:], ps[:, bank, :], AF.Copy, [BK[bank]], [Ha16[u]])
                    if "rotA" in FL:
                        pg.op("dve", lambda e, dsl=dsl, u=u: e.tensor_copy(out=dsl, in_=a16[u][:]), [Ha16[u]], [Hdst])
                        continue
                    if "rotA2" in FL:
                        pg.op("dve", lambda e, u=u, bank=bank: e.tensor_copy(out=t1[u][:], in_=ps[:, bank, :]), [BK[bank]] + ([Ha16[u]] if "ser" in FL else []), [Ht1[u]])
                    else:
                        tt("dve", t1[u][:], ps[:, bank, :], cs_t[b][:, 0, :], ALU.mult, [BK[bank], Hcs[b]], [Ht1[u]])
                if "rotB" in FL:
                    if "rotBact" in FL:
                        act(dsl, t1[u][:], AF.Copy, [Ht1[u]], [Hdst])
                    else:
                        pg.op("dve", lambda e, dsl=dsl, u=u: e.tensor_copy(out=dsl, in_=t1[u][:]), [Ht1[u]], [Hdst])
                    continue
                mmgroup([(ps[:, pb, :], perms[:, pidx, :], a16[u][:], True, True, None)], [Ha16[u]] + CR, [BK[pb]])
                if "rotC" in FL:
                    pg.op("dve", lambda e, dsl=dsl, pb=pb: e.tensor_copy(out=dsl, in_=ps[:, pb, :]), [BK[pb]], [Hdst])
                    continue
                tt("dve", t2[u][:], ps[:, pb, :], cs_t[b][:, 1, :], ALU.mult, [BK[pb], Hcs[b]], [Ht2[u]])
                if kind == "D":
                    tt("pool", t1[u][:], t1[u][:], t2[u][:], ALU.add, [Ht1[u], Ht2[u]], [Ht1[u]])
                    tt("dve", dsl, t1[u][:], rsd[u][:], ALU.mult, [Ht1[u], Hrsd[u]], [Hdst])
                else:
                    tt("pool" if "pooladd" in FL else "dve", dsl, t1[u][:], t2[u][:], ALU.add, [Ht1[u], Ht2[u]], [Hdst])
            for blk in range(4 if "nov" not in FL else 0):
                bank = 6 + blk % 2
                items = [(ps[:, bank, 0:vw], ht[b][:, kc, blk * 128:(blk + 1) * 128], W[:, kc, vcol:vcol + vw], kc == 0, kc == KC - 1, None)
                         for kc in range(KC)]
                mmgroup(items, [H_W, Hht[b]], [BK[bank]])
                gb = t * 4 + blk
                src = ps[:, bank, 0:vw].rearrange("p (h d) -> p h d", h=HV)
                if blk % 2 == 0:
                    act(RV[:, gb, :, 0:64], src, AF.Copy, [BK[bank]], [H_RV[t]])
                else:
                    pg.op("dve", lambda e, gb=gb, src=src: e.tensor_copy(out=RV[:, gb, :, 0:64], in_=src), [BK[bank]], [H_RV[t]])

        if os.environ.get("SUBSTOP") == "proj":
            pg.barrier()
            return
        pg.barrier()
        ar.cur = mark
        PT = [ar.alloc([128, 1024], BF16, "PT") for _ in range(3)]
        HPT = [Hd("PT%d" % i) for i in range(3)]
        fsb = [ar.alloc([65, 512], F32, "fsb") for _ in range(4)]
        Hfsb = [Hd("fsb%d" % i) for i in range(4)]
        rr = fsb
        Hrr = Hfsb
        ost = [ar.alloc([64, 512], BF16, "ost") for _ in range(4)]
        Host = [Hd("ost%d" % i) for i in range(4)]
        if kind in "AD":
            KP = ar.alloc([128, 2, S], BF16, "KP")
            H_KP = Hd("KP")
        if kind == "A":
            o12 = [ar.alloc([64, 512], F32, "o12") for _ in range(4)]
            Ho12 = [Hd("o12_%d" % i) for i in range(4)]
            dd = [ar.alloc([64, 512], F32, "dd") for _ in range(2)]
            Hdd = [Hd("dd%d" % i) for i in range(2)]
            sqa = [ar.alloc([64, 512], F32, "sqa") for _ in range(2)]
            Hsqa = [Hd("sqa%d" % i) for i in range(2)]
        if kind in "BC":
            sbx = [ar.alloc([128, 1024], F32, "sbx") for _ in range(2)]
            Hsbx = [Hd("sbx%d" % i) for i in range(2)]
        if kind == "B":
            bsl = [ar.alloc([128, 2, 512], F32, "bsl") for _ in range(3)]
            Hbsl = [Hd("bsl%d" % i) for i in range(3)]
            bres = ar.alloc([128, 8, 2, 512], F32, "bres")
            H_bres = Hd("bres")
        if kind == "C":
            cm = ar.alloc([128, 20, 512], F32, "cm")
            H_cm = Hd("cm")
            for g in range(4):
                dma("sp", cm[:, g * 5:(g + 1) * 5, :], cm_in[g * 5:(g + 1) * 5].rearrange("j p n -> p j n"), [H_w], [H_cm], H_cm)
        allK = H_RK
        allV = H_RV + [H_RVo]
        scale = {"A": 32 ** -0.5, "B": 0.125, "C": 0.125, "D": 0.125}[kind]
        state = {"u": 0, "pt": 0, "sp": 0, "f": 0, "bs": 0, "bl": 0}

        def finalize(accset, unit_infos, t, bcb):
            tsl = slice(t * 512, (t + 1) * 512)
            if kind != "A":
                for i in range(2):
                    bank = accset[i]
                    f = state["f"] % 4
                    state["f"] += 1
                    pg.op("dve", lambda e, f=f, bank=bank: e.tensor_copy(out=fsb[f][0:65, :], in_=ps[0:65, bank, :]), [BK[bank]], [Hfsb[f]])
                    recip(rr[f][64:65, :], fsb[f][64:65, :], [Hfsb[f]], [Hrr[f]])
                    bb = bcb[i]
                    mmgroup([(ps[0:64, bb, :], onesf[64:65, 0:64], rr[f][64:65, :], True, True, None)], [Hrr[f], H_c2], [BK[bb]])
                    tt("dve", ost[f][:], fsb[f][0:64, :], ps[0:64, bb, :], ALU.mult, [Hfsb[f], BK[bb]], [Host[f]])
                    ch, pb_ = unit_infos[i]
                    dma("pool", oT_v[pb_:pb_ + 64, ch, tsl], ost[f][:], [Host[f]], [H_oT], Host[f])
                return
            fs = []
            for i in range(2):
                bank = accset[i]
                f = state["f"] % 4
                state["f"] += 1
                fs.append(f)
                pg.op("dve", lambda e, f=f, bank=bank: e.tensor_copy(out=fsb[f][0:65, :], in_=ps[0:65, bank, :]), [BK[bank]], [Hfsb[f]])
                recip(rr[f][64:65, :], fsb[f][64:65, :], [Hfsb[f]], [Hrr[f]])
                bb = bcb[i]
                mmgroup([(ps[0:64, bb, :], onesf[64:65, 0:64], rr[f][64:65, :], True, True, None)], [Hrr[f], H_c2], [BK[bb]])
                tt("dve", o12[f][:], fsb[f][0:64, :], ps[0:64, bb, :], ALU.mult, [Hfsb[f], BK[bb]], [Ho12[f]])
            u = (state["f"] // 2) % 2
            stt(dd[u][:], o12[fs[1]][:], neglam[:, 2 * l:2 * l + 1], o12[fs[0]][:], ALU.mult, ALU.add,
                [Ho12[fs[0]], Ho12[fs[1]]] + CR, [Hdd[u]])
            tt("pool", sqa[u][:], dd[u][:], dd[u][:], ALU.mult, [Hdd[u]], [Hsqa[u]])
            bank = bcb[0]
            mmgroup([(ps[0:64, bank, :], onesf[0:64, 0:64], sqa[u][:], True, True, None)], [Hsqa[u], H_c2], [BK[bank]])
            act(sqa[u][:], ps[0:64, bank, :], AF.Sqrt, [BK[bank]] + CR, [Hsqa[u]], scale=1.0 / 64, bias=epsb[0:64, 0:1])
            recip(sqa[u][:], sqa[u][:], [Hsqa[u]], [Hsqa[u]])
            f = fs[0]
            stt(ost[f][:], dd[u][:], gsub[:, l:l + 1], sqa[u][:], ALU.mult, ALU.mult, [Hdd[u], Hsqa[u]] + CR, [Host[f]])
            ch, pb_ = unit_infos[0]
            dma("pool", oT_v[pb_:pb_ + 64, ch, tsl], ost[f][:], [Host[f]], [H_oT], Host[f])

        def attn_unit(t, qk_items, v_aps, kbs, bias_fn, unit_infos, bias_pre=None, kh=None):
            accset = (6, 7)
            state["u"] += 1
            n = len(kbs)

            def emit_qk(j):
                kb = kbs[j]
                if bias_pre is not None:
                    bias_pre(kb)
                sp_ = state["sp"] % 3
                state["sp"] += 1
                banks = (2 * sp_, 2 * sp_ + 1)
                items = []
                for i in range(2):
                    lhsT, rhs, tp = qk_items[i](kb)
                    items.append((ps[:, banks[i], :], lhsT, rhs, True, True, tp))
                mmgroup(items, [H_RQ[t]] + ([H_RK[kb // 4]] if kh is None else kh), [BK[banks[0]], BK[banks[1]]])
                return banks

            pendq = [emit_qk(0)]
            if n > 1:
                pendq.append(emit_qk(1))
            for j in range(n):
                kb = kbs[j]
                banks = pendq.pop(0)
                p = state["pt"] % 3
                state["pt"] += 1
                if bias_fn is None:
                    act(PT[p][:], ps[:, banks[0]:banks[0] + 2, :], AF.Exp, [BK[banks[0]], BK[banks[1]]], [HPT[p]], scale=scale)
                else:
                    sx = state["bs"] % 2
                    for i in range(2):
                        bap, bh = bias_fn(i, kb)
                        stt(sbx[sx][:, i * 512:(i + 1) * 512], ps[:, banks[i], :], scale, bap, ALU.mult, ALU.add,
                            [BK[banks[i]], bh], [Hsbx[sx]])
                    state["bs"] += 1
                    act(PT[p][:], sbx[sx][:], AF.Exp, [Hsbx[sx]], [HPT[p]])
                if j + 2 < n:
                    pendq.append(emit_qk(j + 2))
                items = []
                for i in range(2):
                    items.append((ps[:, accset[i], :], v_aps[i](kb), PT[p][:, i * 512:(i + 1) * 512], j == 0, j == n - 1, None))
                mmgroup(items, [HPT[p], H_RV[kb // 4], H_RVo], [BK[accset[0]], BK[accset[1]]])
            if os.environ.get("SUBSTOP") != "nofin":
                fb = (state["sp"] + 2) % 3
                finalize(accset, unit_infos, t, (2 * fb, 2 * fb + 1))

        if kind == "A":
            order = [(t, h) for h in range(4) for t in range(NT)]
        elif kind == "D":
            order = [(t, g) for g in range(2) for t in range(NT)]
        elif kind == "B":
            order = [(t, hp) for hp in range(2) for t in range(NT)]
        else:
            order = [(t, None) for t in range(NT)]
        for (t, hp_sel) in order:
            if os.environ.get("SUBSTOP") in ("unit1", "nofin") and t > 0:
                break
            tsl = slice(t * 512, (t + 1) * 512)
            if kind == "B" and t == 0 and NT > 2:
                for j in range(8):
                    dma("sp", bres[:, j, :, :], nab_in[l, 2 * hp_sel:2 * hp_sel + 2, 1, j].rearrange("h p n -> p h n"), [H_w], [H_bres], H_bres)
            if kind == "A":
                for h in [hp_sel]:
                    ch, pb_ = h // 2, (h % 2) * 64
                    if t == 0:
                        pg.op("pool", lambda e: e.memset(KP[:], 0.0), [], [H_KP])
                        for c in range(2):
                            p0 = pb_ + 32 * c
                            pg.op("pool", lambda e, p0=p0, c=c, ch=ch: e.tensor_copy(out=KP[p0:p0 + 32, c, :], in_=RK[p0:p0 + 32, ch, :]),
                                  list(H_RK), [H_KP])
                    def mk(i, ch=ch):
                        return lambda kb: (KP[:, i, kb * 128:(kb + 1) * 128], RQ[:, ch, tsl], None)
                    va = lambda kb, h=h: RV[:, kb, h, :]
                    attn_unit(t, [mk(0), mk(1)], [va, va], list(range(NB)), None, [(0 * 2 + ch, pb_), None], kh=[H_KP])
            elif kind == "D":
                for g in [hp_sel]:
                    if t == 0:
                        pg.op("pool", lambda e: e.memset(KP[:], 0.0), [], [H_KP])
                        for i in range(2):
                            pb_ = 64 * i
                            kch = (0 if g == 0 else 1) if i == 0 else (1 if g == 0 else 0)
                            pg.op("pool", lambda e, pb_=pb_, i=i, kch=kch: e.tensor_copy(out=KP[pb_:pb_ + 64, i, :], in_=RK[pb_:pb_ + 64, kch, :]),
                                  list(H_RK), [H_KP])
                    def mk(i, g=g):
                        return lambda kb: (KP[:, i, kb * 128:(kb + 1) * 128], RQ[:, g, tsl], None)
                    va = lambda kb, g=g: RV[:, kb, g, :]
                    attn_unit(t, [mk(0), mk(1)], [va, va], list(range(NB)), None, [(6 + g, 0), (6 + g, 64)], kh=[H_KP])
            else:
                for hp in ([hp_sel] if hp_sel is not None else range(2)):
                    def mk(i, hp=hp):
                        pb_ = 64 * i
                        return lambda kb: (RK[pb_:pb_ + 64, hp, kb * 128:(kb + 1) * 128], RQ[pb_:pb_ + 64, hp, tsl], None)
                    vas = [(lambda kb, h=2 * hp + i: RV[:, kb, h, :]) for i in range(2)]
                    if kind == "B":
                        kbs = [kb for kb in range(4 * t - 2, 4 * t + 6) if 0 <= kb < NB]
                        var = 0 if t == 0 else (2 if t == NT - 1 else 1)
                        cache = {}
                        def bias_pre(kb, hp=hp, t=t, var=var, cache=cache):
                            s_ = state["bl"] % 3
                            state["bl"] += 1
                            j = kb - (4 * t - 2)
                            dma("sp", bsl[s_][:], nab_in[l, 2 * hp:2 * hp + 2, var, j].rearrange("h p n -> p h n"), [H_w], [Hbsl[s_]], Hbsl[s_])
                            cache[kb] = s_
                        def bias_fn(i, kb, cache=cache):
                            s_ = cache[kb]
                            return bsl[s_][:, i, :], Hbsl[s_]
                        if var == 1:
                            bias_pre = None
                            def bias_fn(i, kb, t=t):
                                return bres[:, kb - (4 * t - 2), i, :], H_bres
                    else:
                        kbs = [kb for kb in range(4 * t - 8, 4 * t + 12) if 0 <= kb < NB]
                        bias_pre = None
                        def bias_fn(i, kb, t=t):
                            return cm[:, kb - 4 * t + 8, :], H_cm
                    br_ch = 2 * br + hp
                    attn_unit(t, [mk(0), mk(1)], vas, kbs, bias_fn, [(br_ch, 0), (br_ch, 64)], bias_pre)
        pg.barrier()

    def merge_pass(l, src_v, H_src, dst_v, H_dst):
        ar.reset()
        Wg = ar.alloc([128, KC, 4096], BF16, "Wg")
        Wbr = ar.alloc([128, 8, D], BF16, "Wbr")
        Wo = ar.alloc([128, KC, D], BF16, "Wo")
        H_Wm = Hd("Wm")
        for kc in range(KC):
            dma("pool", Wg[:, kc, :], w_in[l, kc * 128:(kc + 1) * 128, 2816:6912], [H_w], [H_Wm], H_Wm)
        for bq in range(4):
            dma("pool", Wbr[:, 2 * bq:2 * bq + 2, :], w_branch[l, bq].rearrange("(k p) n -> p k n", p=128), [H_w], [H_Wm], H_Wm)
        dma("pool", Wo[:], w_out[l].rearrange("(k p) n -> p k n", p=128), [H_w], [H_Wm], H_Wm)
        xt = [ar.alloc([128, KC, 512], F32, "xt") for _ in range(2)]
        Hx = [Hd("mx%d" % i) for i in range(2)]
        ht = [ar.alloc([128, KC, 512], BF16, "ht") for _ in range(2)]
        Hht = [Hd("mh%d" % i) for i in range(2)]
        ot = [ar.alloc([128, KC, 512], BF16, "ot") for _ in range(2)]
        Hot = [Hd("mo%d" % i) for i in range(2)]
        mg = ar.alloc([128, KC, 512], BF16, "mg")
        H_mg = Hd("mg")
        sg = [ar.alloc([128, 512], F32, "sg") for _ in range(2)]
        Hsg = [Hd("sg%d" % i) for i in range(2)]
        acc = [ar.alloc([128, 512], F32, "macc") for _ in range(2)]
        Hacc = [Hd("macc%d" % i) for i in range(2)]
        tmp = [ar.alloc([128, 512], F32, "mtmp") for _ in range(2)]
        Htmp = [Hd("mtmp%d" % i) for i in range(2)]
        cnt = 0
        for t in range(NT):
            b = t % 2
            tsl = slice(t * 512, (t + 1) * 512)
            dma("sp", ht[b][:], hT_v[:, :, tsl], [H_hT], [Hht[b]], Hht[b])
            dma("sp", ot[b][:], oT_v[:, :, tsl], [H_oT], [Hot[b]], Hot[b])
            dma("sp", xt[b][:], src_v[:, :, tsl], [H_src], [Hx[b]], Hx[b])
            for oc in range(KC):
                a = oc % 2
                for bq in range(4):
                    gb = cnt % 2
                    mb = 2 + cnt % 2
                    u = cnt % 2
                    cnt += 1
                    col = bq * D + oc * 128
                    mmgroup([(ps[:, gb, :], Wg[:, kc, col:col + 128], ht[b][:, kc, :], kc == 0, kc == KC - 1, None) for kc in range(KC)],
                            [H_Wm, Hht[b]], [BK[gb]])
                    mmgroup([(ps[:, mb, :], Wbr[:, 2 * bq + j, oc * 128:(oc + 1) * 128], ot[b][:, 2 * bq + j, :], j == 0, j == 1, None)
                             for j in range(2)], [H_Wm, Hot[b]], [BK[mb]])
                    act(sg[u][:], ps[:, gb, :], AF.Sigmoid, [BK[gb]], [Hsg[u]])
                    if bq == 0:
                        tt("dve", acc[a][:], sg[u][:], ps[:, mb, :], ALU.mult, [Hsg[u], BK[mb]], [Hacc[a]])
                    elif bq < 3:
                        tt("dve", tmp[u][:], sg[u][:], ps[:, mb, :], ALU.mult, [Hsg[u], BK[mb]], [Htmp[u]])
                        tt("pool", acc[a][:], acc[a][:], tmp[u][:], ALU.add, [Hacc[a], Htmp[u]], [Hacc[a]])
                    else:
                        tt("dve", tmp[u][:], sg[u][:], ps[:, mb, :], ALU.mult, [Hsg[u], BK[mb]], [Htmp[u]])
                        tt("pool", mg[:, oc, :], acc[a][:], tmp[u][:], ALU.add, [Hacc[a], Htmp[u]], [H_mg])
            for oc in range(KC):
                bank = 4 + oc % 4
                mmgroup([(ps[:, bank, :], Wo[:, kc, oc * 128:(oc + 1) * 128], mg[:, kc, :], kc == 0, kc == KC - 1, None) for kc in range(KC)],
                        [H_Wm, H_mg], [BK[bank]])
                tt("dve", xt[b][:, oc, :], xt[b][:, oc, :], ps[:, bank, :], ALU.add, [Hx[b], BK[bank]], [Hx[b]])
            dma("pool", dst_v[:, :, tsl], xt[b][:], [Hx[b]], [H_dst], Hx[b])
        pg.barrier()

    def mlp_pass(l, src_v, H_src, dst_v, H_dst):
        ar.reset()
        Wup = ar.alloc([128, KC, 2 * DFF], BF16, "Wup")
        Wdn = ar.alloc([128, NFC, D], BF16, "Wdn")
        H_Wf = Hd("Wf")
        for kc in range(KC):
            dma("pool", Wup[:, kc, :], w_up[l, kc * 128:(kc + 1) * 128, :], [H_w], [H_Wf], H_Wf)
        for j in range(0, NFC, 2):
            dma("pool", Wdn[:, j:j + 2, :], w_down[l, j * 128:(j + 2) * 128, :].rearrange("(k p) n -> p k n", p=128), [H_w], [H_Wf], H_Wf)
        A = {"sq": [ar.alloc([128, 512], F32, "sq") for _ in range(2)], "H_sq": [Hd("sq0"), Hd("sq1")],
             "rstd": ar.alloc([128, 512], F32, "rstd"), "H_rstd": Hd("rstd")}
        xt = [ar.alloc([128, KC, 512], F32, "xt") for _ in range(1)]
        Hx = [Hd("fx%d" % i) for i in range(1)]
        h2 = ar.alloc([128, KC, 512], BF16, "h2")
        H_h2 = Hd("h2")
        gT = ar.alloc([128, NFC, 512], BF16, "gT")
        H_gT = Hd("gT")
        cv = [ar.alloc([128, 512], F32, "cv") for _ in range(2)]
        Hcv = [Hd("cv%d" % i) for i in range(2)]
        cg = [ar.alloc([128, 512], F32, "cg") for _ in range(2)]
        Hcg = [Hd("cg%d" % i) for i in range(2)]
        ntile = (S + 509) // 510
        po = l * PPL
        cnt = 0
        for i in range(ntile):
            b = 0
            c0 = 510 * i
            nv = min(510, S - c0)
            lo = max(c0 - 1, 0)
            hi = min(c0 + nv + 1, S)
            off = lo - (c0 - 1)
            nl = hi - lo
            dma("sp", xt[b][:, :, off:off + nl], src_v[:, :, lo:hi], [H_src], [Hx[b]], Hx[b])
            ncols = off + nl
            if off > 0:
                pg.op("pool", lambda e: e.memset(xt[0][:, :, 0:1], 0.0), [Hx[b]], [Hx[b]])
            rms_tile(xt[b], Hx[b], ncols, po + 8, h2, H_h2, A, 6)
            if off > 0:
                pg.op("pool", lambda e: e.memset(h2[:, :, 0:1], 0.0), [H_h2], [H_h2])
            if ncols < 512:
                pg.op("pool", lambda e, ncols=ncols: e.memset(h2[:, :, ncols:512], 0.0), [H_h2], [H_h2])
            if off > 0:
                pass
            for j in range(NFC):
                u = cnt % 2
                cnt += 1
                bv = 0 + u * 2
                bg = 1 + u * 2
                mmgroup([(ps[:, bv, :], Wup[:, kc, j * 128:(j + 1) * 128], h2[:, kc, :], kc == 0, kc == KC - 1, None) for kc in range(KC)],
                        [H_Wf, H_h2], [BK[bv]])
                mmgroup([(ps[:, bg, :], Wup[:, kc, DFF + j * 128:DFF + (j + 1) * 128], h2[:, kc, :], kc == 0, kc == KC - 1, None) for kc in range(KC)],
                        [H_Wf, H_h2], [BK[bg]])
                for (bank, cbuf, Hc, chn) in ((bv, cv[u], Hcv[u], j), (bg, cg[u], Hcg[u], NFC + j)):
                    w0 = pp[:, po + 16 + 0 * 44 + chn:po + 16 + 0 * 44 + chn + 1]
                    w1 = pp[:, po + 16 + 1 * 44 + chn:po + 16 + 1 * 44 + chn + 1]
                    w2 = pp[:, po + 16 + 2 * 44 + chn:po + 16 + 2 * 44 + chn + 1]
                    bb = pp[:, po + 148 + chn:po + 148 + chn + 1]
                    act(cbuf[:, 0:510], ps[:, bank, 1:511], AF.Identity, [BK[bank]] + CR, [Hc], scale=w1, bias=bb)
                    stt(cbuf[:, 0:510], ps[:, bank, 0:510], w0, cbuf[:, 0:510], ALU.mult, ALU.add, [BK[bank], Hc] + CR, [Hc])
                    stt(cbuf[:, 0:510], ps[:, bank, 2:512], w2, cbuf[:, 0:510], ALU.mult, ALU.add, [BK[bank], Hc] + CR, [Hc])
                act(cg[u][:, 0:510], cg[u][:, 0:510], AF.Gelu, [Hcg[u]], [Hcg[u]])
                tt("pool", gT[:, j, 0:510], cg[u][:, 0:510], cv[u][:, 0:510], ALU.mult, [Hcg[u], Hcv[u]], [H_gT])
            for oc in range(KC):
                bank = 4 + oc % 2
                mmgroup([(ps[:, bank, 0:510], Wdn[:, j, oc * 128:(oc + 1) * 128], gT[:, j, 0:510], j == 0, j == NFC - 1, None) for j in range(NFC)],
                        [H_Wf, H_gT], [BK[bank]])
                tt("dve", xt[b][:, oc, 1:1 + nv], xt[b][:, oc, 1:1 + nv], ps[:, bank, 0:nv], ALU.add, [Hx[b], BK[bank]], [Hx[b]])
            dma("pool", dst_v[:, :, c0:c0 + nv], xt[b][:, :, 1:1 + nv], [Hx[b]], [H_dst], Hx[b])
        pg.barrier()

    stages = []
    for seq in range(NSEQ):
        stages.append(("T", lambda seq=seq: phase_T(seq)))
        for l in range(L):
            stages.append(("P0", lambda l=l: phase_0(l, xTa_v, H_xTa)))
            for br in range(4):
                stages.append(("BR%d" % br, lambda l=l, br=br: branch_pass(l, br)))
            stages.append(("MG", lambda l=l: merge_pass(l, xTa_v, H_xTa, xTb_v, H_xTb)))
            stages.append(("FF", lambda l=l: mlp_pass(l, xTb_v, H_xTb, xTa_v, H_xTa)))
        stages.append(("Z", lambda seq=seq: phase_Z(seq, xTa_v, H_xTa)))
    for i, (nm, fn) in enumerate(stages):
        if stop_after is not None and i >= stop_after:
            break
        fn()
    pg.barrier()
    pg.op("pool", lambda e: e.memset(dummy[:], 0.0), [], [Hd("dummy")])
    pg.emit()
    return nc


_CACHE = {}


def _host_consts(S, inputs):
    c = {}
    perms = np.zeros((128, 3, 128), np.float32)
    for i, k in enumerate("ACD"):
        cs, sn, pm = _rope_tables(k, S)
        c["rope%s_c" % k] = cs
        c["rope%s_s" % k] = sn
        perms[:, i, :] = pm
    c["perms"] = perms
    c["ident"] = np.eye(128, dtype=np.float32)
    c["cmask"] = _cmask_tiles()
    c["nab"] = _na_tiles(np.asarray(inputs["na_rpb"], np.float32), S)
    c["pp"] = _pack_pp(*[np.asarray(inputs[k], np.float32) for k in
                         ("norm_attn", "norm_mlp", "conv_w", "conv_b", "qk_norm", "diff_subln", "norm_final")])
    c["dl"] = np.ascontiguousarray(np.asarray(inputs["diff_lambda"], np.float32).reshape(-1, 128))
    for k in ("w_in", "w_branch", "w_out", "w_up", "w_down"):
        c[k] = np.ascontiguousarray(np.asarray(inputs[k], np.float32))
    return c


def run_sequences(xs, inputs, n_cores=N_CORES, nseq=None, **bk):
    n, S, _ = xs.shape
    if nseq is None:
        nseq = (n + n_cores - 1) // n_cores
    L = np.asarray(inputs["w_in"]).shape[0]
    key = (S, nseq, L, tuple(sorted(bk.items())))
    if key not in _CACHE:
        _CACHE[key] = build_program(S, nseq, L, **bk)
    nc = _CACHE[key]
    consts = _host_consts(S, inputs)
    slots = [[(c + s * n_cores) if (c + s * n_cores) < n else (c % n) for s in range(nseq)] for c in range(n_cores)]
    in_maps = []
    for c in range(n_cores):
        m = dict(consts)
        m["x"] = np.ascontiguousarray(xs[slots[c]])
        in_maps.append(m)
    res = run_bass_kernel_spmd(nc, in_maps, core_ids=list(range(n_cores)))
    out = np.empty_like(xs)
    for c in range(n_cores):
        for s in range(nseq):
            i = c + s * n_cores
            if i < n:
                out[i] = res.results[c]["y"][s]
    return out, res


def kernel(x_prompt, x_sample, norm_attn, w_in, diff_lambda, diff_subln, na_rpb, qk_norm, w_branch,
           w_out, norm_mlp, w_up, conv_w, conv_b, w_down, norm_final):
    inputs = dict(norm_attn=norm_attn, w_in=w_in, diff_lambda=diff_lambda, diff_subln=diff_subln, na_rpb=na_rpb,
                  qk_norm=qk_norm, w_branch=w_branch, w_out=w_out, norm_mlp=norm_mlp, w_up=w_up, conv_w=conv_w,
                  conv_b=conv_b, w_down=w_down, norm_final=norm_final)
    xp = np.asarray(x_prompt, np.float32)
    xs_ = np.asarray(x_sample, np.float32)
    xs = np.concatenate([xs_, xp], axis=0)
    out, _ = run_sequences(xs, inputs)
    nsamp = xs_.shape[0]
    return (np.ascontiguousarray(out[nsamp:]), np.ascontiguousarray(out[:nsamp]))
```

```python
import math
import os
import numpy as np
import concourse.bass as bass
import concourse.mybir as mybir
from concourse.bass_utils import run_bass_kernel_spmd

F32 = mybir.dt.float32
BF16 = mybir.dt.bfloat16
U8 = mybir.dt.uint8
AF = mybir.ActivationFunctionType
ALU = mybir.AluOpType
AX = mybir.AxisListType

D = 1024
KC = 8
DFF = 2816
NFC = 22
INC = 6912
EPS = 1e-6
NEG = -30000.0
N_CORES = 8
ROPE_THETA = 500000.0
AXIAL_THETA = 10000.0
PPL = 196
SELF_SYNC = os.environ.get('NOSELF') is None


class Hd:
    __slots__ = ("name", "w", "r", "dsem", "dcnt")

    def __init__(self, name):
        self.name = name
        self.w = []
        self.r = []
        self.dsem = None
        self.dcnt = 0


class Op:
    __slots__ = ("eng", "fn", "deps", "signal", "ticket", "dma", "sem", "idx")


class Prog:
    CE = ("pe", "act", "dve", "pool")

    def __init__(self, nc):
        self.nc = nc
        self.ops = {e: [] for e in ("pe", "act", "dve", "pool", "sp")}
        self.esem = {e: nc.alloc_semaphore("s_" + e) for e in self.CE}
        self.pending = {e: [] for e in self.ops}
        self.dma_sems = {}
        self.named = {}
        self.nops = 0

    @staticmethod
    def _key(o):
        return o.eng if o.dma is None else ("d", id(o.sem))

    def _push(self, lst, o):
        k = self._key(o)
        for i, p in enumerate(lst):
            if self._key(p) == k:
                lst[i] = o
                return
        lst.append(o)

    def op(self, eng, fn, reads=(), writes=(), dma=None):
        o = Op()
        o.eng = eng
        o.fn = fn
        o.signal = False
        o.ticket = None
        o.dma = dma
        o.sem = None
        o.idx = self.nops
        self.nops += 1
        deps = list(self.pending[eng])
        self.pending[eng] = []
        for h in reads:
            deps.extend(h.w)
            if h.name.startswith("bank"):
                deps.extend(r for r in h.r if r.eng != eng)
        for h in writes:
            if not (dma is not None and h.w and all(p.dma is not None for p in h.w) and not h.r):
                deps.extend(h.w)
            deps.extend(h.r)
        if dma is not None:
            nk = (dma.name, eng)
            if nk not in self.named:
                sem_ = self.nc.alloc_semaphore("d_%s_%s_%d" % (dma.name, eng, len(self.dma_sems)))
                self.named[nk] = [sem_, 0]
                self.dma_sems[id(sem_)] = [sem_, 0]
            ent = self.named[nk]
            ent[1] += 1
            o.sem = ent[0]
            o.ticket = 16 * ent[1]
            self.dma_sems[id(o.sem)][1] = o.ticket
        fd = []
        for p in deps:
            if p is o:
                continue
            if p.dma is not None:
                fd.append(p)
            elif p.eng == eng:
                if eng != "pe" and SELF_SYNC:
                    p.signal = True
                    fd.append(p)
            else:
                p.signal = True
                fd.append(p)
        o.deps = fd
        for h in reads:
            if h not in writes:
                self._push(h.r, o)
        for h in writes:
            if dma is not None and h.w and all(p.dma is not None for p in h.w) and not h.r:
                self._push(h.w, o)
            else:
                h.w = [o]
            h.r = []
        self.ops[eng].append(o)
        return o

    def barrier(self):
        lasts = []
        for e in self.CE:
            for p in reversed(self.ops[e]):
                if p.dma is None:
                    lasts.append(p)
                    break
        dm = []
        for sid, (sem, tot) in self.dma_sems.items():
            if tot > 0:
                f = Op()
                f.eng = "dma"
                f.dma = True
                f.sem = sem
                f.ticket = tot
                f.signal = False
                dm.append(f)
        for e in self.ops:
            for p in lasts:
                if p.eng != e or (e != "pe" and SELF_SYNC):
                    p.signal = True
                    self.pending[e].append(p)
            self.pending[e].extend(dm)

    def emit(self):
        nc = self.nc
        for e in self.CE:
            c = 0
            for o in self.ops[e]:
                if o.signal and o.dma is None:
                    c += 1
                    o.ticket = c
        esem = self.esem

        def mk(ename):
            def body(eng):
                waited = {}
                for o in self.ops[ename]:
                    for p in o.deps:
                        sem = p.sem if p.dma is not None else esem[p.eng]
                        v = p.ticket
                        k = id(sem)
                        if waited.get(k, 0) < v:
                            eng.wait_ge(sem, v)
                            waited[k] = v
                    if os.environ.get("DUMP"):
                        print("OP", ename, o.idx, "dma" if o.dma is not None else "", "sig=%s" % o.ticket if (o.signal or o.dma is not None) else "",
                              "waits:", [((p.sem.name if p.dma is not None else p.eng), p.ticket) for p in o.deps])
                    ins = o.fn(eng)
                    if o.dma is not None:
                        ins.then_inc(o.sem, 16)
                    elif o.signal:
                        ins.then_inc(esem[ename], 1)
            return body

        with nc.Block() as block:
            block.sync(mk("sp"))
            block.tensor(mk("pe"))
            block.scalar(mk("act"))
            block.vector(mk("dve"))
            block.gpsimd(mk("pool"))


class Arena:
    def __init__(self, nc, base, top):
        self.nc = nc
        self.base = base
        self.top = top
        self.cur = base
        self.n = 0

    def reset(self):
        self.cur = self.base

    def alloc(self, shape, dtype, name="t"):
        esz = {F32: 4, BF16: 2, U8: 1}[dtype]
        nb = esz
        for s in shape[1:]:
            nb *= s
        nb = (nb + 31) // 32 * 32
        off = self.cur
        assert off + nb <= self.top, f"SBUF arena overflow {name}: {off + nb} > {self.top}"
        self.cur += nb
        self.n += 1
        return self.nc.alloc_sbuf_tensor_at(f"{name}_{self.n}", list(shape), dtype, offset=off)


def _rope_tables(kind, S):
    pos = np.arange(S)
    cos = np.ones((128, S), np.float32)
    sin = np.zeros((128, S), np.float32)
    perm = np.zeros((128, 128), np.float32)
    for p in range(128):
        if kind == "A":
            j = p % 32
            if j >= 8:
                continue
            half, i, first, theta, pv = 4, j % 4, j < 4, ROPE_THETA, pos
        elif kind == "C":
            j = p % 64
            if j >= 16:
                continue
            half, i, first, theta, pv = 8, j % 8, j < 8, ROPE_THETA, pos
        else:
            j = p % 64
            half, theta = 16, AXIAL_THETA
            if j < 32:
                i, first, pv = j % 16, j < 16, pos // 64
            else:
                i, first, pv = (j - 32) % 16, (j - 32) < 16, pos % 64
        inv = np.exp((np.float32(-math.log(theta)) * np.arange(half, dtype=np.float32)) / np.float32(half)).astype(np.float32)
        ang = (pv.astype(np.float32) * inv[i]).astype(np.float32).astype(np.float64)
        cos[p] = np.cos(ang).astype(np.float32)
        sn = np.sin(ang).astype(np.float32)
        sin[p] = -sn if first else sn
        partner = p + half if first else p - half
        perm[partner, p] = 1.0
    return cos, sin, perm


def _cmask_tiles():
    out = np.empty((20, 128, 512), np.float32)
    kl = np.arange(128)[:, None]
    ql = np.arange(512)[None, :]
    for di in range(20):
        dlt = di - 8
        d = 128 * dlt + kl - ql
        mult = np.zeros(d.shape, np.int64)
        for dil in (1, 4, 16):
            mult += ((d % dil) == 0) & (np.abs(d) <= 64 * dil)
        with np.errstate(divide="ignore"):
            out[di] = np.where(mult > 0, np.log(np.maximum(mult, 1)).astype(np.float32), np.float32(NEG))
    return out


def _na_tiles(rpb, S):
    L = rpb.shape[0]
    rows = S // 64
    T = S // 512
    start = np.clip(np.arange(rows) - 4, 0, rows - 8)
    c = np.arange(64)
    cs = np.clip(c - 8, 0, 48)
    col_in = (c[None, :] >= cs[:, None]) & (c[None, :] < cs[:, None] + 16)
    dc = np.clip(c[None, :] - c[:, None] + 15, 0, 30)
    out = np.full((L, 4, 3, 8, 128, 512), NEG, np.float32)
    tl = [0, 1 if T > 2 else 0, T - 1]
    kk = np.arange(128)
    qq = np.arange(512)
    for vi, t in enumerate(tl):
        for j in range(8):
            kb = 4 * t - 2 + j
            if kb < 0 or kb >= S // 128:
                continue
            krow = 2 * kb + kk // 64
            kcol = kk % 64
            qrow = 8 * t + qq // 64
            qcol = qq % 64
            st = start[qrow]
            vrow = (krow[:, None] >= st[None, :]) & (krow[:, None] < st[None, :] + 8)
            dr = np.clip(krow[:, None] - qrow[None, :] + 7, 0, 14)
            valid = vrow & col_in[qcol[None, :], kcol[:, None]]
            dcc = dc[qcol[None, :], kcol[:, None]]
            vals = rpb[:, :, dr, dcc]
            out[:, :, vi, j] = np.where(valid[None, None], vals, np.float32(NEG))
    return out


def _pack_pp(norm_attn, norm_mlp, conv_w, conv_b, qk_norm, diff_subln, norm_final):
    L = norm_attn.shape[0]
    pp = np.zeros((128, L * PPL + 8), np.float32)
    p64 = np.arange(128) % 64
    for l in range(L):
        o = l * PPL
        pp[:, o:o + 8] = norm_attn[l].reshape(8, 128).T
        pp[:, o + 8:o + 16] = norm_mlp[l].reshape(8, 128).T
        for j in range(3):
            pp[:, o + 16 + j * 44:o + 16 + (j + 1) * 44] = conv_w[l, j].reshape(44, 128).T
        pp[:, o + 148:o + 192] = conv_b[l].reshape(44, 128).T
        pp[:, o + 192] = qk_norm[l, 0][p64]
        pp[:, o + 193] = qk_norm[l, 1][p64]
        pp[:, o + 194] = diff_subln[l][p64]
    pp[:, L * PPL:L * PPL + 8] = norm_final.reshape(8, 128).T
    return pp


def build_program(S, NSEQ, L=2, debug=False, stop_after=None):
    NT = S // 512
    NB = S // 128
    nc = bass.Bass("TRN2", target_bir_lowering=False)
    pg = Prog(nc)

    def din(name, shape, dt=F32):
        return nc.dram_tensor(name, list(shape), dt, kind="ExternalInput").ap()

    def dscr(name, shape, dt):
        return nc.dram_tensor(name, list(shape), dt, kind=("ExternalOutput" if debug else "Internal")).ap()

    x_in = din("x", [NSEQ, S, D])
    w_in = din("w_in", [L, D, INC])
    w_branch = din("w_branch", [L, 4, 256, D])
    w_out = din("w_out", [L, D, D])
    w_up = din("w_up", [L, D, 2 * DFF])
    w_down = din("w_down", [L, DFF, D])
    pp_in = din("pp", [128, L * PPL + 8])
    dl_in = din("dl", [L, 128])
    nab_in = din("nab", [L, 4, 3, 8, 128, 512])
    cm_in = din("cmask", [20, 128, 512])
    rope_in = {k: (din("rope%s_c" % k, [128, S]), din("rope%s_s" % k, [128, S])) for k in "ACD"}
    perm_in = din("perms", [128, 3, 128])
    ident_in = din("ident", [128, 128])
    y_out = nc.dram_tensor("y", [NSEQ, S, D], F32, kind="ExternalOutput").ap()

    xTa = dscr("xTa", [KC, 128, S], F32)
    xTb = dscr("xTb", [KC, 128, S], F32)
    hT = dscr("hT", [KC, 128, S], BF16)
    oT = dscr("oT", [KC, 128, S], BF16)
    H_x = Hd("x_in")
    H_y = Hd("y")
    H_xTa, H_xTb, H_hT, H_oT = Hd("xTa"), Hd("xTb"), Hd("hT"), Hd("oT")
    H_w = Hd("weights")
    xTa_v = xTa.rearrange("k p s -> p k s")
    xTb_v = xTb.rearrange("k p s -> p k s")
    hT_v = hT.rearrange("k p s -> p k s")
    oT_v = oT.rearrange("k p s -> p k s")

    ident = nc.alloc_sbuf_tensor("sb_ident", [128, 128], F32)
    onesf = nc.alloc_sbuf_tensor("sb_onesf", [128, 128], F32)
    blk64 = nc.alloc_sbuf_tensor("sb_blk64", [128, 128], F32)
    perms = nc.alloc_sbuf_tensor("sb_perms", [128, 3, 128], BF16)
    pp = nc.alloc_sbuf_tensor("sb_pp", [128, L * PPL + 8], F32)
    dlr = nc.alloc_sbuf_tensor("sb_dlr", [1, L * 128], F32)
    lamw = nc.alloc_sbuf_tensor("sb_lamw", [1, 80], F32)
    neglam = nc.alloc_sbuf_tensor("sb_neglam", [64, 2 * L], F32)
    gsub = nc.alloc_sbuf_tensor("sb_gsub", [64, L], F32)
    epsb = nc.alloc_sbuf_tensor("sb_epsb", [128, 1], F32)
    dummy = nc.alloc_sbuf_tensor("sb_dummy", [128, 8], F32)
    H_c = Hd("consts")
    ps = nc.alloc_psum_tensor("psum_all", [128, 8, 512], F32)
    BK = [Hd("bank%d" % i) for i in range(8)]

    base = (nc.sbuf_base + 31) // 32 * 32
    top = nc.sbuf_top
    slab = nc.alloc_sbuf_tensor("sb_slab", [128, top - base - 64], U8)
    ar = Arena(nc, base, base + top - base - 64)

    def dma(eng, out, in_, reads, writes, semh):
        return pg.op(eng, lambda e: e.dma_start(out=out, in_=in_), reads, writes, dma=semh)

    def act(out, in_, func, reads, writes, scale=None, bias=None):
        kw = {}
        if scale is not None:
            kw["scale"] = scale
        if bias is not None:
            kw["bias"] = bias
        return pg.op("act", lambda e: e.activation(out=out, in_=in_, func=func, **kw), reads, writes)

    def tt(eng, out, in0, in1, op, reads, writes):
        return pg.op(eng, lambda e: e.tensor_tensor(out=out, in0=in0, in1=in1, op=op), reads, writes)

    def stt(out, in0, scalar, in1, op0, op1, reads, writes):
        return pg.op("dve", lambda e: e.scalar_tensor_tensor(out=out, in0=in0, scalar=scalar, in1=in1, op0=op0, op1=op1),
                     reads, writes)

    def ts(eng, out, in0, s1, s2, op0, op1, reads, writes):
        if op1 is None:
            return pg.op(eng, lambda e: e.tensor_scalar(out=out, in0=in0, scalar1=s1, scalar2=None, op0=op0), reads, writes)
        return pg.op(eng, lambda e: e.tensor_scalar(out=out, in0=in0, scalar1=s1, scalar2=s2, op0=op0, op1=op1), reads, writes)

    def recip(out, in_, reads, writes):
        return pg.op("dve", lambda e: e.reciprocal(out=out, in_=in_), reads, writes)

    def mmgroup(items, reads, writes):
        def fn(e):
            ins = None
            for (o_, l_, r_, st, sp_, tp) in items:
                if tp is None:
                    ins = e.matmul(o_, lhsT=l_, rhs=r_, start=st, stop=sp_)
                else:
                    ins = e.matmul(o_, lhsT=l_, rhs=r_, start=st, stop=sp_, tile_position=tp)
            return ins
        return pg.op("pe", fn, reads, writes)

    dma("sp", ident[:], ident_in, [H_w], [H_c], H_c)
    dma("sp", pp[:], pp_in, [H_w], [H_c], H_c)
    dma("sp", dlr[:], dl_in.rearrange("(o l) n -> o (l n)", o=1), [H_w], [H_c], H_c)
    dma("pool", perms[:], perm_in, [H_w], [H_c], H_c)
    H_c2 = Hd("consts2")
    pg.op("pool", lambda e: e.memset(onesf[:], 1.0), [], [H_c2])
    pg.op("pool", lambda e: e.memset(blk64[:], 0.0), [], [H_c2])
    pg.op("pool", lambda e: e.memset(blk64[0:64, 0:64], 1.0), [], [H_c2])
    pg.op("pool", lambda e: e.memset(blk64[64:128, 64:128], 1.0), [], [H_c2])
    pg.op("pool", lambda e: e.memset(epsb[:], EPS), [], [H_c2])
    CR = [H_c, H_c2]
    H_lam = Hd("lam")
    for l in range(L):
        lam_init = 0.8 - 0.6 * math.exp(-0.3 * l)
        o = l * 128
        tt("dve", lamw[0:1, 0:32], dlr[0:1, o:o + 32], dlr[0:1, o + 32:o + 64], ALU.mult, CR, [H_lam])
        tt("dve", lamw[0:1, 32:64], dlr[0:1, o + 64:o + 96], dlr[0:1, o + 96:o + 128], ALU.mult, CR + [H_lam], [H_lam])
        pg.op("dve", lambda e: e.tensor_reduce(out=lamw[0:1, 64:66], in_=lamw[0:1, 0:64].rearrange("o (a b) -> o a b", a=2),
                                               axis=AX.X, op=ALU.add), [H_lam], [H_lam])
        act(lamw[0:1, 66:68], lamw[0:1, 64:66], AF.Exp, [H_lam], [H_lam])
        for c in range(2):
            stt(lamw[0:1, 68 + c:69 + c], lamw[0:1, 67:68], -lam_init, lamw[0:1, 66:67], ALU.add, ALU.subtract, [H_lam], [H_lam])
        mmgroup([(ps[0:64, 7, 0:2], onesf[0:1, 0:64], lamw[0:1, 68:70], True, True, None)], [H_lam, H_c2], [BK[7]])
        pg.op("dve", lambda e, l=l: e.tensor_copy(out=neglam[:, 2 * l:2 * l + 2], in_=ps[0:64, 7, 0:2]), [BK[7]], [H_lam])
        ts("dve", gsub[:, l:l + 1], pp[0:64, l * PPL + 194:l * PPL + 195], 1.0 - lam_init, None, ALU.mult, None, CR + [H_lam], [H_lam])
    CR = CR + [H_lam]

    def rms_tile(xt, H_xt, ncols, gcol, out_bf, H_out, A, tagbank):
        sq = A["sq"]
        H_sq = A["H_sq"]
        rstd = A["rstd"]
        H_rstd = A["H_rstd"]
        bank = tagbank
        items = []
        for kc in range(KC):
            s = sq[kc % 2]
            hs = H_sq[kc % 2]
            act(s[:, 0:ncols], xt[:, kc, 0:ncols], AF.Square, [H_xt], [hs])
            mmgroup([(ps[:, bank, 0:ncols], onesf[:, :], s[:, 0:ncols], kc == 0, kc == KC - 1, None)], [hs, H_c2], [BK[bank]])
        act(rstd[:, 0:ncols], ps[:, bank, 0:ncols], AF.Sqrt, [BK[bank]] + CR, [H_rstd], scale=1.0 / D, bias=epsb[:, 0:1])
        recip(rstd[:, 0:ncols], rstd[:, 0:ncols], [H_rstd], [H_rstd])
        for kc in range(KC):
            stt(out_bf[:, kc, 0:ncols], xt[:, kc, 0:ncols], pp[:, gcol + kc:gcol + kc + 1], rstd[:, 0:ncols],
                ALU.mult, ALU.mult, [H_xt, H_rstd] + CR, [H_out])

    def phase_T(seq):
        ar.reset()
        xtok = [ar.alloc([128, 4, D], F32, "xtok") for _ in range(2)]
        Hk = [Hd("xtok%d" % i) for i in range(2)]
        xt = [ar.alloc([128, KC, 512], F32, "xt") for _ in range(2)]
        Hx = [Hd("xtT%d" % i) for i in range(2)]
        for t in range(NT):
            b = t % 2
            dma("sp", xtok[b][:], x_in[seq, t * 512:(t + 1) * 512, :].rearrange("(b p) d -> p b d", p=128), [H_x], [Hk[b]], Hk[b])
            for kc in range(KC):
                bank = kc % 4
                def fn(e, kc=kc, bank=bank, b=b):
                    ins = None
                    for blk in range(4):
                        ins = e.transpose(ps[:, bank, blk * 128:(blk + 1) * 128], xtok[b][:, blk, kc * 128:(kc + 1) * 128], ident[:, :])
                    return ins
                pg.op("pe", fn, [Hk[b]] + CR, [BK[bank]])
                if kc % 2 == 0:
                    act(xt[b][:, kc, :], ps[:, bank, :], AF.Copy, [BK[bank]], [Hx[b]])
                else:
                    pg.op("dve", lambda e, kc=kc, bank=bank, b=b: e.tensor_copy(out=xt[b][:, kc, :], in_=ps[:, bank, :]), [BK[bank]], [Hx[b]])
            dma("pool", xTa_v[:, :, t * 512:(t + 1) * 512], xt[b][:], [Hx[b]], [H_xTa], Hx[b])
        pg.barrier()

    def phase_Z(seq, src_v, H_src):
        ar.reset()
        A = {"sq": [ar.alloc([128, 512], F32, "sq") for _ in range(2)], "H_sq": [Hd("sq0"), Hd("sq1")],
             "rstd": ar.alloc([128, 512], F32, "rstd"), "H_rstd": Hd("rstd")}
        xt = [ar.alloc([128, KC, 512], F32, "xt") for _ in range(2)]
        Hx = [Hd("zx%d" % i) for i in range(2)]
        yn = ar.alloc([128, KC, 512], F32, "yn")
        H_yn = Hd("yn")
        ytok = [ar.alloc([128, 4, D], F32, "ytok") for _ in range(2)]
        Hyt = [Hd("ytok%d" % i) for i in range(2)]
        gcol = L * PPL
        for t in range(NT):
            b = t % 2
            dma("sp", xt[b][:], src_v[:, :, t * 512:(t + 1) * 512], [H_src], [Hx[b]], Hx[b])
            sq, H_sq, rstd, H_rstd = A["sq"], A["H_sq"], A["rstd"], A["H_rstd"]
            for kc in range(KC):
                act(sq[kc % 2][:], xt[b][:, kc, :], AF.Square, [Hx[b]], [H_sq[kc % 2]])
                mmgroup([(ps[:, 4, :], onesf[:, :], sq[kc % 2][:], kc == 0, kc == KC - 1, None)], [H_sq[kc % 2], H_c2], [BK[4]])
            act(rstd[:], ps[:, 4, :], AF.Sqrt, [BK[4]] + CR, [H_rstd], scale=1.0 / D, bias=epsb[:, 0:1])
            recip(rstd[:], rstd[:], [H_rstd], [H_rstd])
            for kc in range(KC):
                stt(yn[:, kc, :], xt[b][:, kc, :], pp[:, gcol + kc:gcol + kc + 1], rstd[:], ALU.mult, ALU.mult,
                    [Hx[b], H_rstd] + CR, [H_yn])
            for blk in range(4):
                for half in range(2):
                    bank = (blk * 2 + half) % 4
                    def fn(e, blk=blk, half=half, bank=bank):
                        ins = None
                        for q in range(4):
                            kc = half * 4 + q
                            ins = e.transpose(ps[:, bank, q * 128:(q + 1) * 128], yn[:, kc, blk * 128:(blk + 1) * 128], ident[:, :])
                        return ins
                    pg.op("pe", fn, [H_yn] + CR, [BK[bank]])
                    if half == 0:
                        act(ytok[b][:, blk, 0:512], ps[:, bank, :], AF.Copy, [BK[bank]], [Hyt[b]])
                    else:
                        pg.op("dve", lambda e, blk=blk, bank=bank, b=b: e.tensor_copy(out=ytok[b][:, blk, 512:1024], in_=ps[:, bank, :]),
                              [BK[bank]], [Hyt[b]])
            dma("pool", y_out[seq, t * 512:(t + 1) * 512, :].rearrange("(b p) d -> p b d", p=128), ytok[b][:], [Hyt[b]], [H_y], Hyt[b])
        pg.barrier()

    def phase_0(l, src_v, H_src):
        ar.reset()
        A = {"sq": [ar.alloc([128, 512], F32, "sq") for _ in range(2)], "H_sq": [Hd("sq0"), Hd("sq1")],
             "rstd": ar.alloc([128, 512], F32, "rstd"), "H_rstd": Hd("rstd")}
        xt = [ar.alloc([128, KC, 512], F32, "xt") for _ in range(2)]
        Hx = [Hd("p0x%d" % i) for i in range(2)]
        hb = [ar.alloc([128, KC, 512], BF16, "hb") for _ in range(2)]
        Hh = [Hd("p0h%d" % i) for i in range(2)]
        for t in range(NT):
            b = t % 2
            dma("sp", xt[b][:], src_v[:, :, t * 512:(t + 1) * 512], [H_src], [Hx[b]], Hx[b])
            rms_tile(xt[b], Hx[b], 512, l * PPL + 0, hb[b], Hh[b], A, 4 + (t % 2))
            dma("pool", hT_v[:, :, t * 512:(t + 1) * 512], hb[b][:], [Hh[b]], [H_hT], Hh[b])
        pg.barrier()

    def branch_pass(l, br):
        ar.reset()
        kind = "ABCD"[br]
        HV = 2 if kind == "D" else 4
        RQ = ar.alloc([128, 2, S], BF16, "RQ")
        RK = ar.alloc([128, 2, S], BF16, "RK")
        RV = ar.alloc([128, NB, HV, 128], BF16, "RV")
        mark = ar.cur
        H_RQ = [Hd("RQ%d" % t) for t in range(NT)]
        H_RK = [Hd("RK%d" % t) for t in range(NT)]
        H_RV = [Hd("RV%d" % t) for t in range(NT)]
        H_RVo = Hd("RVones")
        if kind == "D":
            c0 = 9 * 256
            ncol = 256 + 128 + 128
        else:
            c0 = br * 768
            ncol = 768
        W = ar.alloc([128, KC, ncol], BF16, "W")
        H_W = Hd("W")
        for kc in range(KC):
            dma("pool", W[:, kc, :], w_in[l, kc * 128:(kc + 1) * 128, c0:c0 + ncol], [H_w], [H_W], H_W)
        FL = set(os.environ.get("FLAGS", "").split(","))
        if "nomem" not in FL:
            pg.op("pool", lambda e: e.memset(RV[:, :, :, 64:128], 1.0), [], [H_RVo])
        ht = [ar.alloc([128, KC, 512], BF16, "ht") for _ in range(2)]
        Hht = [Hd("ht%d" % i) for i in range(2)]
        rot = kind in "ACD"
        if rot:
            cs_t = [ar.alloc([128, 2, 512], F32, "cs") for _ in range(2)]
            Hcs = [Hd("cs%d" % i) for i in range(2)]
            a16 = [ar.alloc([128, 512], BF16, "a16") for _ in range(2)]
            Ha16 = [Hd("a16_%d" % i) for i in range(2)]
            t1 = [ar.alloc([128, 512], F32, "t1") for _ in range(2)]
            Ht1 = [Hd("t1_%d" % i) for i in range(2)]
            t2 = [ar.alloc([128, 512], F32, "t2") for _ in range(2)]
            Ht2 = [Hd("t2_%d" % i) for i in range(2)]
            pidx = "ACD".index(kind)
        if kind == "D":
            sqd = [ar.alloc([128, 512], F32, "sqd") for _ in range(2)]
            Hsqd = [Hd("sqd%d" % i) for i in range(2)]
            rsd = [ar.alloc([128, 512], F32, "rsd") for _ in range(2)]
            Hrsd = [Hd("rsd%d" % i) for i in range(2)]

        if kind == "D":
            fm = [("q", RQ, 0, [(0, 128)]), ("q", RQ, 1, [(128, 128)]),
                  ("k", RK, 0, [(256, 128)]), ("k", RK, 1, [(320, 64), (256, 64)])]
            vcol = 384
            vw = 128
        else:
            fm = [("q", RQ, 0, [(0, 128)]), ("q", RQ, 1, [(128, 128)]),
                  ("k", RK, 0, [(256, 128)]), ("k", RK, 1, [(384, 128)])]
            vcol = 512
            vw = 256
        cnt = 0
        for t in range(NT):
            b = t % 2
            tsl = slice(t * 512, (t + 1) * 512)
            dma("sp", ht[b][:], hT_v[:, :, tsl], [H_hT], [Hht[b]], Hht[b])
            if rot:
                dma("sp", cs_t[b][:, 0, :], rope_in[kind][0][:, tsl], [H_w], [Hcs[b]], Hcs[b])
                dma("sp", cs_t[b][:, 1, :], rope_in[kind][1][:, tsl], [H_w], [Hcs[b]], Hcs[b])
            for (qk, dst, dch, pieces) in fm:
                bank = cnt % 2
                pb = 2 + cnt % 2
                sb_ = 4 + cnt % 2
                u = cnt % 2
                cnt += 1
                Hdst = (H_RQ if qk == "q" else H_RK)[t]
                items = []
                for kc in range(KC):
                    mo = 0
                    for (pc0, pw) in pieces:
                        items.append((ps[mo:mo + pw, bank, :], W[:, kc, pc0:pc0 + pw], ht[b][:, kc, :], kc == 0, kc == KC - 1, None))
                        mo += pw
                if len(pieces) == 1:
                    mmgroup(items, [H_W, Hht[b]], [BK[bank]])
                else:
                    i0 = [it for i, it in enumerate(items) if i % 2 == 0]
                    i1 = [it for i, it in enumerate(items) if i % 2 == 1]
                    mmgroup(i0 + i1, [H_W, Hht[b]], [BK[bank]])
                dsl = dst[:, dch, tsl]
                if not rot or "norot" in FL:
                    if cnt % 2 == 0:
                        act(dsl, ps[:, bank, :], AF.Copy, [BK[bank]], [Hdst])
                    else:
                        pg.op("dve", lambda e, dsl=dsl, bank=bank: e.tensor_copy(out=dsl, in_=ps[:, bank, :]), [BK[bank]], [Hdst])
                    continue
                if kind == "D":
                    gcol = l * PPL + (192 if qk == "q" else 193)
                    gap = pp[:, gcol:gcol + 1]
                    act(a16[u][:], ps[:, bank, :], AF.Identity, [BK[bank]] + CR, [Ha16[u]], scale=gap)
                    act(sqd[u][:], ps[:, bank, :], AF.Square, [BK[bank]], [Hsqd[u]])
                    mmgroup([(ps[:, sb_, :], blk64[:, :], sqd[u][:], True, True, None)], [Hsqd[u], H_c2], [BK[sb_]])
                    act(rsd[u][:], ps[:, sb_, :], AF.Sqrt, [BK[sb_]] + CR, [Hrsd[u]], scale=1.0 / 64, bias=epsb[:, 0:1])
                    recip(rsd[u][:], rsd[u][:], [Hrsd[u]], [Hrsd[u]])
                    stt(t1[u][:], ps[:, bank, :], gap, cs_t[b][:, 0, :], ALU.mult, ALU.mult, [BK[bank], Hcs[b]] + CR, [Ht1[u]])
                else:
                    act(a16[u][:], ps[:, bank, :], AF.Copy, [BK[bank]], [Ha16[u]])
                    if "rotA" in FL:
                        pg.op("dve", lambda e, dsl=dsl, u=u: e.tensor_copy(out=dsl, in_=a16[u][:]), [Ha16[u]], [Hdst])
                        continue
                    if "rotA2" in FL:
                        pg.op("dve", lambda e, u=u, bank=bank: e.tensor_copy(out=t1[u][:], in_=ps[:, bank, :]), [BK[bank]] + ([Ha16[u]] if "ser" in FL else []), [Ht1[u]])
                    else:
                        tt("dve", t1[u][:], ps[:, bank, :], cs_t[b][:, 0, :], ALU.mult, [BK[bank], Hcs[b]], [Ht1[u]])
                if "rotB" in FL:
                    if "rotBact" in FL:
                        act(dsl, t1[u][:], AF.Copy, [Ht1[u]], [Hdst])
                    else:
                        pg.op("dve", lambda e, dsl=dsl, u=u: e.tensor_copy(out=dsl, in_=t1[u][:]), [Ht1[u]], [Hdst])
                    continue
                mmgroup([(ps[:, pb, :], perms[:, pidx, :], a16[u][:], True, True, None)], [Ha16[u]] + CR, [BK[pb]])
                if "rotC" in FL:
                    pg.op("dve", lambda e, dsl=dsl, pb=pb: e.tensor_copy(out=dsl, in_=ps[:, pb, :]), [BK[pb]], [Hdst])
                    continue
                tt("dve", t2[u][:], ps[:, pb, :], cs_t[b][:, 1, :], ALU.mult, [BK[pb], Hcs[b]], [Ht2[u]])
                if kind == "D":
                    tt("pool", t1[u][:], t1[u][:], t2[u][:], ALU.add, [Ht1[u], Ht2[u]], [Ht1[u]])
                    tt("dve", dsl, t1[u][:], rsd[u][:], ALU.mult, [Ht1[u], Hrsd[u]], [Hdst])
                else:
                    tt("pool" if "pooladd" in FL else "dve", dsl, t1[u][:], t2[u][:], ALU.add, [Ht1[u], Ht2[u]], [Hdst])
            for blk in range(4 if "nov" not in FL else 0):
                bank = 6 + blk % 2
                items = [(ps[:, bank, 0:vw], ht[b][:, kc, blk * 128:(blk + 1) * 128], W[:, kc, vcol:vcol + vw], kc == 0, kc == KC - 1, None)
                         for kc in range(KC)]
                mmgroup(items, [H_W, Hht[b]], [BK[bank]])
                gb = t * 4 + blk
                src = ps[:, bank, 0:vw].rearrange("p (h d) -> p h d", h=HV)
                if blk % 2 == 0:
                    act(RV[:, gb, :, 0:64], src, AF.Copy, [BK[bank]], [H_RV[t]])
                else:
                    pg.op("dve", lambda e, gb=gb, src=src: e.tensor_copy(out=RV[:, gb, :, 0:64], in_=src), [BK[bank]], [H_RV[t]])

        if os.environ.get("SUBSTOP") == "proj":
            pg.barrier()
            return
        pg.barrier()
        ar.cur = mark
        PT = [ar.alloc([128, 1024], BF16, "PT") for _ in range(3)]
        HPT = [Hd("PT%d" % i) for i in range(3)]
        fsb = [ar.alloc([65, 512], F32, "fsb") for _ in range(4)]
        Hfsb = [Hd("fsb%d" % i) for i in range(4)]
        rr = fsb
        Hrr = Hfsb
        ost = [ar.alloc([64, 512], BF16, "ost") for _ in range(4)]
        Host = [Hd("ost%d" % i) for i in range(4)]
        if kind in "AD":
            KP = ar.alloc([128, 2, S], BF16, "KP")
            H_KP = Hd("KP")
        if kind == "A":
            o12 = [ar.alloc([64, 512], F32, "o12") for _ in range(4)]
            Ho12 = [Hd("o12_%d" % i) for i in range(4)]
            dd = [ar.alloc([64, 512], F32, "dd") for _ in range(2)]
            Hdd = [Hd("dd%d" % i) for i in range(2)]
            sqa = [ar.alloc([64, 512], F32, "sqa") for _ in range(2)]
            Hsqa = [Hd("sqa%d" % i) for i in range(2)]
        if kind in "BC":
            sbx = [ar.alloc([128, 1024], F32, "sbx") for _ in range(2)]
            Hsbx = [Hd("sbx%d" % i) for i in range(2)]
        if kind == "B":
            bsl = [ar.alloc([128, 2, 512], F32, "bsl") for _ in range(3)]
            Hbsl = [Hd("bsl%d" % i) for i in range(3)]
            bres = ar.alloc([128, 8, 2, 512], F32, "bres")
            H_bres = Hd("bres")
        if kind == "C":
            cm = ar.alloc([128, 20, 512], F32, "cm")
            H_cm = Hd("cm")
            for g in range(4):
                dma("sp", cm[:, g * 5:(g + 1) * 5, :], cm_in[g * 5:(g + 1) * 5].rearrange("j p n -> p j n"), [H_w], [H_cm], H_cm)
        allK = H_RK
        allV = H_RV + [H_RVo]
        scale = {"A": 32 ** -0.5, "B": 0.125, "C": 0.125, "D": 0.125}[kind]
        state = {"u": 0, "pt": 0, "sp": 0, "f": 0, "bs": 0, "bl": 0}

        pendfin = []

        def fin_p1(accset):
            fs = []
            for i in range(2):
                bank = accset[i]
                f = state["f"] % 4
                state["f"] += 1
                fs.append(f)
                pg.op("dve", lambda e, f=f, bank=bank: e.tensor_copy(out=fsb[f][0:65, :], in_=ps[0:65, bank, :]), [BK[bank]], [Hfsb[f]])
            for f in fs:
                recip(fsb[f][64:65, :], fsb[f][64:65, :], [Hfsb[f]], [Hfsb[f]])
            return fs, (state["f"] // 2) % 2

        def fin_p2(fs, u, unit_infos, t, bcb):
            tsl = slice(t * 512, (t + 1) * 512)
            for i in range(2):
                f = fs[i]
                bb = bcb[i]
                mmgroup([(ps[0:64, bb, :], onesf[64:65, 0:64], fsb[f][64:65, :], True, True, None)], [Hfsb[f], H_c2], [BK[bb]])
                if kind != "A":
                    tt("dve", ost[f][:], fsb[f][0:64, :], ps[0:64, bb, :], ALU.mult, [Hfsb[f], BK[bb]], [Host[f]])
                    ch, pb_ = unit_infos[i]
                    dma("pool", oT_v[pb_:pb_ + 64, ch, tsl], ost[f][:], [Host[f]], [H_oT], Host[f])
                else:
                    tt("dve", o12[f][:], fsb[f][0:64, :], ps[0:64, bb, :], ALU.mult, [Hfsb[f], BK[bb]], [Ho12[f]])
            if kind != "A":
                return
            stt(dd[u][:], o12[fs[1]][:], neglam[:, 2 * l:2 * l + 1], o12[fs[0]][:], ALU.mult, ALU.add,
                [Ho12[fs[0]], Ho12[fs[1]]] + CR, [Hdd[u]])
            tt("pool", sqa[u][:], dd[u][:], dd[u][:], ALU.mult, [Hdd[u]], [Hsqa[u]])
            pendfin.append([20, lambda bcb2: fin_p3(fs, u, unit_infos, t, bcb2)])

        def fin_p3(fs, u, unit_infos, t, bcb):
            tsl = slice(t * 512, (t + 1) * 512)
            bank = bcb[0]
            mmgroup([(ps[0:64, bank, :], onesf[0:64, 0:64], sqa[u][:], True, True, None)], [Hsqa[u], H_c2], [BK[bank]])
            act(sqa[u][:], ps[0:64, bank, :], AF.Sqrt, [BK[bank]] + CR, [Hsqa[u]], scale=1.0 / 64, bias=epsb[0:64, 0:1])
            recip(sqa[u][:], sqa[u][:], [Hsqa[u]], [Hsqa[u]])
            f = fs[0]
            stt(ost[f][:], dd[u][:], gsub[:, l:l + 1], sqa[u][:], ALU.mult, ALU.mult, [Hdd[u], Hsqa[u]] + CR, [Host[f]])
            ch, pb_ = unit_infos[0]
            dma("pool", oT_v[pb_:pb_ + 64, ch, tsl], ost[f][:], [Host[f]], [H_oT], Host[f])

        def run_pending(j, banks, force=False):
            k = 0
            while k < len(pendfin):
                due, fn = pendfin[k]
                if force or j >= due:
                    pendfin.pop(k)
                    fn(banks)
                    if not force:
                        return
                else:
                    k += 1

        def attn_unit(t, qk_items, v_aps, kbs, bias_fn, unit_infos, bias_pre=None, kh=None):
            accset = (6, 7)
            state["u"] += 1
            n = len(kbs)

            def emit_qk(j):
                kb = kbs[j]
                if bias_pre is not None:
                    bias_pre(kb)
                sp_ = state["sp"] % 3
                state["sp"] += 1
                banks = (2 * sp_, 2 * sp_ + 1)
                items = []
                for i in range(2):
                    lhsT, rhs, tp = qk_items[i](kb)
                    items.append((ps[:, banks[i], :], lhsT, rhs, True, True, tp))
                mmgroup(items, [H_RQ[t]] + ([H_RK[kb // 4]] if kh is None else kh), [BK[banks[0]], BK[banks[1]]])
                return banks

            pendq = [emit_qk(0)]
            if n > 1:
                pendq.append(emit_qk(1))
            for j in range(n):
                kb = kbs[j]
                banks = pendq.pop(0)
                p = state["pt"] % 3
                state["pt"] += 1
                if bias_fn is None:
                    act(PT[p][:], ps[:, banks[0]:banks[0] + 2, :], AF.Exp, [BK[banks[0]], BK[banks[1]]], [HPT[p]], scale=scale)
                else:
                    sx = state["bs"] % 2
                    for i in range(2):
                        bap, bh = bias_fn(i, kb)
                        stt(sbx[sx][:, i * 512:(i + 1) * 512], ps[:, banks[i], :], scale, bap, ALU.mult, ALU.add,
                            [BK[banks[i]], bh], [Hsbx[sx]])
                    state["bs"] += 1
                    act(PT[p][:], sbx[sx][:], AF.Exp, [Hsbx[sx]], [HPT[p]])
                if j + 2 < n:
                    pendq.append(emit_qk(j + 2))
                items = []
                for i in range(2):
                    items.append((ps[:, accset[i], :], v_aps[i](kb), PT[p][:, i * 512:(i + 1) * 512], j == 0, j == n - 1, None))
                mmgroup(items, [HPT[p], H_RV[kb // 4], H_RVo], [BK[accset[0]], BK[accset[1]]])
                if j == n - 1:
                    while pendfin:
                        run_pending(j, banks, force=True)
                else:
                    run_pending(j, banks)
            fs_, u_ = fin_p1(accset)
            pendfin.append([10, lambda bcb2, fs_=fs_, u_=u_: fin_p2(fs_, u_, unit_infos, t, bcb2)])

        if kind == "A":
            order = [(t, h) for h in range(4) for t in range(NT)]
        elif kind == "D":
            order = [(t, g) for g in range(2) for t in range(NT)]
        elif kind == "B":
            order = [(t, hp) for hp in range(2) for t in range(NT)]
        else:
            order = [(t, None) for t in range(NT)]
        for (t, hp_sel) in order:
            if os.environ.get("SUBSTOP") in ("unit1", "nofin") and t > 0:
                break
            tsl = slice(t * 512, (t + 1) * 512)
            if kind == "B" and t == 0 and NT > 2:
                for j in range(8):
                    dma("sp", bres[:, j, :, :], nab_in[l, 2 * hp_sel:2 * hp_sel + 2, 1, j].rearrange("h p n -> p h n"), [H_w], [H_bres], H_bres)
            if kind == "A":
                for h in [hp_sel]:
                    ch, pb_ = h // 2, (h % 2) * 64
                    if t == 0:
                        pg.op("pool", lambda e: e.memset(KP[:], 0.0), [], [H_KP])
                        for c in range(2):
                            p0 = pb_ + 32 * c
                            pg.op("pool", lambda e, p0=p0, c=c, ch=ch: e.tensor_copy(out=KP[p0:p0 + 32, c, :], in_=RK[p0:p0 + 32, ch, :]),
                                  list(H_RK), [H_KP])
                    def mk(i, ch=ch):
                        return lambda kb: (KP[:, i, kb * 128:(kb + 1) * 128], RQ[:, ch, tsl], None)
                    va = lambda kb, h=h: RV[:, kb, h, :]
                    attn_unit(t, [mk(0), mk(1)], [va, va], list(range(NB)), None, [(0 * 2 + ch, pb_), None], kh=[H_KP])
            elif kind == "D":
                for g in [hp_sel]:
                    if t == 0:
                        pg.op("pool", lambda e: e.memset(KP[:], 0.0), [], [H_KP])
                        for i in range(2):
                            pb_ = 64 * i
                            kch = (0 if g == 0 else 1) if i == 0 else (1 if g == 0 else 0)
                            pg.op("pool", lambda e, pb_=pb_, i=i, kch=kch: e.tensor_copy(out=KP[pb_:pb_ + 64, i, :], in_=RK[pb_:pb_ + 64, kch, :]),
                                  list(H_RK), [H_KP])
                    def mk(i, g=g):
                        return lambda kb: (KP[:, i, kb * 128:(kb + 1) * 128], RQ[:, g, tsl], None)
                    va = lambda kb, g=g: RV[:, kb, g, :]
                    attn_unit(t, [mk(0), mk(1)], [va, va], list(range(NB)), None, [(6 + g, 0), (6 + g, 64)], kh=[H_KP])
            else:
                for hp in ([hp_sel] if hp_sel is not None else range(2)):
                    def mk(i, hp=hp):
                        pb_ = 64 * i
                        return lambda kb: (RK[pb_:pb_ + 64, hp, kb * 128:(kb + 1) * 128], RQ[pb_:pb_ + 64, hp, tsl], None)
                    vas = [(lambda kb, h=2 * hp + i: RV[:, kb, h, :]) for i in range(2)]
                    if kind == "B":
                        kbs = [kb for kb in range(4 * t - 2, 4 * t + 6) if 0 <= kb < NB]
                        var = 0 if t == 0 else (2 if t == NT - 1 else 1)
                        cache = {}
                        def bias_pre(kb, hp=hp, t=t, var=var, cache=cache):
                            s_ = state["bl"] % 3
                            state["bl"] += 1
                            j = kb - (4 * t - 2)
                            dma("sp", bsl[s_][:], nab_in[l, 2 * hp:2 * hp + 2, var, j].rearrange("h p n -> p h n"), [H_w], [Hbsl[s_]], Hbsl[s_])
                            cache[kb] = s_
                        def bias_fn(i, kb, cache=cache):
                            s_ = cache[kb]
                            return bsl[s_][:, i, :], Hbsl[s_]
                        if var == 1:
                            bias_pre = None
                            def bias_fn(i, kb, t=t):
                                return bres[:, kb - (4 * t - 2), i, :], H_bres
                    else:
                        kbs = [kb for kb in range(4 * t - 8, 4 * t + 12) if 0 <= kb < NB]
                        bias_pre = None
                        def bias_fn(i, kb, t=t):
                            return cm[:, kb - 4 * t + 8, :], H_cm
                    br_ch = 2 * br + hp
                    attn_unit(t, [mk(0), mk(1)], vas, kbs, bias_fn, [(br_ch, 0), (br_ch, 64)], bias_pre)
        while pendfin:
            run_pending(0, (0, 1), force=True)
        pg.barrier()

    def merge_pass(l, src_v, H_src, dst_v, H_dst):
        ar.reset()
        Wg = ar.alloc([128, KC, 4096], BF16, "Wg")
        Wbr = ar.alloc([128, 8, D], BF16, "Wbr")
        Wo = ar.alloc([128, KC, D], BF16, "Wo")
        H_Wm = Hd("Wm")
        for kc in range(KC):
            dma("pool", Wg[:, kc, :], w_in[l, kc * 128:(kc + 1) * 128, 2816:6912], [H_w], [H_Wm], H_Wm)
        for bq in range(4):
            dma("pool", Wbr[:, 2 * bq:2 * bq + 2, :], w_branch[l, bq].rearrange("(k p) n -> p k n", p=128), [H_w], [H_Wm], H_Wm)
        dma("pool", Wo[:], w_out[l].rearrange("(k p) n -> p k n", p=128), [H_w], [H_Wm], H_Wm)
        xt = [ar.alloc([128, KC, 512], F32, "xt") for _ in range(2)]
        Hx = [Hd("mx%d" % i) for i in range(2)]
        ht = [ar.alloc([128, KC, 512], BF16, "ht") for _ in range(2)]
        Hht = [Hd("mh%d" % i) for i in range(2)]
        ot = [ar.alloc([128, KC, 512], BF16, "ot") for _ in range(2)]
        Hot = [Hd("mo%d" % i) for i in range(2)]
        mg = ar.alloc([128, KC, 512], BF16, "mg")
        H_mg = Hd("mg")
        sg = [ar.alloc([128, 512], F32, "sg") for _ in range(2)]
        Hsg = [Hd("sg%d" % i) for i in range(2)]
        acc = [ar.alloc([128, 512], F32, "macc") for _ in range(2)]
        Hacc = [Hd("macc%d" % i) for i in range(2)]
        tmp = [ar.alloc([128, 512], F32, "mtmp") for _ in range(2)]
        Htmp = [Hd("mtmp%d" % i) for i in range(2)]
        cnt = 0
        for t in range(NT):
            b = t % 2
            tsl = slice(t * 512, (t + 1) * 512)
            dma("sp", ht[b][:], hT_v[:, :, tsl], [H_hT], [Hht[b]], Hht[b])
            dma("sp", ot[b][:], oT_v[:, :, tsl], [H_oT], [Hot[b]], Hot[b])
            dma("sp", xt[b][:], src_v[:, :, tsl], [H_src], [Hx[b]], Hx[b])
            for oc in range(KC):
                a = oc % 2
                for bq in range(4):
                    gb = cnt % 2
                    mb = 2 + cnt % 2
                    u = cnt % 2
                    cnt += 1
                    col = bq * D + oc * 128
                    mmgroup([(ps[:, gb, :], Wg[:, kc, col:col + 128], ht[b][:, kc, :], kc == 0, kc == KC - 1, None) for kc in range(KC)],
                            [H_Wm, Hht[b]], [BK[gb]])
                    mmgroup([(ps[:, mb, :], Wbr[:, 2 * bq + j, oc * 128:(oc + 1) * 128], ot[b][:, 2 * bq + j, :], j == 0, j == 1, None)
                             for j in range(2)], [H_Wm, Hot[b]], [BK[mb]])
                    act(sg[u][:], ps[:, gb, :], AF.Sigmoid, [BK[gb]], [Hsg[u]])
                    if bq == 0:
                        tt("dve", acc[a][:], sg[u][:], ps[:, mb, :], ALU.mult, [Hsg[u], BK[mb]], [Hacc[a]])
                    elif bq < 3:
                        tt("dve", tmp[u][:], sg[u][:], ps[:, mb, :], ALU.mult, [Hsg[u], BK[mb]], [Htmp[u]])
                        tt("pool", acc[a][:], acc[a][:], tmp[u][:], ALU.add, [Hacc[a], Htmp[u]], [Hacc[a]])
                    else:
                        tt("dve", tmp[u][:], sg[u][:], ps[:, mb, :], ALU.mult, [Hsg[u], BK[mb]], [Htmp[u]])
                        tt("pool", mg[:, oc, :], acc[a][:], tmp[u][:], ALU.add, [Hacc[a], Htmp[u]], [H_mg])
            for oc in range(KC):
                bank = 4 + oc % 4
                mmgroup([(ps[:, bank, :], Wo[:, kc, oc * 128:(oc + 1) * 128], mg[:, kc, :], kc == 0, kc == KC - 1, None) for kc in range(KC)],
                        [H_Wm, H_mg], [BK[bank]])
                tt("dve", xt[b][:, oc, :], xt[b][:, oc, :], ps[:, bank, :], ALU.add, [Hx[b], BK[bank]], [Hx[b]])
            dma("pool", dst_v[:, :, tsl], xt[b][:], [Hx[b]], [H_dst], Hx[b])
        pg.barrier()

    def mlp_pass(l, src_v, H_src, dst_v, H_dst):
        ar.reset()
        Wup = ar.alloc([128, KC, 2 * DFF], BF16, "Wup")
        Wdn = ar.alloc([128, NFC, D], BF16, "Wdn")
        H_Wf = Hd("Wf")
        for kc in range(KC):
            dma("pool", Wup[:, kc, :], w_up[l, kc * 128:(kc + 1) * 128, :], [H_w], [H_Wf], H_Wf)
        for j in range(0, NFC, 2):
            dma("pool", Wdn[:, j:j + 2, :], w_down[l, j * 128:(j + 2) * 128, :].rearrange("(k p) n -> p k n", p=128), [H_w], [H_Wf], H_Wf)
        A = {"sq": [ar.alloc([128, 512], F32, "sq") for _ in range(2)], "H_sq": [Hd("sq0"), Hd("sq1")],
             "rstd": ar.alloc([128, 512], F32, "rstd"), "H_rstd": Hd("rstd")}
        xt = [ar.alloc([128, KC, 512], F32, "xt") for _ in range(1)]
        Hx = [Hd("fx%d" % i) for i in range(1)]
        h2 = ar.alloc([128, KC, 512], BF16, "h2")
        H_h2 = Hd("h2")
        gT = ar.alloc([128, NFC, 512], BF16, "gT")
        H_gT = Hd("gT")
        cv = [ar.alloc([128, 512], F32, "cv") for _ in range(2)]
        Hcv = [Hd("cv%d" % i) for i in range(2)]
        cg = [ar.alloc([128, 512], F32, "cg") for _ in range(2)]
        Hcg = [Hd("cg%d" % i) for i in range(2)]
        ntile = (S + 509) // 510
        po = l * PPL
        cnt = 0
        for i in range(ntile):
            b = 0
            c0 = 510 * i
            nv = min(510, S - c0)
            lo = max(c0 - 1, 0)
            hi = min(c0 + nv + 1, S)
            off = lo - (c0 - 1)
            nl = hi - lo
            dma("sp", xt[b][:, :, off:off + nl], src_v[:, :, lo:hi], [H_src], [Hx[b]], Hx[b])
            ncols = off + nl
            if off > 0:
                pg.op("pool", lambda e: e.memset(xt[0][:, :, 0:1], 0.0), [Hx[b]], [Hx[b]])
            rms_tile(xt[b], Hx[b], ncols, po + 8, h2, H_h2, A, 6)
            if off > 0:
                pg.op("pool", lambda e: e.memset(h2[:, :, 0:1], 0.0), [H_h2], [H_h2])
            if ncols < 512:
                pg.op("pool", lambda e, ncols=ncols: e.memset(h2[:, :, ncols:512], 0.0), [H_h2], [H_h2])
            if off > 0:
                pass
            for j in range(NFC):
                u = cnt % 2
                cnt += 1
                bv = 0 + u * 2
                bg = 1 + u * 2
                mmgroup([(ps[:, bv, :], Wup[:, kc, j * 128:(j + 1) * 128], h2[:, kc, :], kc == 0, kc == KC - 1, None) for kc in range(KC)],
                        [H_Wf, H_h2], [BK[bv]])
                mmgroup([(ps[:, bg, :], Wup[:, kc, DFF + j * 128:DFF + (j + 1) * 128], h2[:, kc, :], kc == 0, kc == KC - 1, None) for kc in range(KC)],
                        [H_Wf, H_h2], [BK[bg]])
                for (bank, cbuf, Hc, chn) in ((bv, cv[u], Hcv[u], j), (bg, cg[u], Hcg[u], NFC + j)):
                    w0 = pp[:, po + 16 + 0 * 44 + chn:po + 16 + 0 * 44 + chn + 1]
                    w1 = pp[:, po + 16 + 1 * 44 + chn:po + 16 + 1 * 44 + chn + 1]
                    w2 = pp[:, po + 16 + 2 * 44 + chn:po + 16 + 2 * 44 + chn + 1]
                    bb = pp[:, po + 148 + chn:po + 148 + chn + 1]
                    act(cbuf[:, 0:510], ps[:, bank, 1:511], AF.Identity, [BK[bank]] + CR, [Hc], scale=w1, bias=bb)
                    stt(cbuf[:, 0:510], ps[:, bank, 0:510], w0, cbuf[:, 0:510], ALU.mult, ALU.add, [BK[bank], Hc] + CR, [Hc])
                    stt(cbuf[:, 0:510], ps[:, bank, 2:512], w2, cbuf[:, 0:510], ALU.mult, ALU.add, [BK[bank], Hc] + CR, [Hc])
                act(cg[u][:, 0:510], cg[u][:, 0:510], AF.Gelu, [Hcg[u]], [Hcg[u]])
                tt("pool", gT[:, j, 0:510], cg[u][:, 0:510], cv[u][:, 0:510], ALU.mult, [Hcg[u], Hcv[u]], [H_gT])
            for oc in range(KC):
                bank = 4 + oc % 2
                mmgroup([(ps[:, bank, 0:510], Wdn[:, j, oc * 128:(oc + 1) * 128], gT[:, j, 0:510], j == 0, j == NFC - 1, None) for j in range(NFC)],
                        [H_Wf, H_gT], [BK[bank]])
                tt("dve", xt[b][:, oc, 1:1 + nv], xt[b][:, oc, 1:1 + nv], ps[:, bank, 0:nv], ALU.add, [Hx[b], BK[bank]], [Hx[b]])
            dma("pool", dst_v[:, :, c0:c0 + nv], xt[b][:, :, 1:1 + nv], [Hx[b]], [H_dst], Hx[b])
        pg.barrier()

    stages = []
    for seq in range(NSEQ):
        stages.append(("T", lambda seq=seq: phase_T(seq)))
        for l in range(L):
            stages.append(("P0", lambda l=l: phase_0(l, xTa_v, H_xTa)))
            for br in range(4):
                stages.append(("BR%d" % br, lambda l=l, br=br: branch_pass(l, br)))
            stages.append(("MG", lambda l=l: merge_pass(l, xTa_v, H_xTa, xTb_v, H_xTb)))
            stages.append(("FF", lambda l=l: mlp_pass(l, xTb_v, H_xTb, xTa_v, H_xTa)))
        stages.append(("Z", lambda seq=seq: phase_Z(seq, xTa_v, H_xTa)))
    for i, (nm, fn) in enumerate(stages):
        if stop_after is not None and i >= stop_after:
            break
        fn()
    pg.barrier()
    pg.op("pool", lambda e: e.memset(dummy[:], 0.0), [], [Hd("dummy")])
    pg.emit()
    return nc


_CACHE = {}


def _host_consts(S, inputs):
    c = {}
    perms = np.zeros((128, 3, 128), np.float32)
    for i, k in enumerate("ACD"):
        cs, sn, pm = _rope_tables(k, S)
        c["rope%s_c" % k] = cs
        c["rope%s_s" % k] = sn
        perms[:, i, :] = pm
    c["perms"] = perms
    c["ident"] = np.eye(128, dtype=np.float32)
    c["cmask"] = _cmask_tiles()
    c["nab"] = _na_tiles(np.asarray(inputs["na_rpb"], np.float32), S)
    c["pp"] = _pack_pp(*[np.asarray(inputs[k], np.float32) for k in
                         ("norm_attn", "norm_mlp", "conv_w", "conv_b", "qk_norm", "diff_subln", "norm_final")])
    c["dl"] = np.ascontiguousarray(np.asarray(inputs["diff_lambda"], np.float32).reshape(-1, 128))
    for k in ("w_in", "w_branch", "w_out", "w_up", "w_down"):
        c[k] = np.ascontiguousarray(np.asarray(inputs[k], np.float32))
    return c


def run_sequences(xs, inputs, n_cores=N_CORES, nseq=None, **bk):
    n, S, _ = xs.shape
    if nseq is None:
        nseq = (n + n_cores - 1) // n_cores
    L = np.asarray(inputs["w_in"]).shape[0]
    key = (S, nseq, L, tuple(sorted(bk.items())))
    if key not in _CACHE:
        _CACHE[key] = build_program(S, nseq, L, **bk)
    nc = _CACHE[key]
    consts = _host_consts(S, inputs)
    slots = [[(c + s * n_cores) if (c + s * n_cores) < n else (c % n) for s in range(nseq)] for c in range(n_cores)]
    in_maps = []
    for c in range(n_cores):
        m = dict(consts)
        m["x"] = np.ascontiguousarray(xs[slots[c]])
        in_maps.append(m)
    res = run_bass_kernel_spmd(nc, in_maps, core_ids=list(range(n_cores)))
    out = np.empty_like(xs)
    for c in range(n_cores):
        for s in range(nseq):
            i = c + s * n_cores
            if i < n:
                out[i] = res.results[c]["y"][s]
    return out, res


def kernel(x_prompt, x_sample, norm_attn, w_in, diff_lambda, diff_subln, na_rpb, qk_norm, w_branch,
           w_out, norm_mlp, w_up, conv_w, conv_b, w_down, norm_final):
    inputs = dict(norm_attn=norm_attn, w_in=w_in, diff_lambda=diff_lambda, diff_subln=diff_subln, na_rpb=na_rpb,
                  qk_norm=qk_norm, w_branch=w_branch, w_out=w_out, norm_mlp=norm_mlp, w_up=w_up, conv_w=conv_w,
                  conv_b=conv_b, w_down=w_down, norm_final=norm_final)
    xp = np.asarray(x_prompt, np.float32)
    xs_ = np.asarray(x_sample, np.float32)
    xs = np.concatenate([xs_, xp], axis=0)
    out, _ = run_sequences(xs, inputs)
    nsamp = xs_.shape[0]
    return (np.ascontiguousarray(out[nsamp:]), np.ascontiguousarray(out[:nsamp]))
```
